# Optimizing a Trainium2 kernel written in Bass

```python
import jax, jax.numpy as jnp
from jax import lax
import numpy as np

D_MODEL = 1024
BATCH = 16
SEQ = 256
DEPTH = 4
DEC_BATCH = 8
DEC_SEQ = 2048
PAST_LEN = 256

GRID_W = 64
N_AB_LAYERS = (DEPTH + 1) // 2
N_F_LAYERS = DEPTH // 2
A_HEADS = 8
A_KV_HEADS = 2
A_GROUP = A_HEADS // A_KV_HEADS
A_HEAD_DIM = 64
WINDOW = 128
BAND = 128
B_HEADS = 8
B_NOPE_DIM = 64
B_ROPE_DIM = 32
B_V_DIM = 64
B_Q_LORA = 384
B_KV_LORA = 256
A_Q_W = A_HEADS * A_HEAD_DIM
A_KV_W = A_KV_HEADS * A_HEAD_DIM
AB_SPLITS = (A_Q_W, A_Q_W + A_KV_W, A_Q_W + 2 * A_KV_W, A_Q_W + 2 * A_KV_W + B_Q_LORA,
             A_Q_W + 2 * A_KV_W + B_Q_LORA + B_KV_LORA)
AB_IN = AB_SPLITS[-1] + B_ROPE_DIM
AB_MIX = A_HEADS * A_HEAD_DIM + B_HEADS * B_V_DIM
A_SCALE = A_HEAD_DIM ** -0.5
B_SCALE = (B_NOPE_DIM + B_ROPE_DIM) ** -0.5
F_GROUPS = 4
F_WIDTH = D_MODEL
F_GROUP_DIM = F_WIDTH // F_GROUPS
D_FF = 2816
FFN_HALF = 0.5
ALPHA = (2 * DEPTH) ** 0.25
BETA = (8 * DEPTH) ** -0.25
LN_EPS = 1e-5
RMS_EPS = 1e-6
N_MODS = 9
Q_BLOCK = 128
ROPE_BASE = 10000.0
NEG_INF = -1e30

kernel_name = "hybrid_dit_swa_mla_fnet_macaron_step"


def layer_norm(x, g, b):
    xf = x.astype(jnp.float32)
    xc = xf - jnp.mean(xf, axis=-1, keepdims=True)
    var = jnp.mean(xc * xc, axis=-1, keepdims=True)
    return (xc * lax.rsqrt(var + LN_EPS) * g.astype(jnp.float32) + b.astype(jnp.float32)).astype(x.dtype)


def rms_norm(x, g):
    xf = x.astype(jnp.float32)
    y = xf * lax.rsqrt(jnp.mean(xf * xf, axis=-1, keepdims=True) + RMS_EPS)
    return (y * g.astype(jnp.float32)).astype(x.dtype)


def ada_mods(cond, w, b):
    return jnp.split(jax.nn.silu(cond) @ w + b, N_MODS, axis=-1)


def modulate(x, shift, scale):
    return x * (1.0 + scale[:, None, :]) + shift[:, None, :]


def post_norm(x, out, gate, g, b):
    return layer_norm(ALPHA * x + gate[:, None, :] * out, g, b)


def swiglu(h, w_gate, w_up, w_down):
    return (jax.nn.silu(h @ w_gate) * (h @ w_up)) @ w_down


def half_ffn(x, shift, scale, gate, w_gate, w_up, w_down, g, b):
    h = modulate(x, shift, scale)
    return post_norm(x, FFN_HALF * swiglu(h, w_gate, w_up, w_down), gate, g, b)


def rope_1d(x, pos):
    half = x.shape[-1] // 2
    inv = ROPE_BASE ** (-jnp.arange(half, dtype=jnp.float32) / half)
    ang = pos.astype(jnp.float32)[:, None] * inv[None, :]
    ang = ang.reshape((ang.shape[0],) + (1,) * (x.ndim - 3) + (half,))
    cos = jnp.cos(ang).astype(x.dtype)
    sin = jnp.sin(ang).astype(x.dtype)
    x1, x2 = x[..., :half], x[..., half:]
    return jnp.concatenate([x1 * cos - x2 * sin, x2 * cos + x1 * sin], axis=-1)


def axial_rope(x):
    L = x.shape[1]
    rows = L // GRID_W
    row = jnp.repeat(jnp.arange(rows), GRID_W)
    col = jnp.tile(jnp.arange(GRID_W), rows)
    h = x.shape[-1] // 2
    return jnp.concatenate([rope_1d(x[..., :h], row), rope_1d(x[..., h:], col)], axis=-1)


def sink_softmax(s, sink):
    m = jnp.maximum(jnp.max(s, axis=-1, keepdims=True), sink)
    p = jnp.exp(s - m)
    return p / (jnp.sum(p, axis=-1, keepdims=True) + jnp.exp(sink - m))


def dense_attention(q, k, v, scale, sink):
    bsz, lq, hk, g, dk = q.shape
    qb = q.reshape(bsz, lq // Q_BLOCK, Q_BLOCK, hk, g, dk).swapaxes(0, 1)

    def block(qi):
        s = jnp.einsum("bqhgd,bkhd->bhgqk", qi, k).astype(jnp.float32) * scale
        if sink is None:
            p = jax.nn.softmax(s, axis=-1)
        else:
            p = sink_softmax(s, sink.astype(jnp.float32)[None, :, :, None, None])
        return jnp.einsum("bhgqk,bkhd->bqhgd", p.astype(v.dtype), v)

    o = lax.map(block, qb)
    return o.swapaxes(0, 1).reshape(bsz, lq, hk, g, v.shape[-1])


def banded_attention(q, k, v, kc, vc, scale, sink):
    bsz, L, hk, g, dk = q.shape
    nb = L // BAND
    pad = ((0, 0), (BAND, BAND), (0, 0), (0, 0))
    kp = jnp.pad(k, pad).reshape(bsz, nb + 2, BAND, hk, dk)
    vp = jnp.pad(v, pad).reshape(bsz, nb + 2, BAND, hk, v.shape[-1])
    kw = jnp.concatenate([kp[:, :-2], kp[:, 1:-1], kp[:, 2:]], axis=2)
    vw = jnp.concatenate([vp[:, :-2], vp[:, 1:-1], vp[:, 2:]], axis=2)
    qb = q.reshape(bsz, nb, BAND, hk, g, dk)
    s_loc = jnp.einsum("bnqhgd,bnkhd->bhgnqk", qb, kw).astype(jnp.float32) * scale
    qpos = jnp.arange(nb)[:, None] * BAND + jnp.arange(BAND)[None, :]
    kpos = (jnp.arange(nb)[:, None] * BAND - BAND + jnp.arange(3 * BAND)[None, :])[:, None, :]
    valid = (kpos >= 0) & (kpos < L) & (jnp.abs(kpos - qpos[:, :, None]) <= WINDOW)
    s_loc = jnp.where(valid, s_loc, NEG_INF)
    s_ctx = jnp.einsum("bnqhgd,bchd->bhgnqc", qb, kc).astype(jnp.float32) * scale
    lc = kc.shape[1]
    p = sink_softmax(jnp.concatenate([s_ctx, s_loc], axis=-1),
                     sink.astype(jnp.float32)[None, :, :, None, None, None]).astype(v.dtype)
    o = (jnp.einsum("bhgnqc,bchd->bnqhgd", p[..., :lc], vc)
         + jnp.einsum("bhgnqk,bnkhd->bnqhgd", p[..., lc:], vw))
    return o.reshape(bsz, L, hk, g, v.shape[-1])


def split_ab(proj):
    bsz, L, _ = proj.shape
    qa, ka, va, cq, ckv, kpe = jnp.split(proj, AB_SPLITS, axis=-1)
    qa = qa.reshape(bsz, L, A_KV_HEADS, A_GROUP, A_HEAD_DIM)
    ka = ka.reshape(bsz, L, A_KV_HEADS, A_HEAD_DIM)
    va = va.reshape(bsz, L, A_KV_HEADS, A_HEAD_DIM)
    return qa, ka, va, cq, ckv, kpe


def mla_q(cq, g_cq, w_uq):
    bsz, L, _ = cq.shape
    return (rms_norm(cq, g_cq) @ w_uq).reshape(bsz, L, B_HEADS, B_NOPE_DIM + B_ROPE_DIM)


def mla_kv(ckv_n, w_ukv):
    bsz, L, _ = ckv_n.shape
    kv = (ckv_n @ w_ukv).reshape(bsz, L, B_HEADS, B_NOPE_DIM + B_V_DIM)
    return kv[..., :B_NOPE_DIM], kv[..., B_NOPE_DIM:]


def mla_keys(k_nope, kpe):
    kpe_h = jnp.broadcast_to(kpe[:, :, None, :], k_nope.shape[:3] + (B_ROPE_DIM,))
    return jnp.concatenate([k_nope, kpe_h], axis=-1)


def merge_heads(o_a, o_b, w_out):
    bsz, L = o_a.shape[:2]
    return jnp.concatenate([o_a.reshape(bsz, L, -1), o_b.reshape(bsz, L, -1)], axis=-1) @ w_out


def ab_mixer_context(h, w_in, w_out, sink, g_cq, w_uq, g_ckv, w_ukv):
    qa, ka, va, cq, ckv, kpe = split_ab(h @ w_in)
    ckv_n = rms_norm(ckv, g_ckv)
    qb = mla_q(cq, g_cq, w_uq)
    k_nope, vb = mla_kv(ckv_n, w_ukv)
    o_a = dense_attention(qa, ka, va, A_SCALE, sink.reshape(A_KV_HEADS, A_GROUP))
    o_b = dense_attention(qb[:, :, :, None, :], mla_keys(k_nope, kpe), vb, B_SCALE, None)
    return merge_heads(o_a, o_b, w_out), ka, va, ckv_n, kpe


def ab_mixer_latent(h, ck_a, cv_a, c_ckv, c_kpe, w_in, w_out, sink, g_cq, w_uq, g_ckv, w_ukv):
    qa, ka, va, cq, ckv, kpe = split_ab(h @ w_in)
    o_a = banded_attention(axial_rope(qa), axial_rope(ka), va, ck_a, cv_a, A_SCALE,
                           sink.reshape(A_KV_HEADS, A_GROUP))
    qb = mla_q(cq, g_cq, w_uq)
    qb = jnp.concatenate([qb[..., :B_NOPE_DIM], axial_rope(qb[..., B_NOPE_DIM:])], axis=-1)
    k_nope, vb = mla_kv(rms_norm(ckv, g_ckv), w_ukv)
    kc_nope, vc = mla_kv(c_ckv, w_ukv)
    k_all = jnp.concatenate([mla_keys(kc_nope, c_kpe), mla_keys(k_nope, axial_rope(kpe))], axis=1)
    v_all = jnp.concatenate([vc, vb], axis=1)
    o_b = dense_attention(qb[:, :, :, None, :], k_all, v_all, B_SCALE, None)
    return merge_heads(o_a, o_b, w_out)


def fourier_mixer(h, w_in, w_out):
    bsz, L, _ = h.shape
    u = (h @ w_in).reshape(bsz, L, F_GROUPS, F_GROUP_DIM).astype(jnp.float32)
    f = jnp.fft.fft2(u, axes=(1, 3), norm="ortho").real
    return f.astype(h.dtype).reshape(bsz, L, F_WIDTH) @ w_out


def setup_inputs(seed: int = 0) -> dict:
    key = jax.random.key(seed)
    ks = jax.random.split(key, 26)

    def nrm(k, shape, s):
        return s * jax.random.normal(k, shape, jnp.float32)

    return {
        "x_prompt": nrm(ks[0], (BATCH, SEQ, D_MODEL), 1.0),
        "x_sample": nrm(ks[1], (DEC_BATCH, DEC_SEQ, D_MODEL), 1.0),
        "cache_a_k": nrm(ks[2], (DEC_BATCH, N_AB_LAYERS, PAST_LEN, A_KV_HEADS, A_HEAD_DIM), 1.0),
        "cache_a_v": nrm(ks[3], (DEC_BATCH, N_AB_LAYERS, PAST_LEN, A_KV_HEADS, A_HEAD_DIM), 1.0),
        "cache_b_ckv": nrm(ks[4], (DEC_BATCH, N_AB_LAYERS, PAST_LEN, B_KV_LORA), 1.0),
        "cache_b_kpe": nrm(ks[5], (DEC_BATCH, N_AB_LAYERS, PAST_LEN, B_ROPE_DIM), 1.0),
        "c": nrm(ks[6], (DEC_BATCH, D_MODEL), 1.0),
        "c_ctx": nrm(ks[7], (D_MODEL,), 1.0),
        "w_ada": nrm(ks[8], (DEPTH, D_MODEL, N_MODS * D_MODEL), 0.5 * D_MODEL ** -0.5),
        "b_ada": nrm(ks[9], (DEPTH, N_MODS * D_MODEL), 0.02),
        "ln_g": 1.0 + nrm(ks[10], (DEPTH, 3, D_MODEL), 0.02),
        "ln_b": nrm(ks[11], (DEPTH, 3, D_MODEL), 0.02),
        "ffn_w_gate": nrm(ks[12], (DEPTH, 2, D_MODEL, D_FF), D_MODEL ** -0.5),
        "ffn_w_up": nrm(ks[13], (DEPTH, 2, D_MODEL, D_FF), D_MODEL ** -0.5),
        "ffn_w_down": nrm(ks[14], (DEPTH, 2, D_FF, D_MODEL), BETA * D_FF ** -0.5),
        "ab_w_in": nrm(ks[15], (N_AB_LAYERS, D_MODEL, AB_IN), D_MODEL ** -0.5),
        "ab_w_out": nrm(ks[16], (N_AB_LAYERS, AB_MIX, D_MODEL), BETA * AB_MIX ** -0.5),
        "a_sink": nrm(ks[17], (N_AB_LAYERS, A_HEADS), 0.5),
        "b_g_cq": 1.0 + nrm(ks[18], (N_AB_LAYERS, B_Q_LORA), 0.02),
        "b_w_uq": nrm(ks[19], (N_AB_LAYERS, B_Q_LORA, B_HEADS * (B_NOPE_DIM + B_ROPE_DIM)), B_Q_LORA ** -0.5),
        "b_g_ckv": 1.0 + nrm(ks[20], (N_AB_LAYERS, B_KV_LORA), 0.02),
        "b_w_ukv": nrm(ks[21], (N_AB_LAYERS, B_KV_LORA, B_HEADS * (B_NOPE_DIM + B_V_DIM)), B_KV_LORA ** -0.5),
        "f_w_in": nrm(ks[22], (N_F_LAYERS, D_MODEL, F_WIDTH), D_MODEL ** -0.5),
        "f_w_out": nrm(ks[23], (N_F_LAYERS, F_WIDTH, D_MODEL), BETA * F_WIDTH ** -0.5),
    }


def reference(x_prompt, x_sample, cache_a_k, cache_a_v, cache_b_ckv, cache_b_kpe, c, c_ctx,
              w_ada, b_ada, ln_g, ln_b, ffn_w_gate, ffn_w_up, ffn_w_down, ab_w_in, ab_w_out,
              a_sink, b_g_cq, b_w_uq, b_g_ckv, b_w_ukv, f_w_in, f_w_out):
    y_p = x_prompt
    y_s = x_sample
    new_k, new_v, new_ckv, new_kpe = [], [], [], []
    for l in range(DEPTH):
        mc = ada_mods(c_ctx[None, :], w_ada[l], b_ada[l])
        ms = ada_mods(c, w_ada[l], b_ada[l])
        ffn_pre = (ffn_w_gate[l, 0], ffn_w_up[l, 0], ffn_w_down[l, 0], ln_g[l, 0], ln_b[l, 0])
        y_p = half_ffn(y_p, mc[0], mc[1], mc[2], *ffn_pre)
        y_s = half_ffn(y_s, ms[0], ms[1], ms[2], *ffn_pre)
        h_p = modulate(y_p, mc[3], mc[4])
        h_s = modulate(y_s, ms[3], ms[4])
        i = l // 2
        if l % 2 == 0:
            mla_w = (b_g_cq[i], b_w_uq[i], b_g_ckv[i], b_w_ukv[i])
            o_p, ka, va, ckv_n, kpe = ab_mixer_context(h_p, ab_w_in[i], ab_w_out[i], a_sink[i], *mla_w)
            o_s = ab_mixer_latent(h_s, cache_a_k[:, i], cache_a_v[:, i], cache_b_ckv[:, i], cache_b_kpe[:, i],
                                  ab_w_in[i], ab_w_out[i], a_sink[i], *mla_w)
            new_k.append(ka)
            new_v.append(va)
            new_ckv.append(ckv_n)
            new_kpe.append(kpe)
        else:
            o_p = fourier_mixer(h_p, f_w_in[i], f_w_out[i])
            o_s = fourier_mixer(h_s, f_w_in[i], f_w_out[i])
        y_p = post_norm(y_p, o_p, mc[5], ln_g[l, 1], ln_b[l, 1])
        y_s = post_norm(y_s, o_s, ms[5], ln_g[l, 1], ln_b[l, 1])
        ffn_post = (ffn_w_gate[l, 1], ffn_w_up[l, 1], ffn_w_down[l, 1], ln_g[l, 2], ln_b[l, 2])
        y_p = half_ffn(y_p, mc[6], mc[7], mc[8], *ffn_post)
        y_s = half_ffn(y_s, ms[6], ms[7], ms[8], *ffn_post)
    y_prompt = y_p
    y_sample = y_s
    new_a_k = jnp.stack(new_k, axis=1)
    new_a_v = jnp.stack(new_v, axis=1)
    new_b_ckv = jnp.stack(new_ckv, axis=1)
    new_b_kpe = jnp.stack(new_kpe, axis=1)
    return (y_prompt, y_sample, new_a_k, new_a_v, new_b_ckv, new_b_kpe)
```

```python
import contextlib
import numpy as np
import ml_dtypes
import concourse.bass as bass
import concourse.mybir as mybir
import concourse.bass_utils as bu

F32 = mybir.dt.float32
BF16 = mybir.dt.bfloat16
AF = mybir.ActivationFunctionType
ALU = mybir.AluOpType

D = 1024
DFF = 2816
NF = 22
KC = 8
TS = 2048
TP = 512
T = TS + TP
DEPTH = 4
ALPHA = (2 * DEPTH) ** 0.25
LN_EPS = 1e-5
RMS_EPS = 1e-6
ENGS = ("sp", "act", "pool", "dve", "pe")


class Res:
    __slots__ = ("name", "last_w", "readers", "excl")

    def __init__(self, name="", excl=False):
        self.name = name
        self.last_w = None
        self.readers = []
        self.excl = excl


class DmaSem:
    __slots__ = ("h", "count", "name")

    def __init__(self, h, name):
        self.h = h
        self.count = 0
        self.name = name


class Op:
    __slots__ = ("eng", "fn", "seq", "deps", "signal", "sigidx", "dsem", "dval", "is_dma")


class Prog:
    def __init__(self, nc):
        self.nc = nc
        self.ops = {e: [] for e in ENGS}
        self.dsems = []
        self.epoch = Res("epoch")
        self.pending_stores = []
        self.last_dma = {}

    def op(self, eng, fn, reads=(), writes=(), dsem=None, extra=()):
        o = Op()
        o.eng = eng
        o.fn = fn
        o.seq = len(self.ops[eng])
        o.signal = False
        o.sigidx = 0
        o.dsem = dsem
        o.is_dma = dsem is not None
        if dsem is not None:
            dsem.count += 16
            o.dval = dsem.count
        else:
            o.dval = 0
        deps = {}
        for r in reads:
            d = r.last_w
            if d is not None:
                deps[id(d)] = d
            if r.excl:
                for d in r.readers:
                    if d.eng != eng:
                        deps[id(d)] = d
        for w in writes:
            d = w.last_w
            if d is not None:
                deps[id(d)] = d
            for d in w.readers:
                deps[id(d)] = d
        for d in extra:
            deps[id(d)] = d
        dl = []
        for d in deps.values():
            if d is o:
                continue
            if (not d.is_dma) and (not o.is_dma) and d.eng == "pe" and eng == "pe":
                continue
            if d.is_dma and o.is_dma and d.dsem is o.dsem:
                continue
            dl.append(d)
            if not d.is_dma:
                d.signal = True
        best = {}
        for d in dl:
            if d.is_dma:
                k_ = id(d.dsem)
                if k_ not in best or best[k_].dval < d.dval:
                    best[k_] = d
        dl = [d for d in dl if (not d.is_dma) or best[id(d.dsem)] is d]
        o.deps = dl
        for r in reads:
            r.readers.append(o)
        for w in writes:
            w.last_w = o
            w.readers = []
        self.ops[eng].append(o)
        if o.is_dma:
            self.last_dma[id(dsem)] = o
        return o

    def barrier(self):
        lasts = [self.ops[e][-1] for e in ("act", "dve", "pe") if self.ops[e]]
        lasts += list(self.last_dma.values())
        self.op("dve", lambda e: e.nop(), writes=[self.epoch], extra=lasts)
        self.op("act", lambda e: e.nop(), reads=[self.epoch])
        self.op("pe", lambda e: e.nop(), reads=[self.epoch])

    def emit(self):
        nc = self.nc
        for e in ENGS:
            k = 0
            for o in self.ops[e]:
                if o.signal and not o.is_dma:
                    k += 1
                    o.sigidx = k
        with contextlib.ExitStack() as st:
            esem = {e: st.enter_context(nc.semaphore("s_" + e)) for e in ENGS}
            block = st.enter_context(nc.Block())

            def make(e):
                def body(engh):
                    known = {}
                    for o in self.ops[e]:
                        for d in o.deps:
                            if d.is_dma:
                                key = id(d.dsem)
                                val = d.dval
                                sem = d.dsem.h
                            else:
                                key = d.eng
                                val = d.sigidx
                                sem = esem[d.eng]
                            if known.get(key, 0) >= val:
                                continue
                            engh.wait_ge(sem, val)
                            known[key] = val
                        ins = o.fn(engh)
                        if o.is_dma:
                            ins.then_inc(o.dsem.h, 16)
                        elif o.signal:
                            ins.then_inc(esem[e], 1)
                    if e == "sp":
                        for ds in self.dsems:
                            if ds.count > 0:
                                engh.wait_ge(ds.h, ds.count)
                return body

            block.sync(make("sp"))
            block.scalar(make("act"))
            block.gpsimd(make("pool"))
            block.vector(make("dve"))
            block.tensor(make("pe"))


AB_STOP = [99]
MM_TAGS = []
CUR_TAG = ["-"]
DBG = set()


def build_program(stage=99, skip_ffn=False):
    nc = bass.Bass("TRN2", target_bir_lowering=False)

    def din(name, shape, dt=F32):
        return nc.dram_tensor(name, list(shape), dt, kind="ExternalInput").ap()

    def dout(name, shape):
        return nc.dram_tensor(name, list(shape), F32, kind="ExternalOutput").ap()

    xs = din("xs", [TS, D])
    xp = din("xp", [TP, D])
    c2 = din("c2", [2, D])
    w_ada = din("w_ada", [DEPTH, D, 9 * D])
    smallp = din("smallp", [DEPTH, 120, 128])
    wg = din("wg", [DEPTH, 2, D, DFF])
    wu = din("wu", [DEPTH, 2, D, DFF])
    wd = din("wd", [DEPTH, 2, DFF, D])
    ident_d = din("ident", [128, 128])
    f_w_in = din("f_w_in", [2, D, D])
    f_w_out = din("f_w_out", [2, D, D])
    dft_big = din("dft_big", [TS, 2, TS], BF16)
    dft_small = din("dft_small", [256, 2, 256], BF16)
    csg_d = din("csg", [256, 512], BF16)
    ab_w_in = din("ab_w_in", [2, D, 1440])
    ab_w_out = din("ab_w_out", [2, D, D])
    a_sink = din("a_sink", [2, 8])
    b_g_cq = din("b_g_cq", [2, 384])
    b_w_uq = din("b_w_uq", [2, 384, 768])
    b_g_ckv = din("b_g_ckv", [2, 256])
    b_w_ukv = din("b_w_ukv", [2, 256, 1024])
    ca_k = din("ca_k", [2, 256, 128])
    ca_v = din("ca_v", [2, 256, 128])
    cb_ckv = din("cb_ckv", [2, 256, 256])
    cb_kpe = din("cb_kpe", [2, 256, 32])
    ropeA_d = din("ropeA", [128, 2, TS])
    ropeB_d = din("ropeB", [64, 2, TS])
    masks_d = din("masks", [128, 2, 512], BF16)
    nk_o = dout("nk", [2, 2, 256, 128])
    nv_o = dout("nv", [2, 2, 256, 128])
    nckv_o = dout("nckv", [2, 2, 256, 256])
    nkpe_o = dout("nkpe", [2, 2, 256, 32])
    ys = dout("ys", [TS, D])
    yp = dout("yp", [TP, D])

    st = contextlib.ExitStack()
    with st:
        P = Prog(nc)

        def sb(name, shape, dt):
            return st.enter_context(nc.sbuf_tensor(name, list(shape), dt))

        def dsem(name):
            s = DmaSem(st.enter_context(nc.semaphore("d_" + name)), name)
            P.dsems.append(s)
            return s

        xT = sb("xT", [128, KC, T], F32)
        ident = sb("ident_sb", [128, 128], F32)
        onesM = sb("onesM", [128, 128], BF16)
        spT = sb("spT", [128, DEPTH, 120], F32)
        scT = sb("scT", [128, 16], BF16)
        mods_sets = [sb("mods%d" % i, [128, 2, 72], F32) for i in range(2)]
        MODS = {"set": 0}
        csG = sb("csG", [128, 2, 512], BF16)
        identB = sb("identB", [128, 128], BF16)
        masks = sb("masks_sb", [128, 2, 512], BF16)
        esink = sb("esink", [128, 8], F32)
        gvec = sb("gvec", [128, 5], F32)
        PB = [st.enter_context(nc.psum_tensor("pb%d" % i, [128, 512], F32)) for i in range(8)]
        RB = [Res("pb%d" % i, excl=True) for i in range(8)]
        ARENA_BYTES = 121856
        arena = sb("arena", [128, ARENA_BYTES // 4], F32)

        def carve(off, shape, dt):
            esz = 4 if dt == F32 else 2
            n = 1
            for d_ in shape[1:]:
                n *= d_
            assert off % 4 == 0 and (n * esz) % 4 == 0 and off + n * esz <= ARENA_BYTES, (off, shape)
            a = arena[:, off // 4:(off + n * esz) // 4]
            if dt != F32:
                a = a.bitcast(dt)
            if len(shape) == 3:
                a = a.rearrange("p (a b) -> p a b", b=shape[2])
            elif len(shape) == 4:
                a = a.rearrange("p (a b c) -> p a b c", b=shape[2], c=shape[3])
            if shape[0] < 128:
                a = a[0:shape[0]]
            return a

        O_WGU = 0
        O_WDN = O_WGU + 16384
        O_ZB = O_WDN + 11264
        O_ZQ = O_ZB + 8192
        O_ST = O_ZQ + 8192
        O_LT = O_ST + 8192
        O_WORK = O_LT + 4096
        O_HT = O_WORK
        O_AT = O_HT + 16384
        O_SG = O_AT + 45056
        assert O_SG + 4096 <= ARENA_BYTES
        hT = carve(O_HT, [128, KC, 1024], BF16)
        aT = carve(O_AT, [128, NF, 1024], BF16)
        wgu = [carve(O_WGU + 4096 * i, [128, 2, KC, 128], BF16) for i in range(4)]
        wad = [carve(O_WGU + 8192 * i, [128, KC, 512], BF16) for i in range(2)]
        wdn = [carve(O_WDN + 2816 * i, [128, 11, 128], BF16) for i in range(4)]
        sg = [carve(O_SG + 2048 * i, [128, 512], F32) for i in range(2)]
        zb = carve(O_ZB, [128, KC, 512], BF16)
        zq = carve(O_ZQ, [128, KC, 512], BF16)
        st_mean = carve(O_ST, [128, 512], F32)
        st_a = carve(O_ST + 2048, [128, 512], F32)
        st_rstd = carve(O_ST + 4096, [128, 512], F32)
        st_mr = carve(O_ST + 6144, [128, 512], F32)
        lt = [carve(O_LT + 2048 * i, [128, 512], F32) for i in range(2)]
        iost = [carve(O_ZB, [128, D], F32), carve(O_ZQ, [128, D], F32)]
        sp_in = carve(O_HT, [120, DEPTH, 128], F32)
        c_in = carve(O_HT + 4096, [16, 128], F32)

        R_xT = [[Res("xT%d_%d" % (k, s)) for s in range(5)] for k in range(KC)]
        R_ident, R_ones, R_spT, R_scT = Res("ident"), Res("ones"), Res("spT"), Res("scT")
        R_mods_sets = [Res("mods0"), Res("mods1")]

        def cur_rm():
            return R_mods_sets[MODS["set"]]
        R_hT = [Res("hT%d" % s) for s in range(2)]
        R_aT = [[Res("aT%d_%d" % (f, s)) for s in range(2)] for f in range(NF)]
        R_wgu = [Res("wgu%d" % i) for i in range(4)]
        R_wdn = [Res("wdn%d" % i) for i in range(4)]
        R_wad = [Res("wad0"), Res("wad1")]
        R_sg = [Res("sg0"), Res("sg1")]
        R_zb, R_zq = Res("zb"), Res("zq")
        R_mean, R_sta, R_rstd, R_mr = Res("mean"), Res("sta"), Res("rstd"), Res("mr")
        R_lt = [Res("lt0"), Res("lt1")]
        R_io = [Res("io0"), Res("io1")]
        R_spin, R_cin = Res("spin"), Res("cin")
        S_wgu = [dsem("wgu%d" % i) for i in range(4)]
        S_wdn = [dsem("wdn%d" % i) for i in range(4)]
        S_wad = [dsem("wad0"), dsem("wad1")]
        S_io = [dsem("io0"), dsem("io1")]
        S_misc = dsem("misc")

        cnt = {"wgu": 0, "wdn": 0, "wad": 0, "io": 0, "sg": 0, "lt": 0, "eng": 0, "p2": 0}

        def alt_eng():
            cnt["eng"] += 1
            return "act" if cnt["eng"] % 2 else "dve"

        def copy_op(eng, out, in_, reads, writes):
            if eng == "act":
                return P.op("act", lambda e: e.copy(out, in_), reads=reads, writes=writes)
            return P.op("dve", lambda e: e.tensor_copy(out, in_), reads=reads, writes=writes)

        P.op("sp", lambda e: e.dma_start(out=ident[:], in_=ident_d), writes=[R_ident], dsem=dsem("ident"))
        P.op("dve", lambda e: e.memset(onesM[:], 1.0 / 1024.0), writes=[R_ones])
        P.op("sp", lambda e: e.dma_start(out=sp_in, in_=smallp.rearrange("l r p -> r l p")),
             writes=[R_spin], dsem=dsem("spin"))
        S_cin = dsem("cin")
        for s_ in range(2):
            P.op("sp", lambda e, s_=s_: e.dma_start(out=c_in[s_ * 8:(s_ + 1) * 8, :], in_=c2[s_].rearrange("(k p) -> k p", p=128)),
                 writes=[R_cin], dsem=S_cin)
        for l in range(DEPTH):
            P.op("pe", lambda e, l=l: e.transpose(PB[0][:, l * 120:(l + 1) * 120], sp_in[:, l, :], ident[0:120, 0:120]),
                 reads=[R_spin, R_ident], writes=[RB[0]])
        P.op("dve", lambda e: e.tensor_copy(spT[:].rearrange("p l r -> p (l r)"), PB[0][:, 0:480]),
             reads=[RB[0]], writes=[R_spT])
        P.op("act", lambda e: e.activation(c_in, c_in, AF.Silu), reads=[R_cin], writes=[R_cin])
        P.op("pe", lambda e: e.transpose(PB[1][:, 0:16], c_in, ident[0:16, 0:16]), reads=[R_cin, R_ident], writes=[RB[1]])
        P.op("dve", lambda e: e.tensor_copy(scT[:], PB[1][:, 0:16]), reads=[RB[1]], writes=[R_scT])

        def ada_steps(l, mset, bank):
            md, rm = mods_sets[mset], R_mods_sets[mset]
            steps = []

            def fill(cb):
                i = cnt["wad"] % 2
                cnt["wad"] += 1
                P.op("pool", lambda e: e.dma_start(
                    out=wad[i][:], in_=w_ada[l].rearrange("(k p) c -> p k c", p=128)[:, :, cb * 512:(cb + 1) * 512]),
                    reads=[P.epoch], writes=[R_wad[i], R_wgu[2 * i], R_wgu[2 * i + 1]], dsem=S_wad[i])
                for m in range(4):
                    for k in range(KC):
                        P.op("pe", lambda e, m=m, k=k: e.matmul(
                            PB[bank][:, m * 2:m * 2 + 2], wad[i][:, k, m * 128:(m + 1) * 128],
                            scT[:].rearrange("p (s k) -> p k s", k=8)[:, k, :], start=(k == 0), stop=(k == KC - 1)),
                            reads=[R_wad[i], R_wgu[2 * i], R_wgu[2 * i + 1], R_scT], writes=[RB[bank]])
                for s_ in range(2):
                    P.op("dve", lambda e, s_=s_: e.tensor_tensor(
                        md[:, s_, cb * 4:cb * 4 + 4], PB[bank][:, 0:8].rearrange("p (j s) -> p j s", s=2)[:, :, s_],
                        spT[:, l, cb * 4:cb * 4 + 4], ALU.add),
                        reads=[RB[bank], R_spT], writes=[rm])

            def fin():
                for j in (1, 4, 7):
                    P.op("dve", lambda e, j=j: e.tensor_scalar(md[:, :, j * 8:(j + 1) * 8], md[:, :, j * 8:(j + 1) * 8],
                                                               1.0, None, ALU.add), reads=[rm], writes=[rm])
                for j, f in ((2, 0.5 / ALPHA), (5, 1.0 / ALPHA), (8, 0.5 / ALPHA)):
                    P.op("dve", lambda e, j=j, f=f: e.tensor_scalar(md[:, :, j * 8:(j + 1) * 8], md[:, :, j * 8:(j + 1) * 8],
                                                                    f, None, ALU.mult), reads=[rm], writes=[rm])

            for cb in range(18):
                steps.append(lambda cb=cb: fill(cb))
            steps.append(fin)
            return steps

        def load_tokens(src, tok0, ntok, side=None):
            for tb in range(ntok // 128):
                if side:
                    side.pop(0)()
                i = cnt["io"] % 2
                cnt["io"] += 1
                t0 = tok0 + tb * 128
                P.op("sp", lambda e, i=i, tb=tb: e.dma_start(out=iost[i][:], in_=src[tb * 128:(tb + 1) * 128, :]),
                     writes=[R_io[i]], dsem=S_io[i])
                for h in range(2):
                    bank = 2 + 2 * (tb % 2) + h
                    for kk in range(4):
                        k = h * 4 + kk
                        P.op("pe", lambda e, i=i, k=k, kk=kk, bank=bank: e.transpose(
                            PB[bank][:, kk * 128:(kk + 1) * 128], iost[i][:, k * 128:(k + 1) * 128], ident[:]),
                            reads=[R_io[i], R_ident], writes=[RB[bank]])
                    s = t0 // 512
                    copy_op(alt_eng(), xT[:, h * 4:(h + 1) * 4, t0:t0 + 128],
                            PB[bank][:].rearrange("p (k t) -> p k t", t=128),
                            reads=[RB[bank]], writes=[R_xT[h * 4 + kk][s] for kk in range(4)])

        P.barrier()
        LOAD_SIDE = ada_steps(0, 0, 6)
        load_tokens(xs, 0, TS, LOAD_SIDE)
        load_tokens(xp, TS, TP, LOAD_SIDE)
        P.barrier()

        def mod_ap(s, j, k):
            return mods_sets[MODS["set"]][:, s, j * 8 + k:j * 8 + k + 1]

        def lng(l, i, k):
            return spT[:, l, 72 + i * 8 + k:72 + i * 8 + k + 1]

        def lnb(l, i, k):
            return spT[:, l, 96 + i * 8 + k:96 + i * 8 + k + 1]

        def layer_norm(l, i, st_idx, n=512, c0=0):
            ln_prep(l, i, st_idx, n, c0)
            ln_rest(l, i, st_idx, n, c0)

        def ln_prep(l, i, st_idx, n=512, c0=0):
            t0 = st_idx * 512 + c0
            rx = [R_xT[k][st_idx] for k in range(KC)]
            for k in range(KC):
                P.op("act", lambda e, k=k: e.copy(zb[:, k, 0:n], xT[:, k, t0:t0 + n]), reads=[rx[k]], writes=[R_zb])
                P.op("act", lambda e, k=k: e.activation(zq[:, k, 0:n], xT[:, k, t0:t0 + n], AF.Square),
                     reads=[rx[k]], writes=[R_zq])

        def ln_rest(l, i, st_idx, n=512, c0=0):
            t0 = st_idx * 512 + c0
            rx = [R_xT[k][st_idx] for k in range(KC)]
            for k in range(KC):
                P.op("pe", lambda e, k=k: e.matmul(PB[6][:, 0:n], onesM[:], zb[:, k, 0:n], start=(k == 0), stop=(k == KC - 1)),
                     reads=[R_zb, R_ones], writes=[RB[6]])
            for k in range(KC):
                P.op("pe", lambda e, k=k: e.matmul(PB[7][:, 0:n], onesM[:], zq[:, k, 0:n], start=(k == 0), stop=(k == KC - 1)),
                     reads=[R_zq, R_ones], writes=[RB[7]])
            P.op("act", lambda e: e.copy(st_mean[:, 0:n], PB[6][:, 0:n]), reads=[RB[6]], writes=[R_mean])
            P.op("dve", lambda e: e.tensor_tensor(st_a[:, 0:n], st_mean[:, 0:n], st_mean[:, 0:n], ALU.mult),
                 reads=[R_mean], writes=[R_sta])
            P.op("dve", lambda e: e.tensor_tensor(st_a[:, 0:n], PB[7][:, 0:n], st_a[:, 0:n], ALU.subtract),
                 reads=[RB[7], R_sta], writes=[R_sta])
            P.op("act", lambda e: e.activation(st_a[:, 0:n], st_a[:, 0:n], AF.Ln, bias=LN_EPS / (ALPHA * ALPHA), scale=1.0),
                 reads=[R_sta], writes=[R_sta])
            P.op("act", lambda e: e.activation(st_rstd[:, 0:n], st_a[:, 0:n], AF.Exp, scale=-0.5), reads=[R_sta], writes=[R_rstd])
            P.op("dve", lambda e: e.tensor_tensor(st_mr[:, 0:n], st_mean[:, 0:n], st_rstd[:, 0:n], ALU.mult),
                 reads=[R_mean, R_rstd], writes=[R_mr])
            for k in range(KC):
                j = cnt["lt"] % 2
                cnt["lt"] += 1
                P.op("dve", lambda e, k=k, j=j: e.tensor_tensor(lt[j][:, 0:n], xT[:, k, t0:t0 + n], st_rstd[:, 0:n], ALU.mult),
                     reads=[rx[k], R_rstd], writes=[R_lt[j]])
                P.op("dve", lambda e, j=j: e.tensor_tensor(lt[j][:, 0:n], lt[j][:, 0:n], st_mr[:, 0:n], ALU.subtract),
                     reads=[R_lt[j], R_mr], writes=[R_lt[j]])
                P.op("act", lambda e, k=k, j=j: e.activation(xT[:, k, t0:t0 + n], lt[j][:, 0:n], AF.Identity,
                                                             bias=lnb(l, i, k), scale=lng(l, i, k)),
                     reads=[R_lt[j], R_spT], writes=[rx[k]])

        def modulate(dst, dst_res, st_idx, ncol, s, jsh, jsc, dcol0=0, c0=0):
            t0 = st_idx * 512 + c0
            for k in range(KC):
                P.op("act", lambda e, k=k, b_=mod_ap(s, jsh, k), s_=mod_ap(s, jsc, k): e.activation(
                    dst[:, k, dcol0:dcol0 + ncol], xT[:, k, t0:t0 + ncol], AF.Identity, bias=b_, scale=s_),
                     reads=[R_xT[k][st_idx], cur_rm()], writes=[dst_res])

        def half_ffn(l, i, pre_ln=None, side=None):
            jb = 0 if i == 0 else 6
            lni = 0 if i == 0 else 2
            bts = ((0, 1), (2, 3), (4,))
            if pre_ln is not None:
                for st_idx in bts[0]:
                    layer_norm(l, pre_ln, st_idx)
            prev = None
            for bt, sts in enumerate(bts):
                nst = len(sts)
                if bt == 0:
                    for si, st_idx in enumerate(sts):
                        s = 1 if st_idx == 4 else 0
                        modulate(hT, R_hT[si], st_idx, 512, s, jb + 0, jb + 1, dcol0=si * 512)
                for f in range(NF):
                    wi = cnt["wgu"] % 4
                    cnt["wgu"] += 1
                    for ww, src in ((0, wg), (1, wu)):
                        P.op("pool", lambda e, wi=wi, ww=ww, src=src, f=f: e.dma_start(
                            out=wgu[wi][:, ww, :, :],
                            in_=src[l, i].rearrange("(k p) f -> p k f", p=128)[:, :, f * 128:(f + 1) * 128]),
                            reads=[P.epoch], writes=[R_wgu[wi]], dsem=S_wgu[wi])
                    for si, st_idx in enumerate(sts):
                        bsel = cnt["sg"] % 2
                        cnt["sg"] += 1
                        bg, bu_ = bsel, 2 + bsel
                        for ww, bank in ((0, bg), (1, bu_)):
                            for k in range(KC):
                                P.op("pe", lambda e, wi=wi, ww=ww, k=k, si=si, bank=bank: e.matmul(
                                    PB[bank][:, :], wgu[wi][:, ww, k, :],
                                    hT[:, k, si * 512:(si + 1) * 512], start=(k == 0), stop=(k == KC - 1)),
                                    reads=[R_wgu[wi], R_hT[si]], writes=[RB[bank]])
                        P.op("act", lambda e, bsel=bsel, bg=bg: e.activation(sg[bsel][:], PB[bg][:, :], AF.Silu),
                             reads=[RB[bg]], writes=[R_sg[bsel]])
                        P.op("dve", lambda e, bsel=bsel, bu_=bu_, f=f, si=si: e.tensor_tensor(
                            aT[:, f, si * 512:(si + 1) * 512], sg[bsel][:], PB[bu_][:, :], ALU.mult),
                            reads=[R_sg[bsel], RB[bu_]], writes=[R_aT[f][si]])
                jobs = []
                nxt_done = False
                if pre_ln is not None and bt + 1 < len(bts):
                    jobs += [(pre_ln, st_idx) for st_idx in bts[bt + 1]]
                if prev is not None:
                    jobs += [(lni, st_idx) for st_idx in prev]
                for dc in range(KC):
                    wis = []
                    for hf in range(2):
                        wi = cnt["wdn"] % 4
                        cnt["wdn"] += 1
                        wis.append(wi)
                        P.op("pool", lambda e, wi=wi, dc=dc, hf=hf: e.dma_start(
                            out=wdn[wi][:],
                            in_=wd[l, i].rearrange("(f p) d -> p f d", p=128)[:, hf * 11:(hf + 1) * 11, dc * 128:(dc + 1) * 128]),
                            reads=[P.epoch], writes=[R_wdn[wi]], dsem=S_wdn[wi])
                    for si, st_idx in enumerate(sts):
                        s = 1 if st_idx == 4 else 0
                        bank = (4, 5, 0, 1, 2)[cnt["p2"] % 5]
                        cnt["p2"] += 1
                        for f in range(NF):
                            P.op("pe", lambda e, wi=wis[f // 11], f=f, si=si, bank=bank: e.matmul(
                                PB[bank][:, :], wdn[wi][:, f % 11, :], aT[:, f, si * 512:(si + 1) * 512],
                                start=(f == 0), stop=(f == NF - 1)),
                                reads=[R_wdn[wis[f // 11]], R_aT[f][si]], writes=[RB[bank]])
                        t0 = st_idx * 512
                        P.op("dve", lambda e, dc=dc, bank=bank, t0=t0, g_=mod_ap(s, jb + 2, dc): e.scalar_tensor_tensor(
                            xT[:, dc, t0:t0 + 512], PB[bank][:, :], g_, xT[:, dc, t0:t0 + 512],
                            ALU.mult, ALU.add),
                            reads=[RB[bank], cur_rm(), R_xT[dc][st_idx]], writes=[R_xT[dc][st_idx]])
                    if jobs and dc % 2 == 0:
                        ln_prep(l, jobs[0][0], jobs[0][1])
                    if jobs and dc % 2 == 1:
                        lj, sj = jobs.pop(0)
                        ln_rest(l, lj, sj)
                    if side:
                        side.pop(0)()
                    if dc == 5 and bt + 1 < len(bts) and not jobs:
                        for si2, st2 in enumerate(bts[bt + 1]):
                            modulate(hT, R_hT[si2], st2, 512, 1 if st2 == 4 else 0, jb + 0, jb + 1, dcol0=si2 * 512)
                        nxt_done = True
                for lj, sj in jobs:
                    layer_norm(l, lj, sj)
                jobs = []
                if bt + 1 < len(bts) and not nxt_done:
                    for si2, st2 in enumerate(bts[bt + 1]):
                        modulate(hT, R_hT[si2], st2, 512, 1 if st2 == 4 else 0, jb + 0, jb + 1, dcol0=si2 * 512)
                prev = sts
            for st_idx in prev:
                layer_norm(l, lni, st_idx)
            while side:
                side.pop(0)()

        uT_all = carve(O_WORK, [128, KC, 2048], BF16)
        ucs_g = carve(O_WORK + 32768, [128, 16, 512], BF16)
        f_win = carve(O_WORK + 32768, [128, KC, 1024], BF16)
        dfts = [carve(O_WGU, [128, 16, 2, 256], BF16), carve(O_WORK + 49152, [128, 16, 2, 256], BF16)]
        f_wout = carve(O_ZB, [128, KC, 1024], BF16)
        f_hT = carve(O_ST, [128, KC, 512], BF16)
        f_fT = [carve(O_LT + 1024 * i, [128, 2, 256], BF16) for i in range(2)]
        R_uT = [Res("uT%d" % c) for c in range(KC)]
        R_ucs, R_fwin, R_fwout, R_fhT, R_csG = Res("ucs"), Res("fwin"), Res("fwout"), Res("fhT"), Res("csG")
        R_dft1 = Res("dft1")
        R_fT = [Res("fT0"), Res("fT1")]
        S_dft = [dsem("dft0"), dsem("dft1")]
        S_fwin, S_fwout = dsem("fwin"), dsem("fwout")
        P.op("sp", lambda e: e.dma_start(out=csG[:], in_=csg_d.rearrange("(c p) k -> p c k", p=128)),
             writes=[R_csG], dsem=dsem("csg"))
        cnt["dft"] = 0
        cnt["fb"] = 0

        def f_mixer(l):
            fi = l // 2
            P.op("pool", lambda e: e.dma_start(out=f_wout, in_=f_w_out[fi].rearrange("(k p) c -> p k c", p=128)),
                 reads=[P.epoch], writes=[R_fwout], dsem=S_fwout)
            for (tok0, L, s) in ((0, TS, 0), (TS, 256, 1), (TS + 256, 256, 1)):
                nb = L // 128
                dsrc = dft_big if L == TS else dft_small
                P.op("pool", lambda e: e.dma_start(out=f_win, in_=f_w_in[fi].rearrange("(k p) c -> p k c", p=128)),
                     reads=[P.epoch], writes=[R_fwin, R_ucs], dsem=S_fwin)
                n = min(512, L)
                for tt in range(L // n):
                    t0 = tok0 + tt * n
                    st_idx = t0 // 512
                    modulate(f_hT, R_fhT, st_idx, n, s, 3, 4, c0=t0 - st_idx * 512)
                    for c in range(KC):
                        bank = cnt["fb"] % 2
                        cnt["fb"] += 1
                        for k in range(KC):
                            P.op("pe", lambda e, c=c, k=k, bank=bank, n=n: e.matmul(
                                PB[bank][:, 0:n], f_win[:, k, c * 128:(c + 1) * 128], f_hT[:, k, 0:n],
                                start=(k == 0), stop=(k == KC - 1)),
                                reads=[R_fwin, R_fhT], writes=[RB[bank]])
                        copy_op(alt_eng(), uT_all[:, c, tt * n:(tt + 1) * n], PB[bank][:, 0:n], reads=[RB[bank]], writes=[R_uT[c]])
                for g in range(4):
                    for tb in range(nb):
                        bank = 2 + tb % 2
                        for cc in range(2):
                            P.op("pe", lambda e, tb=tb, cc=cc, bank=bank, g=g: e.matmul(
                                PB[bank][:, :], uT_all[:, 2 * g + cc, tb * 128:(tb + 1) * 128], csG[:, cc, :],
                                start=(cc == 0), stop=(cc == 1)),
                                reads=[R_uT[2 * g + cc], R_csG], writes=[RB[bank]])
                        copy_op(alt_eng(), ucs_g[:, tb, :], PB[bank][:, :], reads=[RB[bank]], writes=[R_ucs, R_fwin])
                    for kt in range(L // 256):
                        di = cnt["dft"] % 2
                        cnt["dft"] += 1
                        rd = list(R_wgu) if di == 0 else [R_dft1]
                        for sgn in range(2):
                            P.op("sp", lambda e, di=di, kt=kt, nb=nb, dsrc=dsrc, sgn=sgn: e.dma_start(
                                out=dfts[di][:, 0:nb, sgn, :],
                                in_=dsrc.rearrange("(i p) s k -> p i s k", p=128)[:, :, sgn, kt * 256:(kt + 1) * 256]),
                                reads=[P.epoch], writes=rd, dsem=S_dft[di])
                        fb = kt % 2
                        for cc in range(2):
                            bank = 4 + cc
                            for i in range(nb):
                                for sgn in range(2):
                                    P.op("pe", lambda e, di=di, cc=cc, i=i, sgn=sgn, bank=bank, nb=nb: e.matmul(
                                        PB[bank][:, 0:256], ucs_g[:, i, sgn * 256 + cc * 128:sgn * 256 + (cc + 1) * 128],
                                        dfts[di][:, i, sgn, :], start=(i == 0 and sgn == 0), stop=(i == nb - 1 and sgn == 1)),
                                        reads=[R_ucs] + rd, writes=[RB[bank]])
                            copy_op(alt_eng(), f_fT[fb][:, cc, :], PB[bank][:, 0:256], reads=[RB[bank]], writes=[R_fT[fb]])
                        ta = tok0 + kt * 256
                        st_idx = ta // 512
                        for dc in range(KC):
                            bank = 6 + dc % 2
                            for cc in range(2):
                                P.op("pe", lambda e, fb=fb, cc=cc, dc=dc, bank=bank, g=g: e.matmul(
                                    PB[bank][:, 0:256], f_wout[:, 2 * g + cc, dc * 128:(dc + 1) * 128], f_fT[fb][:, cc, :],
                                    start=(cc == 0), stop=(cc == 1)),
                                    reads=[R_fwout, R_fT[fb]], writes=[RB[bank]])
                            P.op("dve", lambda e, dc=dc, bank=bank, ta=ta, g_=mod_ap(s, 5, dc): e.scalar_tensor_tensor(
                                xT[:, dc, ta:ta + 256], PB[bank][:, 0:256], g_, xT[:, dc, ta:ta + 256],
                                ALU.mult, ALU.add),
                                reads=[RB[bank], cur_rm(), R_xT[dc][st_idx]], writes=[R_xT[dc][st_idx]])

        A_SCALE = 0.125
        B_SCALE = 96.0 ** -0.5
        W0 = O_WORK
        qaT = carve(W0, [128, 4, 2048], BF16)
        oTB = carve(W0, [128, 4, 2048], BF16)
        kaT = carve(W0 + 16384, [128, 2304], BF16)
        va = carve(W0 + 20992, [128, 18, 2, 128], BF16)
        wuq_h = [carve(W0 + 16384 + 768 * i, [128, 3, 128], BF16) for i in range(2)]
        wukv_h = [carve(W0 + 18432 + 768 * i, [128, 2, 192], BF16) for i in range(2)]
        wuq_nat = carve(W0 + 20480, [128, 3, 96], BF16)
        cqnT = carve(W0 + 30208, [128, 3, 2048], BF16)
        ckvnT = carve(W0 + 42496, [128, 2, 2304], BF16)
        kpeT = carve(W0 + 51712, [128, 2304], BF16)
        PT = [carve(W0 + 56320 + 1024 * i, [128, 512], BF16) for i in range(3)]
        zq_ab = carve(W0 + 56320, [128, 3, 512], BF16)
        oblk = [carve(W0 + 59392 + 1024 * i, [128, 4, 128], BF16) for i in range(2)]
        stgk = carve(W0 + 59392, [128, 2, 64], F32)
        gckv_bc = carve(W0 + 59392 + 512, [128, 256], F32)
        rden = carve(W0 + 61440, [128, 512], F32)
        tmpden = carve(W0 + 63488, [128, 512], F32)
        stg = [tmpden, rden]
        wsl = [carve(O_WGU + 2048 * i, [128, KC, 128], BF16) for i in range(3)]
        wperm = [carve(O_WGU + 6144 + 2048 * i, [128, KC, 128], BF16) for i in range(2)]
        ropeBk = carve(O_WGU + 10240, [128, 2, 2048], F32)
        hb = [dict(q=carve(O_WGU + 13312 * i, [128, 2048], BF16),
                   k=carve(O_WGU + 13312 * i + 4096, [128, 2304], BF16),
                   v=carve(O_WGU + 13312 * i + 8704, [128, 18, 128], BF16)) for i in range(2)]
        ropeT = carve(O_ZB, [128, 2, 2048], F32)
        a_hT = carve(O_ST, [128, KC, 512], BF16)
        ta_ = carve(O_ST + 8192, [128, 512], F32)
        tb_ = carve(O_ST + 10240, [128, 512], F32)
        woutS = carve(O_ST, [128, 4, 1024], BF16)

        R_qaT = [Res("qaT%d" % j) for j in range(4)]
        R_kaT, R_va, R_cqn, R_ckvn, R_kpeT = Res("kaT"), Res("va"), Res("cqn"), Res("ckvn"), Res("kpeT")
        R_PT = [Res("PT%d" % i) for i in range(3)]
        R_oblk = [Res("oblk0"), Res("oblk1")]
        R_rden, R_tmpden, R_stgk, R_gbc, R_zqab = Res("rden"), Res("tmpden"), Res("stgk"), Res("gbc"), Res("zqab")
        R_esk = Res("esink_t")
        R_stg = [R_tmpden, R_rden]
        R_wsl = [Res("wsl%d" % i) for i in range(3)]
        R_wperm = [Res("wperm0"), Res("wperm1")]
        R_ropeBk, R_ropeT = Res("ropeBk"), Res("ropeT")
        R_hb = [dict(q=Res("hbq%d" % i), k=Res("hbk%d" % i), v=Res("hbv%d" % i)) for i in range(2)]
        R_ahT, R_ta, R_tb = Res("ahT"), Res("ta"), Res("tb")
        R_woutS = Res("woutS")
        R_wuqh = [Res("wuqh0"), Res("wuqh1")]
        R_wukvh = [Res("wukvh0"), Res("wukvh1")]
        R_wuqn = Res("wuqn")
        R_identB, R_masks, R_esink, R_gvec = Res("identB"), Res("masks"), Res("esink"), Res("gvec")
        S_wsl = [dsem("wsl%d" % i) for i in range(3)]
        S_rope, S_ropeBk, S_woutS, S_wuq = dsem("rope"), dsem("ropeBk"), dsem("woutS"), dsem("wuq")
        S_wukv = [dsem("wukv0"), dsem("wukv1")]
        S_stg = [dsem("stg0"), dsem("stg1")]
        S_stgk, S_esink, S_gvec, S_gbc = dsem("stgk"), dsem("esink"), dsem("gvec"), dsem("gbc")
        cnt.update(wsl=0, pt=0, po=0, ob=0, ps=0, wp=0, s3=0)

        def mm(out, lhsT, rhs, start, stop, reads, writes):
            MM_TAGS.append(CUR_TAG[0])
            P.op("pe", lambda e: e.matmul(out, lhsT, rhs, start=start, stop=stop), reads=reads, writes=writes)

        def tpose(out, in_, idn, reads, writes):
            P.op("pe", lambda e: e.transpose(out, in_, idn), reads=reads, writes=writes)

        def ttop(out, in0, in1, op, reads, writes):
            P.op("dve", lambda e: e.tensor_tensor(out, in0, in1, op), reads=reads, writes=writes)

        def actop(out, in_, func, reads, writes, bias=None, scale=None):
            kw = {}
            if bias is not None:
                kw["bias"] = bias
            if scale is not None:
                kw["scale"] = scale
            P.op("act", lambda e: e.activation(out, in_, func, **kw), reads=reads, writes=writes)

        def dma(eng, out, in_, reads, writes, ds):
            P.op(eng, lambda e: e.dma_start(out=out, in_=in_), reads=reads, writes=writes, dsem=ds)

        def memset(ap, val, writes):
            P.op("dve", lambda e: e.memset(ap, val), writes=writes)

        def xacc(dc, bank, ncol, tq, s):
            st_idx = tq // 512
            g_ = mod_ap(s, 5, dc)
            P.op("dve", lambda e: e.scalar_tensor_tensor(
                xT[:, dc, tq:tq + ncol], PB[bank][:, 0:ncol], g_, xT[:, dc, tq:tq + ncol], ALU.mult, ALU.add),
                reads=[RB[bank], cur_rm(), R_xT[dc][st_idx]], writes=[R_xT[dc][st_idx]])

        P.op("dve", lambda e: e.tensor_copy(identB[:], ident[:]), reads=[R_ident], writes=[R_identB])
        dma("sp", masks[:], masks_d, [], [R_masks], dsem("masks"))

        def load_wgroup(i_l, col0, ncols, perm=None):
            si = cnt["wsl"] % 3
            cnt["wsl"] += 1
            src = ab_w_in[i_l].rearrange("(k p) c -> p k c", p=128)
            if isinstance(col0, tuple):
                for gi, c0 in enumerate(col0):
                    dma("pool", wsl[si][:, :, gi * 64:(gi + 1) * 64], src[:, :, c0:c0 + 64], [P.epoch], [R_wsl[si]], S_wsl[si])
            else:
                dma("pool", wsl[si][:, :, 0:ncols], src[:, :, col0:col0 + ncols], [P.epoch], [R_wsl[si]], S_wsl[si])
            if perm is None:
                return wsl[si], R_wsl[si], None, None
            pi = cnt["wp"] % 2
            cnt["wp"] += 1
            if perm == "A":
                srcv = wsl[si][:].rearrange("p k (g hf b i) -> p (k g) hf b i", g=2, hf=2, b=2)
                dstv = wperm[pi][:].rearrange("p k (g b hf i) -> p (k g) b hf i", g=2, hf=2, b=2)
                for b_ in range(2):
                    copy_op("act", dstv[:, :, b_, :, :], srcv[:, :, :, b_, :], reads=[R_wsl[si]], writes=[R_wperm[pi]])
            else:
                memset(wperm[pi][:, :, 0:64], 0.0, [R_wperm[pi]])
                srcv = wsl[si][:, :, 0:32].rearrange("p k (hf b i) -> p k hf b i", hf=2, b=2)
                dstv = wperm[pi][:, :, 0:64].rearrange("p k (b z hf i) -> p k b z hf i", b=2, z=2, hf=2)
                for b_ in range(2):
                    copy_op("dve", dstv[:, :, b_, 0, :, :], srcv[:, :, :, b_, :], reads=[R_wsl[si]], writes=[R_wperm[pi]])
            return wsl[si], R_wsl[si], wperm[pi], R_wperm[pi]

        def rope_A(dst, dst_res, ps, ps_res, n, tcol0):
            ttop(ta_[:, 0:n], ps, ropeT[:, 0, tcol0:tcol0 + n], ALU.mult, [ps_res, R_ropeT], [R_ta])
            for base in (0, 64):
                ttop(tb_[base:base + 32, 0:n], ps[base + 32:base + 64], ropeT[base + 32:base + 64, 1, tcol0:tcol0 + n],
                     ALU.mult, [ps_res, R_ropeT], [R_tb])
                ttop(tb_[base + 32:base + 64, 0:n], ps[base:base + 32], ropeT[base:base + 32, 1, tcol0:tcol0 + n],
                     ALU.mult, [ps_res, R_ropeT], [R_tb])
            ttop(dst, ta_[:, 0:n], tb_[:, 0:n], ALU.add, [R_ta, R_tb], [dst_res])

        def rope_B(dst64, dst_res, ps64, ps_res, n, tcol0, table, table_res):
            ttop(ta_[0:64, 0:n], ps64, table[0:64, 0, tcol0:tcol0 + n], ALU.mult, [ps_res, table_res], [R_ta])
            ttop(tb_[0:32, 0:n], ps64[32:64], table[32:64, 1, tcol0:tcol0 + n], ALU.mult, [ps_res, table_res], [R_tb])
            ttop(tb_[32:64, 0:n], ps64[0:32], table[0:32, 1, tcol0:tcol0 + n], ALU.mult, [ps_res, table_res], [R_tb])
            ttop(dst64, ta_[0:64, 0:n], tb_[0:64, 0:n], ALU.add, [R_ta, R_tb], [dst_res])

        def rms_feat(banks, nchunks, n, gcol0, dstT, dst_res, dcol0, nfeat):
            for c in range(nchunks):
                actop(zq_ab[:, c, 0:n], PB[banks[c]][:, 0:n], AF.Square, [RB[banks[c]]], [R_zqab])
            for c in range(nchunks):
                mm(PB[3][:, 0:n], onesM[:], zq_ab[:, c, 0:n], c == 0, c == nchunks - 1, [R_zqab, R_ones], [RB[3]])
            actop(tmpden[:, 0:n], PB[3][:, 0:n], AF.Ln, [RB[3]], [R_tmpden], bias=RMS_EPS, scale=1024.0 / nfeat)
            actop(rden[:, 0:n], tmpden[:, 0:n], AF.Exp, [R_tmpden], [R_rden], scale=-0.5)
            for c in range(nchunks):
                ttop(ta_[:, 0:n], PB[banks[c]][:, 0:n], rden[:, 0:n], ALU.mult, [RB[banks[c]], R_rden], [R_ta])
                actop(dstT[:, c, dcol0:dcol0 + n], ta_[:, 0:n], AF.Identity, [R_ta, R_gvec], [dst_res],
                      scale=gvec[:, gcol0 + c:gcol0 + c + 1])

        def ab_mixer(l, do_sample=True):
            i_l = l // 2
            dma("sp", esink[:], a_sink[i_l].partition_broadcast(128), [], [R_esink], S_esink)
            actop(esink[:], esink[:], AF.Exp, [R_esink], [R_esink])
            for col, srcv in ((0, b_g_cq), (3, b_g_ckv)):
                nch = 3 if col == 0 else 2
                P.op("sp", lambda e, col=col, srcv=srcv, nch=nch: e.dma_start(
                    out=gvec[:, col:col + nch], in_=srcv[i_l].rearrange("(k p) -> p k", p=128), allow_slow_non_contiguous=True),
                    writes=[R_gvec], dsem=S_gvec)
            seqs = ((0, TS, 0, True, None), (TS, 512, 1, False, 0))
            for (tok0, L, s, latent, pidx) in seqs:
                if latent and not do_sample:
                    continue
                ab_seq(i_l, tok0, L, s, latent, pidx)
                P.barrier()

        def ab_seq(i_l, tok0, L, s, latent, pidx):
            nqb = L // 128
            koff = 256 if latent else 0
            Lk = L + koff
            nkc = Lk // 128
            n = min(512, L)
            ntile = L // n
            memset(va[:, 0:nkc, :, 64:128], 1.0, [R_va])
            if latent:
                dma("sp", ropeT, ropeA_d, [P.epoch], [R_ropeT], S_rope)
                dma("sp", ropeBk[0:64], ropeB_d, [P.epoch], [R_ropeBk] + R_wgu + R_wdn, S_ropeBk)
                for b_ in range(2):
                    rows = slice(b_ * 128, (b_ + 1) * 128)
                    dma("sp", stg[b_][:, 0:128], ca_k[i_l, rows, :], [P.epoch], [R_stg[b_]], S_stg[b_])
                    dma("sp", stg[b_][:, 128:256], ca_v[i_l, rows, :], [P.epoch], [R_stg[b_]], S_stg[b_])
                    dma("sp", stg[b_][:, 256:512], cb_ckv[i_l, rows, :], [P.epoch], [R_stg[b_]], S_stg[b_])
                    if b_ == 0:
                        for bb in range(2):
                            dma("sp", stgk[:, bb, 0:32], cb_kpe[i_l, bb * 128:(bb + 1) * 128, :], [P.epoch], [R_stgk], S_stgk)
                    srcv = stg[b_][:, 0:128].rearrange("p (g hf b i) -> p g hf b i", g=2, hf=2, b=2)
                    dstv = ta_[:, 0:128].rearrange("p (g b hf i) -> p g b hf i", g=2, hf=2, b=2)
                    for b2 in range(2):
                        copy_op("dve", dstv[:, :, b2, :, :], srcv[:, :, :, b2, :], reads=[R_stg[b_]], writes=[R_ta])
                    tpose(PB[6][:, 0:128], ta_[:, 0:128], ident[:], [R_ta, R_ident], [RB[6]])
                    copy_op("act", kaT[:, rows], PB[6][:, 0:128], reads=[RB[6]], writes=[R_kaT])
                    copy_op("act", va[:, b_, :, 0:64], stg[b_][:, 128:256].rearrange("p (g d) -> p g d", g=2),
                            reads=[R_stg[b_]], writes=[R_va])
                    for c in range(2):
                        tpose(PB[7][:, c * 128:(c + 1) * 128], stg[b_][:, 256 + c * 128:256 + (c + 1) * 128], ident[:],
                              [R_stg[b_], R_ident], [RB[7]])
                    copy_op("dve", ckvnT[:, :, rows], PB[7][:, 0:256].rearrange("p (c t) -> p c t", c=2), reads=[RB[7]], writes=[R_ckvn])
                    memset(tb_[:, 0:64], 0.0, [R_tb])
                    srcv = stgk[:, b_, 0:32].rearrange("p (hf b i) -> p hf b i", hf=2, b=2)
                    dstv = tb_[:, 0:64].rearrange("p (b z hf i) -> p b z hf i", b=2, z=2, hf=2)
                    for b2 in range(2):
                        copy_op("dve", dstv[:, b2, 0, :, :], srcv[:, :, b2, :], reads=[R_stgk], writes=[R_tb])
                    tpose(PB[6][0:64, 128:256], tb_[:, 0:64], ident[:], [R_tb, R_ident], [RB[6]])
                    copy_op("act", kpeT[0:64, rows], PB[6][0:64, 128:256], reads=[RB[6]], writes=[R_kpeT])
            else:
                dma("sp", gckv_bc, b_g_ckv[i_l].partition_broadcast(128), [P.epoch], [R_gbc], S_gbc)
            if AB_STOP[0] < 1:
                return
            CUR_TAG[0] = "P" + ("s" if latent else "p")
            for tt in range(ntile):
                t0 = tok0 + tt * n
                st_idx = t0 // 512
                lc0 = tt * n
                modulate(a_hT, R_ahT, st_idx, n, s, 3, 4, c0=t0 - st_idx * 512)

                def proj(bank, w, wr, mrows=128):
                    for k in range(KC):
                        mm(PB[bank][0:mrows, 0:n], w[:, k, 0:mrows], a_hT[:, k, 0:n], k == 0, k == KC - 1, [wr, R_ahT], [RB[bank]])

                def tok_major(bank, col, w, wr, tb, ncol):
                    for k in range(KC):
                        mm(PB[bank][:, col:col + ncol], a_hT[:, k, tb * 128:(tb + 1) * 128], w[:, k, 0:ncol],
                           k == 0, k == KC - 1, [wr, R_ahT], [RB[bank]])

                for j in range(4):
                    _, _, wp_, wpr = load_wgroup(i_l, (j * 64, (4 + j) * 64), 128, perm="A")
                    bank = j % 3
                    proj(bank, wp_, wpr)
                    if latent:
                        rope_A(qaT[:, j, lc0:lc0 + n], R_qaT[j], PB[bank][:, 0:n], RB[bank], n, lc0)
                    else:
                        copy_op(alt_eng(), qaT[:, j, lc0:lc0 + n], PB[bank][:, 0:n], reads=[RB[bank]], writes=[R_qaT[j]])
                if AB_STOP[0] < 1.2:
                    return
                wn, wnr, wp_, wpr = load_wgroup(i_l, 512, 128, perm="A")
                proj(1, wp_, wpr)
                if latent:
                    rope_A(kaT[:, koff + lc0:koff + lc0 + n], R_kaT, PB[1][:, 0:n], RB[1], n, lc0)
                else:
                    copy_op(alt_eng(), kaT[:, lc0:lc0 + n], PB[1][:, 0:n], reads=[RB[1]], writes=[R_kaT])
                    for tb in range(n // 128):
                        tok_major(4 + tb % 2, (tb // 2) * 256, wn, wnr, tb, 128)
                if AB_STOP[0] < 1.3:
                    return
                wn, wnr, _, _ = load_wgroup(i_l, 640, 128)
                for tb in range(n // 128):
                    bank = 4 + tb % 2
                    cb_ = (tb // 2) * 256
                    tok_major(bank, cb_ + 128, wn, wnr, tb, 128)
                    ch = (koff + lc0) // 128 + tb
                    copy_op("act", va[:, ch, :, 0:64], PB[bank][:, cb_ + 128:cb_ + 256].rearrange("p (g d) -> p g d", g=2),
                            reads=[RB[bank]], writes=[R_va])
                    if not latent and "nostg" not in DBG:
                        sb_ = tb % 2
                        copy_op("dve", stg[sb_][:, 0:256], PB[bank][:, cb_:cb_ + 256], reads=[RB[bank]], writes=[R_stg[sb_]])
                        if "nostore" in DBG:
                            continue
                        pidx, r0 = tb // 2, (tb % 2) * 128
                        dma("sp", nk_o[pidx, i_l, r0:r0 + 128, :], stg[sb_][:, 0:128], [R_stg[sb_]], [R_stg[sb_]], S_stg[sb_])
                        dma("sp", nv_o[pidx, i_l, r0:r0 + 128, :], stg[sb_][:, 128:256], [R_stg[sb_]], [R_stg[sb_]], S_stg[sb_])
                if AB_STOP[0] < 1.4:
                    return
                for c in range(3):
                    wn, wnr, _, _ = load_wgroup(i_l, 768 + 128 * c, 128)
                    proj(c, wn, wnr)
                rms_feat((0, 1, 2), 3, n, 0, cqnT, R_cqn, lc0, 384)
                if AB_STOP[0] < 1.5:
                    return
                for c in range(2):
                    wn, wnr, _, _ = load_wgroup(i_l, 1152 + 128 * c, 128)
                    proj(c, wn, wnr)
                    if not latent:
                        for tb in range(n // 128):
                            tok_major(4 + tb % 2, (tb // 2) * 256 + 128 * c, wn, wnr, tb, 128)
                rms_feat((0, 1), 2, n, 3, ckvnT, R_ckvn, koff + lc0, 256)
                if AB_STOP[0] < 1.6:
                    return
                wn, wnr, wp_, wpr = load_wgroup(i_l, 1408, 32, perm="K")
                proj(2, wp_, wpr, mrows=64)
                if latent:
                    rope_B(kpeT[0:64, koff + lc0:koff + lc0 + n], R_kpeT, PB[2][0:64, 0:n], RB[2], n, lc0, ropeBk, R_ropeBk)
                else:
                    copy_op(alt_eng(), kpeT[0:64, lc0:lc0 + n], PB[2][0:64, 0:n], reads=[RB[2]], writes=[R_kpeT])
                    for tb in range(n // 128):
                        bank = 4 + tb % 2
                        cb_ = (tb // 2) * 256
                        tok_major(3, tb * 32, wn, wnr, tb, 32)
                        sb_ = tb % 2
                        actop(ta_[:, 0:256], PB[bank][:, cb_:cb_ + 256], AF.Square, [RB[bank]], [R_ta])
                        P.op("dve", lambda e: e.reduce_sum(tb_[:, 0:1], ta_[:, 0:256], axis=mybir.AxisListType.X),
                             reads=[R_ta], writes=[R_tb])
                        actop(tb_[:, 1:2], tb_[:, 0:1], AF.Sqrt, [R_tb], [R_tb], bias=RMS_EPS, scale=1.0 / 256.0)
                        P.op("dve", lambda e: e.reciprocal(tb_[:, 2:3], tb_[:, 1:2]), reads=[R_tb], writes=[R_tb])
                        P.op("dve", lambda e, bank=bank, sb_=sb_, cb_=cb_: e.scalar_tensor_tensor(
                            stg[sb_][:, 256:512], PB[bank][:, cb_:cb_ + 256], tb_[:, 2:3], gckv_bc, ALU.mult, ALU.mult),
                            reads=[RB[bank], R_tb, R_gbc, R_stg[sb_]], writes=[R_stg[sb_]])
                        copy_op("act", stgk[:, sb_, 0:32], PB[3][:, tb * 32:(tb + 1) * 32], reads=[RB[3]], writes=[R_stgk])
                        pidx, r0 = tb // 2, (tb % 2) * 128
                        dma("sp", nckv_o[pidx, i_l, r0:r0 + 128, :], stg[sb_][:, 256:512], [R_stg[sb_]], [R_stg[sb_]], S_stg[sb_])
                        dma("sp", nkpe_o[pidx, i_l, r0:r0 + 128, :], stgk[:, sb_, 0:32], [R_stgk], [R_stgk], S_stgk)
            P.barrier()
            if AB_STOP[0] < 2:
                return
            CUR_TAG[0] = "A" + ("s" if latent else "p")
            dma("pool", woutS, ab_w_out[i_l, 0:512, :].rearrange("(c p) d -> p c d", p=128), [P.epoch], [R_woutS], S_woutS)
            esink_t = carve(O_ZB, [128, 2, 512], F32)
            oblk2 = [carve(O_ZB + 4096 + 2048 * i, [128, 4, 256], BF16) for i in range(2)]
            memset(esink_t[64:128, :, :], 0.0, [R_esk])
            for g in range(2):
                for j in range(4):
                    P.op("dve", lambda e, g=g, j=j: e.tensor_scalar(
                        esink_t[64:128, g, j * 128:(j + 1) * 128], esink_t[64:128, g, j * 128:(j + 1) * 128],
                        esink[64:128, 4 * g + j:4 * g + j + 1], None, ALU.add), reads=[R_esink, R_esk], writes=[R_esk])
            for nbp in range(nqb // 2):
                oi = cnt["ob"] % 2
                cnt["ob"] += 1
                ob = oblk2[oi]
                for sub in range(2):
                    nb = 2 * nbp + sub
                    for g in range(2):
                        if latent:
                            chunks = [(0, None), (1, None)]
                            if nb > 0:
                                chunks.append((2 + nb - 1, 0))
                            chunks.append((2 + nb, None))
                            if nb < nqb - 1:
                                chunks.append((2 + nb + 1, 1))
                        else:
                            chunks = [(2 * (nb // 2), None), (2 * (nb // 2) + 1, None)]
                        po = 4 + cnt["po"] % 2
                        cnt["po"] += 1
                        qsl = qaT[64 * g:64 * g + 64, :, nb * 128:(nb + 1) * 128]
                        pend = []
                        nch = len(chunks)
                        for ci in range(nch + 2):
                            if ci < nch:
                                c, mk = chunks[ci]
                                ps = cnt["s3"] % 3
                                cnt["s3"] += 1
                                mm(PB[ps][:, :].rearrange("p (j q) -> p j q", q=128), kaT[64 * g:64 * g + 64, c * 128:(c + 1) * 128], qsl,
                                   True, mk is None, [R_kaT] + R_qaT, [RB[ps]])
                                if mk is not None:
                                    mm(PB[ps][:, :], identB[:], masks[:, mk, :], False, True, [R_identB, R_masks], [RB[ps]])
                                pt = cnt["pt"] % 3
                                cnt["pt"] += 1
                                actop(PT[pt], PB[ps][:, :], AF.Exp, [RB[ps]], [R_PT[pt]], scale=A_SCALE)
                                pend.append((c, pt))
                            if ci >= 2:
                                c2, pt2 = pend[ci - 2]
                                mm(PB[po][:, :], va[:, c2, g, :], PT[pt2], ci - 2 == 0, ci - 2 == nch - 1, [R_va, R_PT[pt2]], [RB[po]])
                        ttop(tmpden[64:128, :], PB[po][64:128, :], esink_t[64:128, g, :], ALU.add, [RB[po], R_esk], [R_tmpden])
                        actop(tmpden[64:128, :], tmpden[64:128, :], AF.Ln, [R_tmpden], [R_tmpden])
                        actop(rden[0:64, :], tmpden[64:128, :], AF.Exp, [R_tmpden], [R_rden], scale=-1.0)
                        for par in range(2):
                            ttop(ob[64 * par:64 * par + 64, 2 * g:2 * g + 2, sub * 128:(sub + 1) * 128],
                                 PB[po][0:64, :].rearrange("p (jj par q) -> p jj par q", par=2, q=128)[:, :, par, :],
                                 rden[0:64, :].rearrange("p (jj par q) -> p jj par q", par=2, q=128)[:, :, par, :],
                                 ALU.mult, [RB[po], R_rden], [R_oblk[oi]])
                tq = tok0 + nbp * 256
                for dc in range(KC):
                    bank = 3 if dc % 2 == 0 else 7
                    for pr in range(4):
                        mm(PB[bank][:, 0:256], woutS[:, pr, dc * 128:(dc + 1) * 128], ob[:, pr, :], pr == 0, pr == 3,
                           [R_woutS, R_oblk[oi]], [RB[bank]])
                    xacc(dc, bank, 256, tq, s)
            P.barrier()
            if AB_STOP[0] < 3:
                return
            CUR_TAG[0] = "Bprep" + ("s" if latent else "p")
            dma("pool", woutS, ab_w_out[i_l, 512:1024, :].rearrange("(c p) d -> p c d", p=128), [P.epoch], [R_woutS], S_woutS)
            if latent:
                dma("sp", ropeT[0:64], ropeB_d, [P.epoch], [R_ropeT], S_rope)
            for bi in range(2):
                memset(hb[bi]["v"][:, 0:nkc, 64:128], 1.0, [R_hb[bi]["v"]])
                memset(wukv_h[bi][:, :, 0:64], 0.0, [R_wukvh[bi]])
                memset(wuq_h[bi][:, :, 0:64], 0.0, [R_wuqh[bi]])
            def b_prep(h):
                bi = h % 2
                H, RH = hb[bi], R_hb[bi]
                units = []

                def u_w():
                    dma("pool", wuq_nat, b_w_uq[i_l].rearrange("(k p) c -> p k c", p=128)[:, :, h * 96:(h + 1) * 96],
                        [P.epoch], [R_wuqn], S_wuq)
                    copy_op("dve", wuq_h[bi][:, :, 64:128], wuq_nat[:, :, 0:64], reads=[R_wuqn], writes=[R_wuqh[bi]])
                    srcv = wuq_nat[:, :, 64:96].rearrange("p k (hf b i) -> p k hf b i", hf=2, b=2)
                    dstv = wuq_h[bi][:, :, 0:64].rearrange("p k (b z hf i) -> p k b z hf i", b=2, z=2, hf=2)
                    for b2 in range(2):
                        copy_op("dve", dstv[:, :, b2, 0, :, :], srcv[:, :, :, b2, :], reads=[R_wuqn], writes=[R_wuqh[bi]])
                    dma("pool", wukv_h[bi][:, :, 64:192], b_w_ukv[i_l].rearrange("(k p) c -> p k c", p=128)[:, :, h * 128:(h + 1) * 128],
                        [P.epoch], [R_wukvh[bi]], S_wukv[bi])
                    copy_op("dve", H["k"][0:64, 0:Lk], kpeT[0:64, 0:Lk], reads=[R_kpeT], writes=[RH["k"]])
                units.append(u_w)

                def nbank():
                    bk = (6, 7, 3)[cnt["pb"] % 3]
                    cnt["pb"] += 1
                    return bk

                def u_q(tt):
                    CUR_TAG[0] = "Bprep" + ("s" if latent else "p")
                    lc0 = tt * n
                    bk = nbank()
                    for kc in range(3):
                        mm(PB[bk][:, 0:n], wuq_h[bi][:, kc, :], cqnT[:, kc, lc0:lc0 + n], kc == 0, kc == 2, [R_wuqh[bi], R_cqn], [RB[bk]])
                    copy_op("dve", H["q"][64:128, lc0:lc0 + n], PB[bk][64:128, 0:n], reads=[RB[bk]], writes=[RH["q"]])
                    if latent:
                        rope_B(H["q"][0:64, lc0:lc0 + n], RH["q"], PB[bk][0:64, 0:n], RB[bk], n, lc0, ropeT, R_ropeT)
                    else:
                        copy_op("dve", H["q"][0:64, lc0:lc0 + n], PB[bk][0:64, 0:n], reads=[RB[bk]], writes=[RH["q"]])

                def u_k(k0):
                    CUR_TAG[0] = "Bprep" + ("s" if latent else "p")
                    m = min(512, Lk - k0)
                    bk = nbank()
                    for kc in range(2):
                        mm(PB[bk][:, 0:m], wukv_h[bi][:, kc, 0:128], ckvnT[:, kc, k0:k0 + m], kc == 0, kc == 1, [R_wukvh[bi], R_ckvn], [RB[bk]])
                    copy_op("dve", H["k"][64:128, k0:k0 + m], PB[bk][64:128, 0:m], reads=[RB[bk]], writes=[RH["k"]])

                def u_v(c0):
                    CUR_TAG[0] = "Bprep" + ("s" if latent else "p")
                    nc_ = min(8, nkc - c0)
                    bk = nbank()
                    for c in range(nc_):
                        for kc in range(2):
                            mm(PB[bk][:, c * 64:(c + 1) * 64], ckvnT[:, kc, (c0 + c) * 128:(c0 + c + 1) * 128], wukv_h[bi][:, kc, 128:192],
                               kc == 0, kc == 1, [R_wukvh[bi], R_ckvn], [RB[bk]])
                    copy_op("dve", H["v"][:, c0:c0 + nc_, 0:64], PB[bk][:, 0:nc_ * 64].rearrange("p (c d) -> p c d", d=64),
                            reads=[RB[bk]], writes=[RH["v"]])

                for tt in range(ntile):
                    units.append(lambda tt=tt: u_q(tt))
                for k0 in range(0, Lk, 512):
                    units.append(lambda k0=k0: u_k(k0))
                for c0 in range(0, nkc, 8):
                    units.append(lambda c0=c0: u_v(c0))
                return units

            def b_att(h, tt, side):
                bi = h % 2
                H, RH = hb[bi], R_hb[bi]
                lc0 = tt * n
                po = 4 + cnt["po"] % 2
                cnt["po"] += 1
                if latent:
                    work = [(c, 0, n, c == 0, c == nkc - 1) for c in range(nkc)]
                else:
                    work = [(c, 256 * (c // 2), 256, c % 2 == 0, c % 2 == 1) for c in range(nkc)]
                pend = []
                nw = len(work)
                for ci in range(nw + 2):
                    CUR_TAG[0] = "Batt" + ("s" if latent else "p")
                    if ci < nw:
                        c, q0, qn, _, _ = work[ci]
                        ps = cnt["s3"] % 3
                        cnt["s3"] += 1
                        mm(PB[ps][:, 0:qn], H["k"][:, c * 128:(c + 1) * 128], H["q"][:, lc0 + q0:lc0 + q0 + qn], True, True,
                           [RH["k"], RH["q"]], [RB[ps]])
                        pt = cnt["pt"] % 3
                        cnt["pt"] += 1
                        actop(PT[pt][:, 0:qn], PB[ps][:, 0:qn], AF.Exp, [RB[ps]], [R_PT[pt]], scale=B_SCALE)
                        pend.append(pt)
                    if ci >= 2:
                        c, q0, qn, st_, sp_ = work[ci - 2]
                        pt2 = pend[ci - 2]
                        mm(PB[po][:, q0:q0 + qn], H["v"][:, c, :], PT[pt2][:, 0:qn], st_, sp_, [RH["v"], R_PT[pt2]], [RB[po]])
                    if side and ci % 6 == 5:
                        side.pop(0)()
                P.op("dve", lambda e, po=po: e.reciprocal(rden[0:64, 0:n], PB[po][64:128, 0:n]), reads=[RB[po]], writes=[R_rden])
                hp = 64 * (h % 2)
                ttop(oTB[hp:hp + 64, h // 2, lc0:lc0 + n], PB[po][0:64, 0:n], rden[0:64, 0:n], ALU.mult,
                     [RB[po], R_rden], [R_qaT[h // 2]])

            cnt["pb"] = 0
            for u_ in b_prep(0):
                u_()
            for h in range(8):
                side = b_prep(h + 1) if h + 1 < 8 else []
                for tt in range(ntile):
                    b_att(h, tt, side)
                while side:
                    side.pop(0)()
            CUR_TAG[0] = "Bout" + ("s" if latent else "p")
            for tt in range(ntile):
                lc0 = tt * n
                for dc in range(KC):
                    bank = 3 if dc % 2 == 0 else 7
                    for pr in range(4):
                        mm(PB[bank][:, 0:n], woutS[:, pr, dc * 128:(dc + 1) * 128], oTB[:, pr, lc0:lc0 + n], pr == 0, pr == 3,
                           [R_woutS, R_qaT[pr]], [RB[bank]])
                    xacc(dc, bank, n, tok0 + lc0, s)

        def store_tokens(dst, tok0, ntok):
            for tb in range(ntok // 128):
                i = cnt["io"] % 2
                cnt["io"] += 1
                t0 = tok0 + tb * 128
                s = t0 // 512
                for h in range(2):
                    bank = 2 * (tb % 2) + h
                    for kk in range(4):
                        k = h * 4 + kk
                        P.op("pe", lambda e, k=k, kk=kk, bank=bank, t0=t0: e.transpose(
                            PB[bank][:, kk * 128:(kk + 1) * 128], xT[:, k, t0:t0 + 128], ident[:]),
                            reads=[R_xT[k][s], R_ident], writes=[RB[bank]])
                    copy_op(alt_eng(), iost[i][:, h * 512:(h + 1) * 512], PB[bank][:, :], reads=[RB[bank]], writes=[R_io[i]])
                P.op("sp", lambda e, i=i, tb=tb: e.dma_start(out=dst[tb * 128:(tb + 1) * 128, :], in_=iost[i][:]),
                     reads=[R_io[i]], writes=[R_io[i]], dsem=S_io[i])

        nlayers = DEPTH if stage >= 10 else (2 if stage == 3 else 1)
        while LOAD_SIDE:
            LOAD_SIDE.pop(0)()
        for l in range(nlayers):
            MODS["set"] = l % 2
            if stage >= 1 and not skip_ffn:
                half_ffn(l, 0)
            if stage >= 2:
                P.barrier()
                if l % 2 == 1 and stage >= 3:
                    f_mixer(l)
                if l % 2 == 0 and stage >= 4:
                    ab_mixer(l, do_sample=(stage >= 5))
                P.barrier()
                side = ada_steps(l + 1, (l + 1) % 2, 3) if l + 1 < nlayers else None
                if not skip_ffn:
                    half_ffn(l, 1, pre_ln=1, side=side)
                else:
                    for st_idx in range(5):
                        layer_norm(l, 1, st_idx)
                    while side:
                        side.pop(0)()
        P.barrier()
        store_tokens(ys, 0, TS)
        store_tokens(yp, TS, TP)
        P.emit()
    return nc


_CACHE = {}


def _constants():
    if "const" in _CACHE:
        return _CACHE["const"]
    bf = ml_dtypes.bfloat16
    out = {}
    for name, L in (("dft_big", TS), ("dft_small", 256)):
        n = np.arange(L, dtype=np.int64)
        ang = 2.0 * np.pi * ((n[:, None] * n[None, :]) % L).astype(np.float64) / L
        m = np.stack([np.cos(ang), -np.sin(ang)], axis=1) / np.sqrt(L)
        out[name] = np.ascontiguousarray(m.astype(np.float32).astype(bf))
    n = np.arange(256, dtype=np.int64)
    ang = 2.0 * np.pi * ((n[:, None] * n[None, :]) % 256).astype(np.float64) / 256
    out["csg"] = np.ascontiguousarray((np.concatenate([np.cos(ang), np.sin(ang)], axis=1) / 16.0).astype(np.float32).astype(bf))
    t = np.arange(TS)
    pos = np.stack([t // 64, t % 64]).astype(np.float64)
    ra = np.zeros((128, 2, TS), np.float64)
    for p in range(128):
        dp = p % 64
        b_, hf, i = dp // 32, (dp % 32) // 16, dp % 16
        ang = pos[hf] * (10000.0 ** (-i / 16.0))
        ra[p, 0] = np.cos(ang)
        ra[p, 1] = np.sin(ang) * (-1.0 if b_ == 1 else 1.0)
    out["ropeA"] = ra.astype(np.float32)
    rb = np.zeros((64, 2, TS), np.float64)
    for p in range(64):
        b_, r = p // 32, p % 32
        if r < 16:
            hf, i = r // 8, r % 8
            ang = pos[hf] * (10000.0 ** (-i / 8.0))
            rb[p, 0] = np.cos(ang)
            rb[p, 1] = np.sin(ang) * (-1.0 if b_ == 1 else 1.0)
    out["ropeB"] = rb.astype(np.float32)
    kj = np.arange(128)[:, None]
    qi = np.arange(128)[None, :]
    NEG = -30000.0
    m0 = np.where(qi <= kj, 0.0, NEG)
    m1 = np.where(kj <= qi, 0.0, NEG)
    out["masks"] = np.ascontiguousarray(np.stack([np.tile(m0, (1, 4)), np.tile(m1, (1, 4))], axis=1).astype(np.float32).astype(bf))
    _CACHE["const"] = out
    return out


def _prep_inputs(inp):
    f32 = np.float32
    small = np.concatenate([
        np.asarray(inp["b_ada"], f32).reshape(DEPTH, 72, 128),
        np.asarray(inp["ln_g"], f32).reshape(DEPTH, 24, 128),
        np.asarray(inp["ln_b"], f32).reshape(DEPTH, 24, 128)], axis=1)
    shared = {
        "w_ada": np.ascontiguousarray(inp["w_ada"], f32),
        "smallp": np.ascontiguousarray(small),
        "wg": np.ascontiguousarray(inp["ffn_w_gate"], f32),
        "wu": np.ascontiguousarray(inp["ffn_w_up"], f32),
        "wd": np.ascontiguousarray(inp["ffn_w_down"], f32),
        "ident": np.eye(128, dtype=f32),
        "f_w_in": np.ascontiguousarray(inp["f_w_in"], f32),
        "f_w_out": np.ascontiguousarray(inp["f_w_out"], f32),
    }
    for k_ in ("ab_w_in", "ab_w_out", "a_sink", "b_g_cq", "b_w_uq", "b_g_ckv", "b_w_ukv"):
        shared[k_] = np.ascontiguousarray(inp[k_], f32)
    shared.update(_constants())
    maps = []
    for b in range(8):
        m = dict(shared)
        m["xs"] = np.ascontiguousarray(inp["x_sample"][b], f32)
        m["xp"] = np.ascontiguousarray(np.asarray(inp["x_prompt"][2 * b:2 * b + 2], f32).reshape(TP, D))
        m["c2"] = np.ascontiguousarray(np.stack([np.asarray(inp["c"][b], f32), np.asarray(inp["c_ctx"], f32)]))
        m["ca_k"] = np.ascontiguousarray(np.asarray(inp["cache_a_k"][b], f32).reshape(2, 256, 128))
        m["ca_v"] = np.ascontiguousarray(np.asarray(inp["cache_a_v"][b], f32).reshape(2, 256, 128))
        m["cb_ckv"] = np.ascontiguousarray(inp["cache_b_ckv"][b], f32)
        m["cb_kpe"] = np.ascontiguousarray(inp["cache_b_kpe"][b], f32)
        maps.append(m)
    return maps


def kernel(**inputs):
    stage = inputs.pop("_stage", 99)
    ncores = inputs.pop("_ncores", 8)
    if stage not in _CACHE:
        _CACHE[stage] = build_program(stage)
    nc = _CACHE[stage]
    maps = _prep_inputs(inputs)[:ncores]
    res = bu.run_bass_kernel_spmd(nc, maps, core_ids=list(range(ncores)))
    r = res.results
    y_s = np.stack([r[b]["ys"] for b in range(ncores)])
    y_p = np.concatenate([r[b]["yp"].reshape(2, 256, D) for b in range(ncores)])
    nk = np.concatenate([r[b]["nk"] for b in range(ncores)]).reshape(2 * ncores, 2, 256, 2, 64)
    nv = np.concatenate([r[b]["nv"] for b in range(ncores)]).reshape(2 * ncores, 2, 256, 2, 64)
    nckv = np.concatenate([r[b]["nckv"] for b in range(ncores)])
    nkpe = np.concatenate([r[b]["nkpe"] for b in range(ncores)])
    return y_p, y_s, nk, nv, nckv, nkpe
```

```python
import contextlib
import numpy as np
import ml_dtypes
import concourse.bass as bass
import concourse.mybir as mybir
import concourse.bass_utils as bu

F32 = mybir.dt.float32
BF16 = mybir.dt.bfloat16
AF = mybir.ActivationFunctionType
ALU = mybir.AluOpType

D = 1024
DFF = 2816
NF = 22
KC = 8
TS = 2048
TP = 512
T = TS + TP
DEPTH = 4
ALPHA = (2 * DEPTH) ** 0.25
LN_EPS = 1e-5
RMS_EPS = 1e-6
ENGS = ("sp", "act", "pool", "dve", "pe")


class Res:
    __slots__ = ("name", "last_w", "readers", "excl")

    def __init__(self, name="", excl=False):
        self.name = name
        self.last_w = None
        self.readers = []
        self.excl = excl


class DmaSem:
    __slots__ = ("h", "count", "name")

    def __init__(self, h, name):
        self.h = h
        self.count = 0
        self.name = name


class Op:
    __slots__ = ("eng", "fn", "seq", "deps", "signal", "sigidx", "dsem", "dval", "is_dma")


class Prog:
    def __init__(self, nc):
        self.nc = nc
        self.ops = {e: [] for e in ENGS}
        self.dsems = []
        self.epoch = Res("epoch")
        self.pending_stores = []
        self.last_dma = {}

    def op(self, eng, fn, reads=(), writes=(), dsem=None, extra=()):
        o = Op()
        o.eng = eng
        o.fn = fn
        o.seq = len(self.ops[eng])
        o.signal = False
        o.sigidx = 0
        o.dsem = dsem
        o.is_dma = dsem is not None
        if dsem is not None:
            dsem.count += 16
            o.dval = dsem.count
        else:
            o.dval = 0
        deps = {}
        for r in reads:
            d = r.last_w
            if d is not None:
                deps[id(d)] = d
            if r.excl:
                for d in r.readers:
                    if d.eng != eng:
                        deps[id(d)] = d
        for w in writes:
            d = w.last_w
            if d is not None:
                deps[id(d)] = d
            for d in w.readers:
                deps[id(d)] = d
        for d in extra:
            deps[id(d)] = d
        dl = []
        for d in deps.values():
            if d is o:
                continue
            if (not d.is_dma) and (not o.is_dma) and d.eng == "pe" and eng == "pe":
                continue
            if d.is_dma and o.is_dma and d.dsem is o.dsem:
                continue
            dl.append(d)
            if not d.is_dma:
                d.signal = True
        best = {}
        for d in dl:
            if d.is_dma:
                k_ = id(d.dsem)
                if k_ not in best or best[k_].dval < d.dval:
                    best[k_] = d
        dl = [d for d in dl if (not d.is_dma) or best[id(d.dsem)] is d]
        o.deps = dl
        for r in reads:
            r.readers.append(o)
        for w in writes:
            w.last_w = o
            w.readers = []
        self.ops[eng].append(o)
        if o.is_dma:
            self.last_dma[id(dsem)] = o
        return o

    def barrier(self):
        lasts = [self.ops[e][-1] for e in ("act", "dve", "pe") if self.ops[e]]
        lasts += list(self.last_dma.values())
        self.op("dve", lambda e: e.nop(), writes=[self.epoch], extra=lasts)
        self.op("act", lambda e: e.nop(), reads=[self.epoch])
        self.op("pe", lambda e: e.nop(), reads=[self.epoch])

    def emit(self):
        nc = self.nc
        for e in ENGS:
            k = 0
            for o in self.ops[e]:
                if o.signal and not o.is_dma:
                    k += 1
                    o.sigidx = k
        with contextlib.ExitStack() as st:
            esem = {e: st.enter_context(nc.semaphore("s_" + e)) for e in ENGS}
            block = st.enter_context(nc.Block())

            def make(e):
                def body(engh):
                    known = {}
                    for o in self.ops[e]:
                        for d in o.deps:
                            if d.is_dma:
                                key = id(d.dsem)
                                val = d.dval
                                sem = d.dsem.h
                            else:
                                key = d.eng
                                val = d.sigidx
                                sem = esem[d.eng]
                            if known.get(key, 0) >= val:
                                continue
                            engh.wait_ge(sem, val)
                            known[key] = val
                        ins = o.fn(engh)
                        if o.is_dma:
                            ins.then_inc(o.dsem.h, 16)
                        elif o.signal:
                            ins.then_inc(esem[e], 1)
                    if e == "sp":
                        for ds in self.dsems:
                            if ds.count > 0:
                                engh.wait_ge(ds.h, ds.count)
                return body

            block.sync(make("sp"))
            block.scalar(make("act"))
            block.gpsimd(make("pool"))
            block.vector(make("dve"))
            block.tensor(make("pe"))


AB_STOP = [99]
MM_TAGS = []
CUR_TAG = ["-"]
DBG = set()


def build_program(stage=99, skip_ffn=False):
    nc = bass.Bass("TRN2", target_bir_lowering=False)

    def din(name, shape, dt=F32):
        return nc.dram_tensor(name, list(shape), dt, kind="ExternalInput").ap()

    def dout(name, shape):
        return nc.dram_tensor(name, list(shape), F32, kind="ExternalOutput").ap()

    xs = din("xs", [TS, D])
    xp = din("xp", [TP, D])
    c2 = din("c2", [2, D])
    w_ada = din("w_ada", [DEPTH, D, 9 * D])
    smallp = din("smallp", [DEPTH, 120, 128])
    wg = din("wg", [DEPTH, 2, D, DFF])
    wu = din("wu", [DEPTH, 2, D, DFF])
    wd = din("wd", [DEPTH, 2, DFF, D])
    ident_d = din("ident", [128, 128])
    f_w_in = din("f_w_in", [2, D, D])
    f_w_out = din("f_w_out", [2, D, D])
    dft_big = din("dft_big", [TS, 2, TS], BF16)
    dft_small = din("dft_small", [256, 2, 256], BF16)
    csg_d = din("csg", [256, 512], BF16)
    ab_w_in = din("ab_w_in", [2, D, 1440])
    ab_w_out = din("ab_w_out", [2, D, D])
    a_sink = din("a_sink", [2, 8])
    b_g_cq = din("b_g_cq", [2, 384])
    b_w_uq = din("b_w_uq", [2, 384, 768])
    b_g_ckv = din("b_g_ckv", [2, 256])
    b_w_ukv = din("b_w_ukv", [2, 256, 1024])
    ca_k = din("ca_k", [2, 256, 128])
    ca_v = din("ca_v", [2, 256, 128])
    cb_ckv = din("cb_ckv", [2, 256, 256])
    cb_kpe = din("cb_kpe", [2, 256, 32])
    ropeA_d = din("ropeA", [128, 2, TS])
    ropeB_d = din("ropeB", [64, 2, TS])
    masks_d = din("masks", [128, 2, 512], BF16)
    nk_o = dout("nk", [2, 2, 256, 128])
    nv_o = dout("nv", [2, 2, 256, 128])
    nckv_o = dout("nckv", [2, 2, 256, 256])
    nkpe_o = dout("nkpe", [2, 2, 256, 32])
    ys = dout("ys", [TS, D])
    yp = dout("yp", [TP, D])

    st = contextlib.ExitStack()
    with st:
        P = Prog(nc)

        def sb(name, shape, dt):
            return st.enter_context(nc.sbuf_tensor(name, list(shape), dt))

        def dsem(name):
            s = DmaSem(st.enter_context(nc.semaphore("d_" + name)), name)
            P.dsems.append(s)
            return s

        xT = sb("xT", [128, KC, T], F32)
        ident = sb("ident_sb", [128, 128], F32)
        onesM = sb("onesM", [128, 128], BF16)
        spT = sb("spT", [128, DEPTH, 120], F32)
        scT = sb("scT", [128, 16], BF16)
        mods_sets = [sb("mods%d" % i, [128, 2, 72], F32) for i in range(2)]
        MODS = {"set": 0}
        csG = sb("csG", [128, 2, 512], BF16)
        identB = sb("identB", [128, 128], BF16)
        masks = sb("masks_sb", [128, 2, 512], BF16)
        esink = sb("esink", [128, 8], F32)
        gvec = sb("gvec", [128, 5], F32)
        PB = [st.enter_context(nc.psum_tensor("pb%d" % i, [128, 512], F32)) for i in range(8)]
        RB = [Res("pb%d" % i, excl=True) for i in range(8)]
        ARENA_BYTES = 121856
        arena = sb("arena", [128, ARENA_BYTES // 4], F32)

        def carve(off, shape, dt):
            esz = 4 if dt == F32 else 2
            n = 1
            for d_ in shape[1:]:
                n *= d_
            assert off % 4 == 0 and (n * esz) % 4 == 0 and off + n * esz <= ARENA_BYTES, (off, shape)
            a = arena[:, off // 4:(off + n * esz) // 4]
            if dt != F32:
                a = a.bitcast(dt)
            if len(shape) == 3:
                a = a.rearrange("p (a b) -> p a b", b=shape[2])
            elif len(shape) == 4:
                a = a.rearrange("p (a b c) -> p a b c", b=shape[2], c=shape[3])
            if shape[0] < 128:
                a = a[0:shape[0]]
            return a

        O_WGU = 0
        O_WDN = O_WGU + 16384
        O_ZB = O_WDN + 11264
        O_ZQ = O_ZB + 8192
        O_ST = O_ZQ + 8192
        O_LT = O_ST + 8192
        O_WORK = O_LT + 4096
        O_HT = O_WORK
        O_AT = O_HT + 16384
        O_SG = O_AT + 45056
        assert O_SG + 4096 <= ARENA_BYTES
        hT = carve(O_HT, [128, KC, 1024], BF16)
        aT = carve(O_AT, [128, NF, 1024], BF16)
        wgu = [carve(O_WGU + 4096 * i, [128, 2, KC, 128], BF16) for i in range(4)]
        wad = [carve(O_WGU + 8192 * i, [128, KC, 512], BF16) for i in range(2)]
        wdn = [carve(O_WDN + 2816 * i, [128, 11, 128], BF16) for i in range(4)]
        sg = [carve(O_SG + 2048 * i, [128, 512], F32) for i in range(2)]
        zb = carve(O_ZB, [128, KC, 512], BF16)
        zq = carve(O_ZQ, [128, KC, 512], BF16)
        st_mean = carve(O_ST, [128, 512], F32)
        st_a = carve(O_ST + 2048, [128, 512], F32)
        st_rstd = carve(O_ST + 4096, [128, 512], F32)
        st_mr = carve(O_ST + 6144, [128, 512], F32)
        lt = [carve(O_LT + 2048 * i, [128, 512], F32) for i in range(2)]
        iost = [carve(O_ZB, [128, D], F32), carve(O_ZQ, [128, D], F32)]
        sp_in = carve(O_HT, [120, DEPTH, 128], F32)
        c_in = carve(O_HT + 4096, [16, 128], F32)

        R_xT = [[Res("xT%d_%d" % (k, s)) for s in range(5)] for k in range(KC)]
        R_ident, R_ones, R_spT, R_scT = Res("ident"), Res("ones"), Res("spT"), Res("scT")
        R_mods_sets = [Res("mods0"), Res("mods1")]

        def cur_rm():
            return R_mods_sets[MODS["set"]]
        R_hT = [Res("hT%d" % s) for s in range(2)]
        R_aT = [[Res("aT%d_%d" % (f, s)) for s in range(2)] for f in range(NF)]
        R_wgu = [Res("wgu%d" % i) for i in range(4)]
        R_wdn = [Res("wdn%d" % i) for i in range(4)]
        R_wad = [Res("wad0"), Res("wad1")]
        R_sg = [Res("sg0"), Res("sg1")]
        R_zb, R_zq = Res("zb"), Res("zq")
        R_mean, R_sta, R_rstd, R_mr = Res("mean"), Res("sta"), Res("rstd"), Res("mr")
        R_lt = [Res("lt0"), Res("lt1")]
        R_io = [Res("io0"), Res("io1")]
        R_spin, R_cin = Res("spin"), Res("cin")
        S_wgu = [dsem("wgu%d" % i) for i in range(4)]
        S_wdn = [dsem("wdn%d" % i) for i in range(4)]
        S_wad = [dsem("wad0"), dsem("wad1")]
        S_io = [dsem("io0"), dsem("io1")]
        S_misc = dsem("misc")

        cnt = {"wgu": 0, "wdn": 0, "wad": 0, "io": 0, "sg": 0, "lt": 0, "eng": 0, "p2": 0}

        def alt_eng():
            cnt["eng"] += 1
            return "act" if cnt["eng"] % 2 else "dve"

        def copy_op(eng, out, in_, reads, writes):
            if eng == "act":
                return P.op("act", lambda e: e.copy(out, in_), reads=reads, writes=writes)
            return P.op("dve", lambda e: e.tensor_copy(out, in_), reads=reads, writes=writes)

        P.op("sp", lambda e: e.dma_start(out=ident[:], in_=ident_d), writes=[R_ident], dsem=dsem("ident"))
        P.op("dve", lambda e: e.memset(onesM[:], 1.0 / 1024.0), writes=[R_ones])
        P.op("sp", lambda e: e.dma_start(out=sp_in, in_=smallp.rearrange("l r p -> r l p")),
             writes=[R_spin], dsem=dsem("spin"))
        S_cin = dsem("cin")
        for s_ in range(2):
            P.op("sp", lambda e, s_=s_: e.dma_start(out=c_in[s_ * 8:(s_ + 1) * 8, :], in_=c2[s_].rearrange("(k p) -> k p", p=128)),
                 writes=[R_cin], dsem=S_cin)
        for l in range(DEPTH):
            P.op("pe", lambda e, l=l: e.transpose(PB[0][:, l * 120:(l + 1) * 120], sp_in[:, l, :], ident[0:120, 0:120]),
                 reads=[R_spin, R_ident], writes=[RB[0]])
        P.op("dve", lambda e: e.tensor_copy(spT[:].rearrange("p l r -> p (l r)"), PB[0][:, 0:480]),
             reads=[RB[0]], writes=[R_spT])
        P.op("act", lambda e: e.activation(c_in, c_in, AF.Silu), reads=[R_cin], writes=[R_cin])
        P.op("pe", lambda e: e.transpose(PB[1][:, 0:16], c_in, ident[0:16, 0:16]), reads=[R_cin, R_ident], writes=[RB[1]])
        P.op("dve", lambda e: e.tensor_copy(scT[:], PB[1][:, 0:16]), reads=[RB[1]], writes=[R_scT])

        def ada_steps(l, mset, bank):
            md, rm = mods_sets[mset], R_mods_sets[mset]
            steps = []

            def fill(cb):
                i = cnt["wad"] % 2
                cnt["wad"] += 1
                P.op("pool", lambda e: e.dma_start(
                    out=wad[i][:], in_=w_ada[l].rearrange("(k p) c -> p k c", p=128)[:, :, cb * 512:(cb + 1) * 512]),
                    reads=[P.epoch], writes=[R_wad[i], R_wgu[2 * i], R_wgu[2 * i + 1]], dsem=S_wad[i])
                for m in range(4):
                    for k in range(KC):
                        P.op("pe", lambda e, m=m, k=k: e.matmul(
                            PB[bank][:, m * 2:m * 2 + 2], wad[i][:, k, m * 128:(m + 1) * 128],
                            scT[:].rearrange("p (s k) -> p k s", k=8)[:, k, :], start=(k == 0), stop=(k == KC - 1)),
                            reads=[R_wad[i], R_wgu[2 * i], R_wgu[2 * i + 1], R_scT], writes=[RB[bank]])
                for s_ in range(2):
                    P.op("dve", lambda e, s_=s_: e.tensor_tensor(
                        md[:, s_, cb * 4:cb * 4 + 4], PB[bank][:, 0:8].rearrange("p (j s) -> p j s", s=2)[:, :, s_],
                        spT[:, l, cb * 4:cb * 4 + 4], ALU.add),
                        reads=[RB[bank], R_spT], writes=[rm])

            def fin():
                for j in (1, 4, 7):
                    P.op("dve", lambda e, j=j: e.tensor_scalar(md[:, :, j * 8:(j + 1) * 8], md[:, :, j * 8:(j + 1) * 8],
                                                               1.0, None, ALU.add), reads=[rm], writes=[rm])
                for j, f in ((2, 0.5 / ALPHA), (5, 1.0 / ALPHA), (8, 0.5 / ALPHA)):
                    P.op("dve", lambda e, j=j, f=f: e.tensor_scalar(md[:, :, j * 8:(j + 1) * 8], md[:, :, j * 8:(j + 1) * 8],
                                                                    f, None, ALU.mult), reads=[rm], writes=[rm])

            for cb in range(18):
                steps.append(lambda cb=cb: fill(cb))
            steps.append(fin)
            return steps

        def load_tokens(src, tok0, ntok, side=None):
            for tb in range(ntok // 128):
                if side:
                    side.pop(0)()
                i = cnt["io"] % 2
                cnt["io"] += 1
                t0 = tok0 + tb * 128
                P.op("sp", lambda e, i=i, tb=tb: e.dma_start(out=iost[i][:], in_=src[tb * 128:(tb + 1) * 128, :]),
                     writes=[R_io[i]], dsem=S_io[i])
                for h in range(2):
                    bank = 2 + 2 * (tb % 2) + h
                    for kk in range(4):
                        k = h * 4 + kk
                        P.op("pe", lambda e, i=i, k=k, kk=kk, bank=bank: e.transpose(
                            PB[bank][:, kk * 128:(kk + 1) * 128], iost[i][:, k * 128:(k + 1) * 128], ident[:]),
                            reads=[R_io[i], R_ident], writes=[RB[bank]])
                    s = t0 // 512
                    copy_op(alt_eng(), xT[:, h * 4:(h + 1) * 4, t0:t0 + 128],
                            PB[bank][:].rearrange("p (k t) -> p k t", t=128),
                            reads=[RB[bank]], writes=[R_xT[h * 4 + kk][s] for kk in range(4)])

        P.barrier()
        LOAD_SIDE = ada_steps(0, 0, 6)
        load_tokens(xs, 0, TS, LOAD_SIDE)
        load_tokens(xp, TS, TP, LOAD_SIDE)
        P.barrier()

        def mod_ap(s, j, k):
            return mods_sets[MODS["set"]][:, s, j * 8 + k:j * 8 + k + 1]

        def lng(l, i, k):
            return spT[:, l, 72 + i * 8 + k:72 + i * 8 + k + 1]

        def lnb(l, i, k):
            return spT[:, l, 96 + i * 8 + k:96 + i * 8 + k + 1]

        def layer_norm(l, i, st_idx, n=512, c0=0):
            ln_prep(l, i, st_idx, n, c0)
            ln_rest(l, i, st_idx, n, c0)

        def ln_prep(l, i, st_idx, n=512, c0=0):
            t0 = st_idx * 512 + c0
            rx = [R_xT[k][st_idx] for k in range(KC)]
            for k in range(KC):
                P.op("act", lambda e, k=k: e.copy(zb[:, k, 0:n], xT[:, k, t0:t0 + n]), reads=[rx[k]], writes=[R_zb])
                P.op("act", lambda e, k=k: e.activation(zq[:, k, 0:n], xT[:, k, t0:t0 + n], AF.Square),
                     reads=[rx[k]], writes=[R_zq])

        def ln_rest(l, i, st_idx, n=512, c0=0):
            t0 = st_idx * 512 + c0
            rx = [R_xT[k][st_idx] for k in range(KC)]
            for k in range(KC):
                P.op("pe", lambda e, k=k: e.matmul(PB[6][:, 0:n], onesM[:], zb[:, k, 0:n], start=(k == 0), stop=(k == KC - 1)),
                     reads=[R_zb, R_ones], writes=[RB[6]])
            for k in range(KC):
                P.op("pe", lambda e, k=k: e.matmul(PB[7][:, 0:n], onesM[:], zq[:, k, 0:n], start=(k == 0), stop=(k == KC - 1)),
                     reads=[R_zq, R_ones], writes=[RB[7]])
            P.op("act", lambda e: e.copy(st_mean[:, 0:n], PB[6][:, 0:n]), reads=[RB[6]], writes=[R_mean])
            P.op("dve", lambda e: e.tensor_tensor(st_a[:, 0:n], st_mean[:, 0:n], st_mean[:, 0:n], ALU.mult),
                 reads=[R_mean], writes=[R_sta])
            P.op("dve", lambda e: e.tensor_tensor(st_a[:, 0:n], PB[7][:, 0:n], st_a[:, 0:n], ALU.subtract),
                 reads=[RB[7], R_sta], writes=[R_sta])
            P.op("act", lambda e: e.activation(st_a[:, 0:n], st_a[:, 0:n], AF.Ln, bias=LN_EPS / (ALPHA * ALPHA), scale=1.0),
                 reads=[R_sta], writes=[R_sta])
            P.op("act", lambda e: e.activation(st_rstd[:, 0:n], st_a[:, 0:n], AF.Exp, scale=-0.5), reads=[R_sta], writes=[R_rstd])
            P.op("dve", lambda e: e.tensor_tensor(st_mr[:, 0:n], st_mean[:, 0:n], st_rstd[:, 0:n], ALU.mult),
                 reads=[R_mean, R_rstd], writes=[R_mr])
            for k in range(KC):
                j = cnt["lt"] % 2
                cnt["lt"] += 1
                P.op("dve", lambda e, k=k, j=j: e.tensor_tensor(lt[j][:, 0:n], xT[:, k, t0:t0 + n], st_rstd[:, 0:n], ALU.mult),
                     reads=[rx[k], R_rstd], writes=[R_lt[j]])
                P.op("dve", lambda e, j=j: e.tensor_tensor(lt[j][:, 0:n], lt[j][:, 0:n], st_mr[:, 0:n], ALU.subtract),
                     reads=[R_lt[j], R_mr], writes=[R_lt[j]])
                P.op("act", lambda e, k=k, j=j: e.activation(xT[:, k, t0:t0 + n], lt[j][:, 0:n], AF.Identity,
                                                             bias=lnb(l, i, k), scale=lng(l, i, k)),
                     reads=[R_lt[j], R_spT], writes=[rx[k]])

        def modulate(dst, dst_res, st_idx, ncol, s, jsh, jsc, dcol0=0, c0=0):
            t0 = st_idx * 512 + c0
            for k in range(KC):
                P.op("act", lambda e, k=k, b_=mod_ap(s, jsh, k), s_=mod_ap(s, jsc, k): e.activation(
                    dst[:, k, dcol0:dcol0 + ncol], xT[:, k, t0:t0 + ncol], AF.Identity, bias=b_, scale=s_),
                     reads=[R_xT[k][st_idx], cur_rm()], writes=[dst_res])

        def half_ffn(l, i, pre_ln=None, side=None):
            jb = 0 if i == 0 else 6
            lni = 0 if i == 0 else 2
            bts = ((0, 1), (2, 3), (4,))
            if pre_ln is not None:
                for st_idx in bts[0]:
                    layer_norm(l, pre_ln, st_idx)
            prev = None
            for bt, sts in enumerate(bts):
                nst = len(sts)
                if bt == 0:
                    for si, st_idx in enumerate(sts):
                        s = 1 if st_idx == 4 else 0
                        modulate(hT, R_hT[si], st_idx, 512, s, jb + 0, jb + 1, dcol0=si * 512)
                for f in range(NF):
                    wi = cnt["wgu"] % 4
                    cnt["wgu"] += 1
                    for ww, src in ((0, wg), (1, wu)):
                        P.op("pool", lambda e, wi=wi, ww=ww, src=src, f=f: e.dma_start(
                            out=wgu[wi][:, ww, :, :],
                            in_=src[l, i].rearrange("(k p) f -> p k f", p=128)[:, :, f * 128:(f + 1) * 128]),
                            reads=[P.epoch], writes=[R_wgu[wi]], dsem=S_wgu[wi])
                    for si, st_idx in enumerate(sts):
                        bsel = cnt["sg"] % 2
                        cnt["sg"] += 1
                        bg, bu_ = bsel, 2 + bsel
                        for ww, bank in ((0, bg), (1, bu_)):
                            for k in range(KC):
                                P.op("pe", lambda e, wi=wi, ww=ww, k=k, si=si, bank=bank: e.matmul(
                                    PB[bank][:, :], wgu[wi][:, ww, k, :],
                                    hT[:, k, si * 512:(si + 1) * 512], start=(k == 0), stop=(k == KC - 1)),
                                    reads=[R_wgu[wi], R_hT[si]], writes=[RB[bank]])
                        P.op("act", lambda e, bsel=bsel, bg=bg: e.activation(sg[bsel][:], PB[bg][:, :], AF.Silu),
                             reads=[RB[bg]], writes=[R_sg[bsel]])
                        P.op("dve", lambda e, bsel=bsel, bu_=bu_, f=f, si=si: e.tensor_tensor(
                            aT[:, f, si * 512:(si + 1) * 512], sg[bsel][:], PB[bu_][:, :], ALU.mult),
                            reads=[R_sg[bsel], RB[bu_]], writes=[R_aT[f][si]])
                jobs = []
                nxt_done = False
                if pre_ln is not None and bt + 1 < len(bts):
                    jobs += [(pre_ln, st_idx) for st_idx in bts[bt + 1]]
                if prev is not None:
                    jobs += [(lni, st_idx) for st_idx in prev]
                for dc in range(KC):
                    wis = []
                    for hf in range(2):
                        wi = cnt["wdn"] % 4
                        cnt["wdn"] += 1
                        wis.append(wi)
                        P.op("pool", lambda e, wi=wi, dc=dc, hf=hf: e.dma_start(
                            out=wdn[wi][:],
                            in_=wd[l, i].rearrange("(f p) d -> p f d", p=128)[:, hf * 11:(hf + 1) * 11, dc * 128:(dc + 1) * 128]),
                            reads=[P.epoch], writes=[R_wdn[wi]], dsem=S_wdn[wi])
                    for si, st_idx in enumerate(sts):
                        s = 1 if st_idx == 4 else 0
                        bank = (4, 5, 0, 1, 2)[cnt["p2"] % 5]
                        cnt["p2"] += 1
                        for f in range(NF):
                            P.op("pe", lambda e, wi=wis[f // 11], f=f, si=si, bank=bank: e.matmul(
                                PB[bank][:, :], wdn[wi][:, f % 11, :], aT[:, f, si * 512:(si + 1) * 512],
                                start=(f == 0), stop=(f == NF - 1)),
                                reads=[R_wdn[wis[f // 11]], R_aT[f][si]], writes=[RB[bank]])
                        t0 = st_idx * 512
                        P.op("dve", lambda e, dc=dc, bank=bank, t0=t0, g_=mod_ap(s, jb + 2, dc): e.scalar_tensor_tensor(
                            xT[:, dc, t0:t0 + 512], PB[bank][:, :], g_, xT[:, dc, t0:t0 + 512],
                            ALU.mult, ALU.add),
                            reads=[RB[bank], cur_rm(), R_xT[dc][st_idx]], writes=[R_xT[dc][st_idx]])
                    if jobs and dc % 2 == 0:
                        ln_prep(l, jobs[0][0], jobs[0][1])
                    if jobs and dc % 2 == 1:
                        lj, sj = jobs.pop(0)
                        ln_rest(l, lj, sj)
                    if side:
                        side.pop(0)()
                    if dc == 5 and bt + 1 < len(bts) and not jobs:
                        for si2, st2 in enumerate(bts[bt + 1]):
                            modulate(hT, R_hT[si2], st2, 512, 1 if st2 == 4 else 0, jb + 0, jb + 1, dcol0=si2 * 512)
                        nxt_done = True
                for lj, sj in jobs:
                    layer_norm(l, lj, sj)
                jobs = []
                if bt + 1 < len(bts) and not nxt_done:
                    for si2, st2 in enumerate(bts[bt + 1]):
                        modulate(hT, R_hT[si2], st2, 512, 1 if st2 == 4 else 0, jb + 0, jb + 1, dcol0=si2 * 512)
                prev = sts
            for st_idx in prev:
                layer_norm(l, lni, st_idx)
            while side:
                side.pop(0)()

        uT_all = carve(O_WORK, [128, KC, 2048], BF16)
        ucs_g = carve(O_WORK + 32768, [128, 16, 512], BF16)
        f_win = carve(O_WORK + 32768, [128, KC, 1024], BF16)
        dfts = [carve(O_WGU, [128, 16, 2, 256], BF16), carve(O_WORK + 49152, [128, 16, 2, 256], BF16)]
        f_wout = carve(O_ZB, [128, KC, 1024], BF16)
        f_hT = carve(O_ST, [128, KC, 512], BF16)
        f_fT = [carve(O_LT + 1024 * i, [128, 2, 256], BF16) for i in range(2)]
        R_uT = [Res("uT%d" % c) for c in range(KC)]
        R_ucs, R_fwin, R_fwout, R_fhT, R_csG = Res("ucs"), Res("fwin"), Res("fwout"), Res("fhT"), Res("csG")
        R_dft1 = Res("dft1")
        R_fT = [Res("fT0"), Res("fT1")]
        S_dft = [dsem("dft0"), dsem("dft1")]
        S_fwin, S_fwout = dsem("fwin"), dsem("fwout")
        P.op("sp", lambda e: e.dma_start(out=csG[:], in_=csg_d.rearrange("(c p) k -> p c k", p=128)),
             writes=[R_csG], dsem=dsem("csg"))
        cnt["dft"] = 0
        cnt["fb"] = 0

        def f_mixer(l):
            fi = l // 2
            P.op("pool", lambda e: e.dma_start(out=f_wout, in_=f_w_out[fi].rearrange("(k p) c -> p k c", p=128)),
                 reads=[P.epoch], writes=[R_fwout], dsem=S_fwout)
            for (tok0, L, s) in ((0, TS, 0), (TS, 256, 1), (TS + 256, 256, 1)):
                nb = L // 128
                dsrc = dft_big if L == TS else dft_small
                P.op("pool", lambda e: e.dma_start(out=f_win, in_=f_w_in[fi].rearrange("(k p) c -> p k c", p=128)),
                     reads=[P.epoch], writes=[R_fwin, R_ucs], dsem=S_fwin)
                n = min(512, L)
                for tt in range(L // n):
                    t0 = tok0 + tt * n
                    st_idx = t0 // 512
                    modulate(f_hT, R_fhT, st_idx, n, s, 3, 4, c0=t0 - st_idx * 512)
                    for c in range(KC):
                        bank = cnt["fb"] % 2
                        cnt["fb"] += 1
                        for k in range(KC):
                            P.op("pe", lambda e, c=c, k=k, bank=bank, n=n: e.matmul(
                                PB[bank][:, 0:n], f_win[:, k, c * 128:(c + 1) * 128], f_hT[:, k, 0:n],
                                start=(k == 0), stop=(k == KC - 1)),
                                reads=[R_fwin, R_fhT], writes=[RB[bank]])
                        copy_op(alt_eng(), uT_all[:, c, tt * n:(tt + 1) * n], PB[bank][:, 0:n], reads=[RB[bank]], writes=[R_uT[c]])
                for g in range(4):
                    for tb in range(nb):
                        bank = 2 + tb % 2
                        for cc in range(2):
                            P.op("pe", lambda e, tb=tb, cc=cc, bank=bank, g=g: e.matmul(
                                PB[bank][:, :], uT_all[:, 2 * g + cc, tb * 128:(tb + 1) * 128], csG[:, cc, :],
                                start=(cc == 0), stop=(cc == 1)),
                                reads=[R_uT[2 * g + cc], R_csG], writes=[RB[bank]])
                        copy_op(alt_eng(), ucs_g[:, tb, :], PB[bank][:, :], reads=[RB[bank]], writes=[R_ucs, R_fwin])
                    for kt in range(L // 256):
                        di = cnt["dft"] % 2
                        cnt["dft"] += 1
                        rd = list(R_wgu) if di == 0 else [R_dft1]
                        for sgn in range(2):
                            P.op("sp", lambda e, di=di, kt=kt, nb=nb, dsrc=dsrc, sgn=sgn: e.dma_start(
                                out=dfts[di][:, 0:nb, sgn, :],
                                in_=dsrc.rearrange("(i p) s k -> p i s k", p=128)[:, :, sgn, kt * 256:(kt + 1) * 256]),
                                reads=[P.epoch], writes=rd, dsem=S_dft[di])
                        fb = kt % 2
                        for cc in range(2):
                            bank = 4 + cc
                            for i in range(nb):
                                for sgn in range(2):
                                    P.op("pe", lambda e, di=di, cc=cc, i=i, sgn=sgn, bank=bank, nb=nb: e.matmul(
                                        PB[bank][:, 0:256], ucs_g[:, i, sgn * 256 + cc * 128:sgn * 256 + (cc + 1) * 128],
                                        dfts[di][:, i, sgn, :], start=(i == 0 and sgn == 0), stop=(i == nb - 1 and sgn == 1)),
                                        reads=[R_ucs] + rd, writes=[RB[bank]])
                            copy_op(alt_eng(), f_fT[fb][:, cc, :], PB[bank][:, 0:256], reads=[RB[bank]], writes=[R_fT[fb]])
                        ta = tok0 + kt * 256
                        st_idx = ta // 512
                        for dc in range(KC):
                            bank = 6 + dc % 2
                            for cc in range(2):
                                P.op("pe", lambda e, fb=fb, cc=cc, dc=dc, bank=bank, g=g: e.matmul(
                                    PB[bank][:, 0:256], f_wout[:, 2 * g + cc, dc * 128:(dc + 1) * 128], f_fT[fb][:, cc, :],
                                    start=(cc == 0), stop=(cc == 1)),
                                    reads=[R_fwout, R_fT[fb]], writes=[RB[bank]])
                            P.op("dve", lambda e, dc=dc, bank=bank, ta=ta, g_=mod_ap(s, 5, dc): e.scalar_tensor_tensor(
                                xT[:, dc, ta:ta + 256], PB[bank][:, 0:256], g_, xT[:, dc, ta:ta + 256],
                                ALU.mult, ALU.add),
                                reads=[RB[bank], cur_rm(), R_xT[dc][st_idx]], writes=[R_xT[dc][st_idx]])

        A_SCALE = 0.125
        B_SCALE = 96.0 ** -0.5
        W0 = O_WORK
        qaT = carve(W0, [128, 4, 2048], BF16)
        oTB = carve(W0, [128, 4, 2048], BF16)
        kaT = carve(W0 + 16384, [128, 2304], BF16)
        va = carve(W0 + 20992, [128, 18, 2, 128], BF16)
        wuq_h = [carve(W0 + 16384 + 768 * i, [128, 3, 128], BF16) for i in range(2)]
        wukv_h = [carve(W0 + 18432 + 768 * i, [128, 2, 192], BF16) for i in range(2)]
        wuq_nat = carve(W0 + 20480, [128, 3, 96], BF16)
        cqnT = carve(W0 + 30208, [128, 3, 2048], BF16)
        ckvnT = carve(W0 + 42496, [128, 2, 2304], BF16)
        kpeT = carve(W0 + 51712, [128, 2304], BF16)
        PT = [carve(W0 + 56320 + 1024 * i, [128, 512], BF16) for i in range(3)]
        zq_ab = carve(W0 + 56320, [128, 3, 512], BF16)
        oblk = [carve(W0 + 59392 + 1024 * i, [128, 4, 128], BF16) for i in range(2)]
        stgk = carve(W0 + 59392, [128, 2, 64], F32)
        gckv_bc = carve(W0 + 59392 + 512, [128, 256], F32)
        rden = carve(W0 + 61440, [128, 512], F32)
        tmpden = carve(W0 + 63488, [128, 512], F32)
        stg = [tmpden, rden]
        wsl = [carve(O_WGU + 2048 * i, [128, KC, 128], BF16) for i in range(3)]
        wperm = [carve(O_WGU + 6144 + 2048 * i, [128, KC, 128], BF16) for i in range(2)]
        ropeBk = carve(O_WGU + 10240, [128, 2, 2048], F32)
        hb = [dict(q=carve(O_WGU + 13312 * i, [128, 2048], BF16),
                   k=carve(O_WGU + 13312 * i + 4096, [128, 2304], BF16),
                   v=carve(O_WGU + 13312 * i + 8704, [128, 18, 128], BF16)) for i in range(2)]
        ropeT = carve(O_ZB, [128, 2, 2048], F32)
        a_hT = carve(O_ST, [128, KC, 512], BF16)
        ta_ = carve(O_ST + 8192, [128, 512], F32)
        tb_ = carve(O_ST + 10240, [128, 512], F32)
        woutS = carve(O_ST, [128, 4, 1024], BF16)

        R_qaT = [Res("qaT%d" % j) for j in range(4)]
        R_kaT, R_va, R_cqn, R_ckvn, R_kpeT = Res("kaT"), Res("va"), Res("cqn"), Res("ckvn"), Res("kpeT")
        R_PT = [Res("PT%d" % i) for i in range(3)]
        R_oblk = [Res("oblk0"), Res("oblk1")]
        R_rden, R_tmpden, R_stgk, R_gbc, R_zqab = Res("rden"), Res("tmpden"), Res("stgk"), Res("gbc"), Res("zqab")
        R_esk = Res("esink_t")
        R_stg = [R_tmpden, R_rden]
        R_wsl = [Res("wsl%d" % i) for i in range(3)]
        R_wperm = [Res("wperm0"), Res("wperm1")]
        R_ropeBk, R_ropeT = Res("ropeBk"), Res("ropeT")
        R_hb = [dict(q=Res("hbq%d" % i), k=Res("hbk%d" % i), v=Res("hbv%d" % i)) for i in range(2)]
        R_ahT, R_ta, R_tb = Res("ahT"), Res("ta"), Res("tb")
        R_woutS = Res("woutS")
        R_wuqh = [Res("wuqh0"), Res("wuqh1")]
        R_wukvh = [Res("wukvh0"), Res("wukvh1")]
        R_wuqn = Res("wuqn")
        R_identB, R_masks, R_esink, R_gvec = Res("identB"), Res("masks"), Res("esink"), Res("gvec")
        S_wsl = [dsem("wsl%d" % i) for i in range(3)]
        S_rope, S_ropeBk, S_woutS, S_wuq = dsem("rope"), dsem("ropeBk"), dsem("woutS"), dsem("wuq")
        S_wukv = [dsem("wukv0"), dsem("wukv1")]
        S_stg = [dsem("stg0"), dsem("stg1")]
        S_stgk, S_esink, S_gvec, S_gbc = dsem("stgk"), dsem("esink"), dsem("gvec"), dsem("gbc")
        cnt.update(wsl=0, pt=0, po=0, ob=0, ps=0, wp=0, s3=0)

        def mm(out, lhsT, rhs, start, stop, reads, writes):
            MM_TAGS.append(CUR_TAG[0])
            P.op("pe", lambda e: e.matmul(out, lhsT, rhs, start=start, stop=stop), reads=reads, writes=writes)

        def tpose(out, in_, idn, reads, writes):
            P.op("pe", lambda e: e.transpose(out, in_, idn), reads=reads, writes=writes)

        def ttop(out, in0, in1, op, reads, writes):
            P.op("dve", lambda e: e.tensor_tensor(out, in0, in1, op), reads=reads, writes=writes)

        def actop(out, in_, func, reads, writes, bias=None, scale=None):
            kw = {}
            if bias is not None:
                kw["bias"] = bias
            if scale is not None:
                kw["scale"] = scale
            P.op("act", lambda e: e.activation(out, in_, func, **kw), reads=reads, writes=writes)

        def dma(eng, out, in_, reads, writes, ds):
            P.op(eng, lambda e: e.dma_start(out=out, in_=in_), reads=reads, writes=writes, dsem=ds)

        def memset(ap, val, writes):
            P.op("dve", lambda e: e.memset(ap, val), writes=writes)

        def xacc(dc, bank, ncol, tq, s):
            st_idx = tq // 512
            g_ = mod_ap(s, 5, dc)
            P.op("dve", lambda e: e.scalar_tensor_tensor(
                xT[:, dc, tq:tq + ncol], PB[bank][:, 0:ncol], g_, xT[:, dc, tq:tq + ncol], ALU.mult, ALU.add),
                reads=[RB[bank], cur_rm(), R_xT[dc][st_idx]], writes=[R_xT[dc][st_idx]])

        P.op("dve", lambda e: e.tensor_copy(identB[:], ident[:]), reads=[R_ident], writes=[R_identB])
        dma("sp", masks[:], masks_d, [], [R_masks], dsem("masks"))

        def load_wgroup(i_l, col0, ncols, perm=None):
            si = cnt["wsl"] % 3
            cnt["wsl"] += 1
            src = ab_w_in[i_l].rearrange("(k p) c -> p k c", p=128)
            if isinstance(col0, tuple):
                for gi, c0 in enumerate(col0):
                    dma("pool", wsl[si][:, :, gi * 64:(gi + 1) * 64], src[:, :, c0:c0 + 64], [P.epoch], [R_wsl[si]], S_wsl[si])
            else:
                dma("pool", wsl[si][:, :, 0:ncols], src[:, :, col0:col0 + ncols], [P.epoch], [R_wsl[si]], S_wsl[si])
            if perm is None:
                return wsl[si], R_wsl[si], None, None
            pi = cnt["wp"] % 2
            cnt["wp"] += 1
            if perm == "A":
                srcv = wsl[si][:].rearrange("p k (g hf b i) -> p (k g) hf b i", g=2, hf=2, b=2)
                dstv = wperm[pi][:].rearrange("p k (g b hf i) -> p (k g) b hf i", g=2, hf=2, b=2)
                for b_ in range(2):
                    copy_op("act", dstv[:, :, b_, :, :], srcv[:, :, :, b_, :], reads=[R_wsl[si]], writes=[R_wperm[pi]])
            else:
                memset(wperm[pi][:, :, 0:64], 0.0, [R_wperm[pi]])
                srcv = wsl[si][:, :, 0:32].rearrange("p k (hf b i) -> p k hf b i", hf=2, b=2)
                dstv = wperm[pi][:, :, 0:64].rearrange("p k (b z hf i) -> p k b z hf i", b=2, z=2, hf=2)
                for b_ in range(2):
                    copy_op("dve", dstv[:, :, b_, 0, :, :], srcv[:, :, :, b_, :], reads=[R_wsl[si]], writes=[R_wperm[pi]])
            return wsl[si], R_wsl[si], wperm[pi], R_wperm[pi]

        def rope_A(dst, dst_res, ps, ps_res, n, tcol0):
            ttop(ta_[:, 0:n], ps, ropeT[:, 0, tcol0:tcol0 + n], ALU.mult, [ps_res, R_ropeT], [R_ta])
            for base in (0, 64):
                ttop(tb_[base:base + 32, 0:n], ps[base + 32:base + 64], ropeT[base + 32:base + 64, 1, tcol0:tcol0 + n],
                     ALU.mult, [ps_res, R_ropeT], [R_tb])
                ttop(tb_[base + 32:base + 64, 0:n], ps[base:base + 32], ropeT[base:base + 32, 1, tcol0:tcol0 + n],
                     ALU.mult, [ps_res, R_ropeT], [R_tb])
            ttop(dst, ta_[:, 0:n], tb_[:, 0:n], ALU.add, [R_ta, R_tb], [dst_res])

        def rope_B(dst64, dst_res, ps64, ps_res, n, tcol0, table, table_res):
            ttop(ta_[0:64, 0:n], ps64, table[0:64, 0, tcol0:tcol0 + n], ALU.mult, [ps_res, table_res], [R_ta])
            ttop(tb_[0:32, 0:n], ps64[32:64], table[32:64, 1, tcol0:tcol0 + n], ALU.mult, [ps_res, table_res], [R_tb])
            ttop(tb_[32:64, 0:n], ps64[0:32], table[0:32, 1, tcol0:tcol0 + n], ALU.mult, [ps_res, table_res], [R_tb])
            ttop(dst64, ta_[0:64, 0:n], tb_[0:64, 0:n], ALU.add, [R_ta, R_tb], [dst_res])

        def rms_feat(banks, nchunks, n, gcol0, dstT, dst_res, dcol0, nfeat):
            for c in range(nchunks):
                actop(zq_ab[:, c, 0:n], PB[banks[c]][:, 0:n], AF.Square, [RB[banks[c]]], [R_zqab])
            for c in range(nchunks):
                mm(PB[3][:, 0:n], onesM[:], zq_ab[:, c, 0:n], c == 0, c == nchunks - 1, [R_zqab, R_ones], [RB[3]])
            actop(tmpden[:, 0:n], PB[3][:, 0:n], AF.Ln, [RB[3]], [R_tmpden], bias=RMS_EPS, scale=1024.0 / nfeat)
            actop(rden[:, 0:n], tmpden[:, 0:n], AF.Exp, [R_tmpden], [R_rden], scale=-0.5)
            for c in range(nchunks):
                ttop(ta_[:, 0:n], PB[banks[c]][:, 0:n], rden[:, 0:n], ALU.mult, [RB[banks[c]], R_rden], [R_ta])
                actop(dstT[:, c, dcol0:dcol0 + n], ta_[:, 0:n], AF.Identity, [R_ta, R_gvec], [dst_res],
                      scale=gvec[:, gcol0 + c:gcol0 + c + 1])

        def ab_mixer(l, do_sample=True):
            i_l = l // 2
            dma("sp", esink[:], a_sink[i_l].partition_broadcast(128), [], [R_esink], S_esink)
            actop(esink[:], esink[:], AF.Exp, [R_esink], [R_esink])
            for col, srcv in ((0, b_g_cq), (3, b_g_ckv)):
                nch = 3 if col == 0 else 2
                P.op("sp", lambda e, col=col, srcv=srcv, nch=nch: e.dma_start(
                    out=gvec[:, col:col + nch], in_=srcv[i_l].rearrange("(k p) -> p k", p=128), allow_slow_non_contiguous=True),
                    writes=[R_gvec], dsem=S_gvec)
            seqs = ((0, TS, 0, True, None), (TS, 512, 1, False, 0))
            for (tok0, L, s, latent, pidx) in seqs:
                if latent and not do_sample:
                    continue
                ab_seq(i_l, tok0, L, s, latent, pidx)
                P.barrier()

        def ab_seq(i_l, tok0, L, s, latent, pidx):
            nqb = L // 128
            koff = 256 if latent else 0
            Lk = L + koff
            nkc = Lk // 128
            n = min(512, L)
            ntile = L // n
            memset(va[:, 0:nkc, :, 64:128], 1.0, [R_va])
            if latent:
                dma("sp", ropeT, ropeA_d, [P.epoch], [R_ropeT], S_rope)
                dma("sp", ropeBk[0:64], ropeB_d, [P.epoch], [R_ropeBk] + R_wgu + R_wdn, S_ropeBk)
                for b_ in range(2):
                    rows = slice(b_ * 128, (b_ + 1) * 128)
                    dma("sp", stg[b_][:, 0:128], ca_k[i_l, rows, :], [P.epoch], [R_stg[b_]], S_stg[b_])
                    dma("sp", stg[b_][:, 128:256], ca_v[i_l, rows, :], [P.epoch], [R_stg[b_]], S_stg[b_])
                    dma("sp", stg[b_][:, 256:512], cb_ckv[i_l, rows, :], [P.epoch], [R_stg[b_]], S_stg[b_])
                    if b_ == 0:
                        for bb in range(2):
                            dma("sp", stgk[:, bb, 0:32], cb_kpe[i_l, bb * 128:(bb + 1) * 128, :], [P.epoch], [R_stgk], S_stgk)
                    srcv = stg[b_][:, 0:128].rearrange("p (g hf b i) -> p g hf b i", g=2, hf=2, b=2)
                    dstv = ta_[:, 0:128].rearrange("p (g b hf i) -> p g b hf i", g=2, hf=2, b=2)
                    for b2 in range(2):
                        copy_op("dve", dstv[:, :, b2, :, :], srcv[:, :, :, b2, :], reads=[R_stg[b_]], writes=[R_ta])
                    tpose(PB[6][:, 0:128], ta_[:, 0:128], ident[:], [R_ta, R_ident], [RB[6]])
                    copy_op("act", kaT[:, rows], PB[6][:, 0:128], reads=[RB[6]], writes=[R_kaT])
                    copy_op("act", va[:, b_, :, 0:64], stg[b_][:, 128:256].rearrange("p (g d) -> p g d", g=2),
                            reads=[R_stg[b_]], writes=[R_va])
                    for c in range(2):
                        tpose(PB[7][:, c * 128:(c + 1) * 128], stg[b_][:, 256 + c * 128:256 + (c + 1) * 128], ident[:],
                              [R_stg[b_], R_ident], [RB[7]])
                    copy_op("dve", ckvnT[:, :, rows], PB[7][:, 0:256].rearrange("p (c t) -> p c t", c=2), reads=[RB[7]], writes=[R_ckvn])
                    memset(tb_[:, 0:64], 0.0, [R_tb])
                    srcv = stgk[:, b_, 0:32].rearrange("p (hf b i) -> p hf b i", hf=2, b=2)
                    dstv = tb_[:, 0:64].rearrange("p (b z hf i) -> p b z hf i", b=2, z=2, hf=2)
                    for b2 in range(2):
                        copy_op("dve", dstv[:, b2, 0, :, :], srcv[:, :, b2, :], reads=[R_stgk], writes=[R_tb])
                    tpose(PB[6][0:64, 128:256], tb_[:, 0:64], ident[:], [R_tb, R_ident], [RB[6]])
                    copy_op("act", kpeT[0:64, rows], PB[6][0:64, 128:256], reads=[RB[6]], writes=[R_kpeT])
            else:
                dma("sp", gckv_bc, b_g_ckv[i_l].partition_broadcast(128), [P.epoch], [R_gbc], S_gbc)
            if AB_STOP[0] < 1:
                return
            CUR_TAG[0] = "P" + ("s" if latent else "p")
            for tt in range(ntile):
                t0 = tok0 + tt * n
                st_idx = t0 // 512
                lc0 = tt * n
                modulate(a_hT, R_ahT, st_idx, n, s, 3, 4, c0=t0 - st_idx * 512)

                def proj(bank, w, wr, mrows=128):
                    for k in range(KC):
                        mm(PB[bank][0:mrows, 0:n], w[:, k, 0:mrows], a_hT[:, k, 0:n], k == 0, k == KC - 1, [wr, R_ahT], [RB[bank]])

                def tok_major(bank, col, w, wr, tb, ncol):
                    for k in range(KC):
                        mm(PB[bank][:, col:col + ncol], a_hT[:, k, tb * 128:(tb + 1) * 128], w[:, k, 0:ncol],
                           k == 0, k == KC - 1, [wr, R_ahT], [RB[bank]])

                for j in range(4):
                    _, _, wp_, wpr = load_wgroup(i_l, (j * 64, (4 + j) * 64), 128, perm="A")
                    bank = j % 3
                    proj(bank, wp_, wpr)
                    if latent:
                        rope_A(qaT[:, j, lc0:lc0 + n], R_qaT[j], PB[bank][:, 0:n], RB[bank], n, lc0)
                    else:
                        copy_op(alt_eng(), qaT[:, j, lc0:lc0 + n], PB[bank][:, 0:n], reads=[RB[bank]], writes=[R_qaT[j]])
                if AB_STOP[0] < 1.2:
                    return
                wn, wnr, wp_, wpr = load_wgroup(i_l, 512, 128, perm="A")
                proj(1, wp_, wpr)
                if latent:
                    rope_A(kaT[:, koff + lc0:koff + lc0 + n], R_kaT, PB[1][:, 0:n], RB[1], n, lc0)
                else:
                    copy_op(alt_eng(), kaT[:, lc0:lc0 + n], PB[1][:, 0:n], reads=[RB[1]], writes=[R_kaT])
                    for tb in range(n // 128):
                        tok_major(4 + tb % 2, (tb // 2) * 256, wn, wnr, tb, 128)
                if AB_STOP[0] < 1.3:
                    return
                wn, wnr, _, _ = load_wgroup(i_l, 640, 128)
                for tb in range(n // 128):
                    bank = 4 + tb % 2
                    cb_ = (tb // 2) * 256
                    tok_major(bank, cb_ + 128, wn, wnr, tb, 128)
                    ch = (koff + lc0) // 128 + tb
                    copy_op("act", va[:, ch, :, 0:64], PB[bank][:, cb_ + 128:cb_ + 256].rearrange("p (g d) -> p g d", g=2),
                            reads=[RB[bank]], writes=[R_va])
                    if not latent and "nostg" not in DBG:
                        sb_ = tb % 2
                        copy_op("dve", stg[sb_][:, 0:256], PB[bank][:, cb_:cb_ + 256], reads=[RB[bank]], writes=[R_stg[sb_]])
                        if "nostore" in DBG:
                            continue
                        pidx, r0 = tb // 2, (tb % 2) * 128
                        dma("sp", nk_o[pidx, i_l, r0:r0 + 128, :], stg[sb_][:, 0:128], [R_stg[sb_]], [R_stg[sb_]], S_stg[sb_])
                        dma("sp", nv_o[pidx, i_l, r0:r0 + 128, :], stg[sb_][:, 128:256], [R_stg[sb_]], [R_stg[sb_]], S_stg[sb_])
                if AB_STOP[0] < 1.4:
                    return
                for c in range(3):
                    wn, wnr, _, _ = load_wgroup(i_l, 768 + 128 * c, 128)
                    proj(c, wn, wnr)
                rms_feat((0, 1, 2), 3, n, 0, cqnT, R_cqn, lc0, 384)
                if AB_STOP[0] < 1.5:
                    return
                for c in range(2):
                    wn, wnr, _, _ = load_wgroup(i_l, 1152 + 128 * c, 128)
                    proj(c, wn, wnr)
                    if not latent:
                        for tb in range(n // 128):
                            tok_major(4 + tb % 2, (tb // 2) * 256 + 128 * c, wn, wnr, tb, 128)
                rms_feat((0, 1), 2, n, 3, ckvnT, R_ckvn, koff + lc0, 256)
                if AB_STOP[0] < 1.6:
                    return
                wn, wnr, wp_, wpr = load_wgroup(i_l, 1408, 32, perm="K")
                proj(2, wp_, wpr, mrows=64)
                if latent:
                    rope_B(kpeT[0:64, koff + lc0:koff + lc0 + n], R_kpeT, PB[2][0:64, 0:n], RB[2], n, lc0, ropeBk, R_ropeBk)
                else:
                    copy_op(alt_eng(), kpeT[0:64, lc0:lc0 + n], PB[2][0:64, 0:n], reads=[RB[2]], writes=[R_kpeT])
                    for tb in range(n // 128):
                        bank = 4 + tb % 2
                        cb_ = (tb // 2) * 256
                        tok_major(3, tb * 32, wn, wnr, tb, 32)
                        sb_ = tb % 2
                        actop(ta_[:, 0:256], PB[bank][:, cb_:cb_ + 256], AF.Square, [RB[bank]], [R_ta])
                        P.op("dve", lambda e: e.reduce_sum(tb_[:, 0:1], ta_[:, 0:256], axis=mybir.AxisListType.X),
                             reads=[R_ta], writes=[R_tb])
                        actop(tb_[:, 1:2], tb_[:, 0:1], AF.Sqrt, [R_tb], [R_tb], bias=RMS_EPS, scale=1.0 / 256.0)
                        P.op("dve", lambda e: e.reciprocal(tb_[:, 2:3], tb_[:, 1:2]), reads=[R_tb], writes=[R_tb])
                        P.op("dve", lambda e, bank=bank, sb_=sb_, cb_=cb_: e.scalar_tensor_tensor(
                            stg[sb_][:, 256:512], PB[bank][:, cb_:cb_ + 256], tb_[:, 2:3], gckv_bc, ALU.mult, ALU.mult),
                            reads=[RB[bank], R_tb, R_gbc, R_stg[sb_]], writes=[R_stg[sb_]])
                        copy_op("act", stgk[:, sb_, 0:32], PB[3][:, tb * 32:(tb + 1) * 32], reads=[RB[3]], writes=[R_stgk])
                        pidx, r0 = tb // 2, (tb % 2) * 128
                        dma("sp", nckv_o[pidx, i_l, r0:r0 + 128, :], stg[sb_][:, 256:512], [R_stg[sb_]], [R_stg[sb_]], S_stg[sb_])
                        dma("sp", nkpe_o[pidx, i_l, r0:r0 + 128, :], stgk[:, sb_, 0:32], [R_stgk], [R_stgk], S_stgk)
            P.barrier()
            if AB_STOP[0] < 2:
                return
            CUR_TAG[0] = "A" + ("s" if latent else "p")
            dma("pool", woutS, ab_w_out[i_l, 0:512, :].rearrange("(c p) d -> p c d", p=128), [P.epoch], [R_woutS], S_woutS)
            esink_t = carve(O_ZB, [128, 2, 512], F32)
            oblk2 = [carve(O_ZB + 4096 + 2048 * i, [128, 4, 256], BF16) for i in range(2)]
            memset(esink_t[64:128, :, :], 0.0, [R_esk])
            for g in range(2):
                for j in range(4):
                    P.op("dve", lambda e, g=g, j=j: e.tensor_scalar(
                        esink_t[64:128, g, j * 128:(j + 1) * 128], esink_t[64:128, g, j * 128:(j + 1) * 128],
                        esink[64:128, 4 * g + j:4 * g + j + 1], None, ALU.add), reads=[R_esink, R_esk], writes=[R_esk])
            for nbp in range(nqb // 2):
                oi = cnt["ob"] % 2
                cnt["ob"] += 1
                ob = oblk2[oi]
                for sub in range(2):
                    nb = 2 * nbp + sub
                    for g in range(2):
                        if latent:
                            chunks = [(0, None), (1, None)]
                            if nb > 0:
                                chunks.append((2 + nb - 1, 0))
                            chunks.append((2 + nb, None))
                            if nb < nqb - 1:
                                chunks.append((2 + nb + 1, 1))
                        else:
                            chunks = [(2 * (nb // 2), None), (2 * (nb // 2) + 1, None)]
                        po = 4 + cnt["po"] % 2
                        cnt["po"] += 1
                        qsl = qaT[64 * g:64 * g + 64, :, nb * 128:(nb + 1) * 128]
                        pend = []
                        nch = len(chunks)
                        for ci in range(nch + 2):
                            if ci < nch:
                                c, mk = chunks[ci]
                                ps = cnt["s3"] % 3
                                cnt["s3"] += 1
                                mm(PB[ps][:, :].rearrange("p (j q) -> p j q", q=128), kaT[64 * g:64 * g + 64, c * 128:(c + 1) * 128], qsl,
                                   True, mk is None, [R_kaT] + R_qaT, [RB[ps]])
                                if mk is not None:
                                    mm(PB[ps][:, :], identB[:], masks[:, mk, :], False, True, [R_identB, R_masks], [RB[ps]])
                                pt = cnt["pt"] % 3
                                cnt["pt"] += 1
                                actop(PT[pt], PB[ps][:, :], AF.Exp, [RB[ps]], [R_PT[pt]], scale=A_SCALE)
                                pend.append((c, pt))
                            if ci >= 2:
                                c2, pt2 = pend[ci - 2]
                                mm(PB[po][:, :], va[:, c2, g, :], PT[pt2], ci - 2 == 0, ci - 2 == nch - 1, [R_va, R_PT[pt2]], [RB[po]])
                        ttop(tmpden[64:128, :], PB[po][64:128, :], esink_t[64:128, g, :], ALU.add, [RB[po], R_esk], [R_tmpden])
                        actop(tmpden[64:128, :], tmpden[64:128, :], AF.Ln, [R_tmpden], [R_tmpden])
                        actop(rden[0:64, :], tmpden[64:128, :], AF.Exp, [R_tmpden], [R_rden], scale=-1.0)
                        for par in range(2):
                            ttop(ob[64 * par:64 * par + 64, 2 * g:2 * g + 2, sub * 128:(sub + 1) * 128],
                                 PB[po][0:64, :].rearrange("p (jj par q) -> p jj par q", par=2, q=128)[:, :, par, :],
                                 rden[0:64, :].rearrange("p (jj par q) -> p jj par q", par=2, q=128)[:, :, par, :],
                                 ALU.mult, [RB[po], R_rden], [R_oblk[oi]])
                tq = tok0 + nbp * 256
                for dc in range(KC):
                    bank = 3 if dc % 2 == 0 else 7
                    for pr in range(4):
                        mm(PB[bank][:, 0:256], woutS[:, pr, dc * 128:(dc + 1) * 128], ob[:, pr, :], pr == 0, pr == 3,
                           [R_woutS, R_oblk[oi]], [RB[bank]])
                    xacc(dc, bank, 256, tq, s)
            P.barrier()
            if AB_STOP[0] < 3:
                return
            CUR_TAG[0] = "Bprep" + ("s" if latent else "p")
            dma("pool", woutS, ab_w_out[i_l, 512:1024, :].rearrange("(c p) d -> p c d", p=128), [P.epoch], [R_woutS], S_woutS)
            if latent:
                dma("sp", ropeT[0:64], ropeB_d, [P.epoch], [R_ropeT], S_rope)
            for bi in range(2):
                memset(hb[bi]["v"][:, 0:18, 64:128], 1.0, [R_hb[bi]["v"]])
                memset(wukv_h[bi][:, :, 0:64], 0.0, [R_wukvh[bi]])
                memset(wuq_h[bi][:, :, 0:64], 0.0, [R_wuqh[bi]])
            def hsel(h):
                if latent:
                    return h % 2, 0, 0, 0
                return h // 4, (h % 4) * 512, (h % 4) * 512, (h % 4) * 4

            def b_prep(h):
                bi = h % 2
                hi, qo, ko, vo = hsel(h)
                H, RH = hb[hi], R_hb[hi]
                units = []

                def u_w():
                    dma("pool", wuq_nat, b_w_uq[i_l].rearrange("(k p) c -> p k c", p=128)[:, :, h * 96:(h + 1) * 96],
                        [P.epoch], [R_wuqn], S_wuq)
                    copy_op("dve", wuq_h[bi][:, :, 64:128], wuq_nat[:, :, 0:64], reads=[R_wuqn], writes=[R_wuqh[bi]])
                    srcv = wuq_nat[:, :, 64:96].rearrange("p k (hf b i) -> p k hf b i", hf=2, b=2)
                    dstv = wuq_h[bi][:, :, 0:64].rearrange("p k (b z hf i) -> p k b z hf i", b=2, z=2, hf=2)
                    for b2 in range(2):
                        copy_op("dve", dstv[:, :, b2, 0, :, :], srcv[:, :, :, b2, :], reads=[R_wuqn], writes=[R_wuqh[bi]])
                    dma("pool", wukv_h[bi][:, :, 64:192], b_w_ukv[i_l].rearrange("(k p) c -> p k c", p=128)[:, :, h * 128:(h + 1) * 128],
                        [P.epoch], [R_wukvh[bi]], S_wukv[bi])
                    copy_op("dve", H["k"][0:64, ko:ko + Lk], kpeT[0:64, 0:Lk], reads=[R_kpeT], writes=[RH["k"]])
                units.append(u_w)

                def nbank():
                    bk = (6, 7, 3)[cnt["pb"] % 3]
                    cnt["pb"] += 1
                    return bk

                def u_q(tt):
                    CUR_TAG[0] = "Bprep" + ("s" if latent else "p")
                    lc0 = tt * n
                    bk = nbank()
                    for kc in range(3):
                        mm(PB[bk][:, 0:n], wuq_h[bi][:, kc, :], cqnT[:, kc, lc0:lc0 + n], kc == 0, kc == 2, [R_wuqh[bi], R_cqn], [RB[bk]])
                    copy_op("dve", H["q"][64:128, qo + lc0:qo + lc0 + n], PB[bk][64:128, 0:n], reads=[RB[bk]], writes=[RH["q"]])
                    if latent:
                        rope_B(H["q"][0:64, qo + lc0:qo + lc0 + n], RH["q"], PB[bk][0:64, 0:n], RB[bk], n, lc0, ropeT, R_ropeT)
                    else:
                        copy_op("dve", H["q"][0:64, qo + lc0:qo + lc0 + n], PB[bk][0:64, 0:n], reads=[RB[bk]], writes=[RH["q"]])

                def u_k(k0):
                    CUR_TAG[0] = "Bprep" + ("s" if latent else "p")
                    m = min(512, Lk - k0)
                    bk = nbank()
                    for kc in range(2):
                        mm(PB[bk][:, 0:m], wukv_h[bi][:, kc, 0:128], ckvnT[:, kc, k0:k0 + m], kc == 0, kc == 1, [R_wukvh[bi], R_ckvn], [RB[bk]])
                    copy_op("dve", H["k"][64:128, ko + k0:ko + k0 + m], PB[bk][64:128, 0:m], reads=[RB[bk]], writes=[RH["k"]])

                def u_v(c0):
                    CUR_TAG[0] = "Bprep" + ("s" if latent else "p")
                    nc_ = min(8, nkc - c0)
                    bk = nbank()
                    for c in range(nc_):
                        for kc in range(2):
                            mm(PB[bk][:, c * 64:(c + 1) * 64], ckvnT[:, kc, (c0 + c) * 128:(c0 + c + 1) * 128], wukv_h[bi][:, kc, 128:192],
                               kc == 0, kc == 1, [R_wukvh[bi], R_ckvn], [RB[bk]])
                    copy_op("dve", H["v"][:, vo + c0:vo + c0 + nc_, 0:64], PB[bk][:, 0:nc_ * 64].rearrange("p (c d) -> p c d", d=64),
                            reads=[RB[bk]], writes=[RH["v"]])

                for tt in range(ntile):
                    units.append(lambda tt=tt: u_q(tt))
                for k0 in range(0, Lk, 512):
                    units.append(lambda k0=k0: u_k(k0))
                for c0 in range(0, nkc, 8):
                    units.append(lambda c0=c0: u_v(c0))
                return units

            def b_att(h, tt, side):
                hi, qo, ko, vo = hsel(h)
                H, RH = hb[hi], R_hb[hi]
                lc0 = tt * n
                po = 4 + cnt["po"] % 2
                cnt["po"] += 1
                if latent:
                    work = [(c, 0, n, c == 0, c == nkc - 1) for c in range(nkc)]
                else:
                    work = [(c, 256 * (c // 2), 256, c % 2 == 0, c % 2 == 1) for c in range(nkc)]
                pend = []
                nw = len(work)
                for ci in range(nw + 2):
                    CUR_TAG[0] = "Batt" + ("s" if latent else "p")
                    if ci < nw:
                        c, q0, qn, _, _ = work[ci]
                        ps = cnt["s3"] % 3
                        cnt["s3"] += 1
                        mm(PB[ps][:, 0:qn], H["k"][:, ko + c * 128:ko + (c + 1) * 128], H["q"][:, qo + lc0 + q0:qo + lc0 + q0 + qn], True, True,
                           [RH["k"], RH["q"]], [RB[ps]])
                        pt = cnt["pt"] % 3
                        cnt["pt"] += 1
                        actop(PT[pt][:, 0:qn], PB[ps][:, 0:qn], AF.Exp, [RB[ps]], [R_PT[pt]], scale=B_SCALE)
                        pend.append(pt)
                    if ci >= 2:
                        c, q0, qn, st_, sp_ = work[ci - 2]
                        pt2 = pend[ci - 2]
                        mm(PB[po][:, q0:q0 + qn], H["v"][:, vo + c, :], PT[pt2][:, 0:qn], st_, sp_, [RH["v"], R_PT[pt2]], [RB[po]])
                    if side and ci % 6 == 5:
                        side.pop(0)()
                P.op("dve", lambda e, po=po: e.reciprocal(rden[0:64, 0:n], PB[po][64:128, 0:n]), reads=[RB[po]], writes=[R_rden])
                hp = 64 * (h % 2)
                ttop(oTB[hp:hp + 64, h // 2, lc0:lc0 + n], PB[po][0:64, 0:n], rden[0:64, 0:n], ALU.mult,
                     [RB[po], R_rden], [R_qaT[h // 2]])

            cnt["pb"] = 0
            if latent:
                for u_ in b_prep(0):
                    u_()
                for h in range(8):
                    side = b_prep(h + 1) if h + 1 < 8 else []
                    for tt in range(ntile):
                        b_att(h, tt, side)
                    while side:
                        side.pop(0)()
            else:
                for h in range(8):
                    for u_ in b_prep(h):
                        u_()
                for h in range(8):
                    b_att(h, 0, None)
            CUR_TAG[0] = "Bout" + ("s" if latent else "p")
            for tt in range(ntile):
                lc0 = tt * n
                for dc in range(KC):
                    bank = 3 if dc % 2 == 0 else 7
                    for pr in range(4):
                        mm(PB[bank][:, 0:n], woutS[:, pr, dc * 128:(dc + 1) * 128], oTB[:, pr, lc0:lc0 + n], pr == 0, pr == 3,
                           [R_woutS, R_qaT[pr]], [RB[bank]])
                    xacc(dc, bank, n, tok0 + lc0, s)

        def store_tokens(dst, tok0, ntok):
            for tb in range(ntok // 128):
                i = cnt["io"] % 2
                cnt["io"] += 1
                t0 = tok0 + tb * 128
                s = t0 // 512
                for h in range(2):
                    bank = 2 * (tb % 2) + h
                    for kk in range(4):
                        k = h * 4 + kk
                        P.op("pe", lambda e, k=k, kk=kk, bank=bank, t0=t0: e.transpose(
                            PB[bank][:, kk * 128:(kk + 1) * 128], xT[:, k, t0:t0 + 128], ident[:]),
                            reads=[R_xT[k][s], R_ident], writes=[RB[bank]])
                    copy_op(alt_eng(), iost[i][:, h * 512:(h + 1) * 512], PB[bank][:, :], reads=[RB[bank]], writes=[R_io[i]])
                P.op("sp", lambda e, i=i, tb=tb: e.dma_start(out=dst[tb * 128:(tb + 1) * 128, :], in_=iost[i][:]),
                     reads=[R_io[i]], writes=[R_io[i]], dsem=S_io[i])

        nlayers = DEPTH if stage >= 10 else (2 if stage == 3 else 1)
        while LOAD_SIDE:
            LOAD_SIDE.pop(0)()
        for l in range(nlayers):
            MODS["set"] = l % 2
            if stage >= 1 and not skip_ffn:
                half_ffn(l, 0)
            if stage >= 2:
                P.barrier()
                if l % 2 == 1 and stage >= 3:
                    f_mixer(l)
                if l % 2 == 0 and stage >= 4:
                    ab_mixer(l, do_sample=(stage >= 5))
                P.barrier()
                side = ada_steps(l + 1, (l + 1) % 2, 3) if l + 1 < nlayers else None
                if not skip_ffn:
                    half_ffn(l, 1, pre_ln=1, side=side)
                else:
                    for st_idx in range(5):
                        layer_norm(l, 1, st_idx)
                    while side:
                        side.pop(0)()
        P.barrier()
        store_tokens(ys, 0, TS)
        store_tokens(yp, TS, TP)
        P.emit()
    return nc


_CACHE = {}


def _constants():
    if "const" in _CACHE:
        return _CACHE["const"]
    bf = ml_dtypes.bfloat16
    out = {}
    for name, L in (("dft_big", TS), ("dft_small", 256)):
        n = np.arange(L, dtype=np.int64)
        ang = 2.0 * np.pi * ((n[:, None] * n[None, :]) % L).astype(np.float64) / L
        m = np.stack([np.cos(ang), -np.sin(ang)], axis=1) / np.sqrt(L)
        out[name] = np.ascontiguousarray(m.astype(np.float32).astype(bf))
    n = np.arange(256, dtype=np.int64)
    ang = 2.0 * np.pi * ((n[:, None] * n[None, :]) % 256).astype(np.float64) / 256
    out["csg"] = np.ascontiguousarray((np.concatenate([np.cos(ang), np.sin(ang)], axis=1) / 16.0).astype(np.float32).astype(bf))
    t = np.arange(TS)
    pos = np.stack([t // 64, t % 64]).astype(np.float64)
    ra = np.zeros((128, 2, TS), np.float64)
    for p in range(128):
        dp = p % 64
        b_, hf, i = dp // 32, (dp % 32) // 16, dp % 16
        ang = pos[hf] * (10000.0 ** (-i / 16.0))
        ra[p, 0] = np.cos(ang)
        ra[p, 1] = np.sin(ang) * (-1.0 if b_ == 1 else 1.0)
    out["ropeA"] = ra.astype(np.float32)
    rb = np.zeros((64, 2, TS), np.float64)
    for p in range(64):
        b_, r = p // 32, p % 32
        if r < 16:
            hf, i = r // 8, r % 8
            ang = pos[hf] * (10000.0 ** (-i / 8.0))
            rb[p, 0] = np.cos(ang)
            rb[p, 1] = np.sin(ang) * (-1.0 if b_ == 1 else 1.0)
    out["ropeB"] = rb.astype(np.float32)
    kj = np.arange(128)[:, None]
    qi = np.arange(128)[None, :]
    NEG = -30000.0
    m0 = np.where(qi <= kj, 0.0, NEG)
    m1 = np.where(kj <= qi, 0.0, NEG)
    out["masks"] = np.ascontiguousarray(np.stack([np.tile(m0, (1, 4)), np.tile(m1, (1, 4))], axis=1).astype(np.float32).astype(bf))
    _CACHE["const"] = out
    return out


def _prep_inputs(inp):
    f32 = np.float32
    small = np.concatenate([
        np.asarray(inp["b_ada"], f32).reshape(DEPTH, 72, 128),
        np.asarray(inp["ln_g"], f32).reshape(DEPTH, 24, 128),
        np.asarray(inp["ln_b"], f32).reshape(DEPTH, 24, 128)], axis=1)
    shared = {
        "w_ada": np.ascontiguousarray(inp["w_ada"], f32),
        "smallp": np.ascontiguousarray(small),
        "wg": np.ascontiguousarray(inp["ffn_w_gate"], f32),
        "wu": np.ascontiguousarray(inp["ffn_w_up"], f32),
        "wd": np.ascontiguousarray(inp["ffn_w_down"], f32),
        "ident": np.eye(128, dtype=f32),
        "f_w_in": np.ascontiguousarray(inp["f_w_in"], f32),
        "f_w_out": np.ascontiguousarray(inp["f_w_out"], f32),
    }
    for k_ in ("ab_w_in", "ab_w_out", "a_sink", "b_g_cq", "b_w_uq", "b_g_ckv", "b_w_ukv"):
        shared[k_] = np.ascontiguousarray(inp[k_], f32)
    shared.update(_constants())
    maps = []
    for b in range(8):
        m = dict(shared)
        m["xs"] = np.ascontiguousarray(inp["x_sample"][b], f32)
        m["xp"] = np.ascontiguousarray(np.asarray(inp["x_prompt"][2 * b:2 * b + 2], f32).reshape(TP, D))
        m["c2"] = np.ascontiguousarray(np.stack([np.asarray(inp["c"][b], f32), np.asarray(inp["c_ctx"], f32)]))
        m["ca_k"] = np.ascontiguousarray(np.asarray(inp["cache_a_k"][b], f32).reshape(2, 256, 128))
        m["ca_v"] = np.ascontiguousarray(np.asarray(inp["cache_a_v"][b], f32).reshape(2, 256, 128))
        m["cb_ckv"] = np.ascontiguousarray(inp["cache_b_ckv"][b], f32)
        m["cb_kpe"] = np.ascontiguousarray(inp["cache_b_kpe"][b], f32)
        maps.append(m)
    return maps


def kernel(**inputs):
    stage = inputs.pop("_stage", 99)
    ncores = inputs.pop("_ncores", 8)
    if stage not in _CACHE:
        _CACHE[stage] = build_program(stage)
    nc = _CACHE[stage]
    maps = _prep_inputs(inputs)[:ncores]
    res = bu.run_bass_kernel_spmd(nc, maps, core_ids=list(range(ncores)))
    r = res.results
    y_s = np.stack([r[b]["ys"] for b in range(ncores)])
    y_p = np.concatenate([r[b]["yp"].reshape(2, 256, D) for b in range(ncores)])
    nk = np.concatenate([r[b]["nk"] for b in range(ncores)]).reshape(2 * ncores, 2, 256, 2, 64)
    nv = np.concatenate([r[b]["nv"] for b in range(ncores)]).reshape(2 * ncores, 2, 256, 2, 64)
    nckv = np.concatenate([r[b]["nckv"] for b in range(ncores)])
    nkpe = np.concatenate([r[b]["nkpe"] for b in range(ncores)])
    return y_p, y_s, nk, nv, nckv, nkpe
```

```python
import contextlib
import numpy as np
import ml_dtypes
import concourse.bass as bass
import concourse.mybir as mybir
import concourse.bass_utils as bu

F32 = mybir.dt.float32
BF16 = mybir.dt.bfloat16
AF = mybir.ActivationFunctionType
ALU = mybir.AluOpType

D = 1024
DFF = 2816
NF = 22
KC = 8
TS = 2048
TP = 512
T = TS + TP
DEPTH = 4
ALPHA = (2 * DEPTH) ** 0.25
LN_EPS = 1e-5
RMS_EPS = 1e-6
ENGS = ("sp", "act", "pool", "dve", "pe")


class Res:
    __slots__ = ("name", "last_w", "readers", "excl")

    def __init__(self, name="", excl=False):
        self.name = name
        self.last_w = None
        self.readers = []
        self.excl = excl


class DmaSem:
    __slots__ = ("h", "count", "name")

    def __init__(self, h, name):
        self.h = h
        self.count = 0
        self.name = name


class Op:
    __slots__ = ("eng", "fn", "seq", "deps", "signal", "sigidx", "dsem", "dval", "is_dma")


class Prog:
    def __init__(self, nc):
        self.nc = nc
        self.ops = {e: [] for e in ENGS}
        self.dsems = []
        self.epoch = Res("epoch")
        self.pending_stores = []
        self.last_dma = {}

    def op(self, eng, fn, reads=(), writes=(), dsem=None, extra=()):
        o = Op()
        o.eng = eng
        o.fn = fn
        o.seq = len(self.ops[eng])
        o.signal = False
        o.sigidx = 0
        o.dsem = dsem
        o.is_dma = dsem is not None
        if dsem is not None:
            dsem.count += 16
            o.dval = dsem.count
        else:
            o.dval = 0
        deps = {}
        for r in reads:
            d = r.last_w
            if d is not None:
                deps[id(d)] = d
            if r.excl:
                for d in r.readers:
                    if d.eng != eng:
                        deps[id(d)] = d
        for w in writes:
            d = w.last_w
            if d is not None:
                deps[id(d)] = d
            for d in w.readers:
                deps[id(d)] = d
        for d in extra:
            deps[id(d)] = d
        dl = []
        for d in deps.values():
            if d is o:
                continue
            if (not d.is_dma) and (not o.is_dma) and d.eng == "pe" and eng == "pe":
                continue
            if d.is_dma and o.is_dma and d.dsem is o.dsem:
                continue
            dl.append(d)
            if not d.is_dma:
                d.signal = True
        best = {}
        for d in dl:
            if d.is_dma:
                k_ = id(d.dsem)
                if k_ not in best or best[k_].dval < d.dval:
                    best[k_] = d
        dl = [d for d in dl if (not d.is_dma) or best[id(d.dsem)] is d]
        o.deps = dl
        for r in reads:
            r.readers.append(o)
        for w in writes:
            w.last_w = o
            w.readers = []
        self.ops[eng].append(o)
        if o.is_dma:
            self.last_dma[id(dsem)] = o
        return o

    def barrier(self):
        lasts = [self.ops[e][-1] for e in ("act", "dve", "pe") if self.ops[e]]
        lasts += list(self.last_dma.values())
        self.op("dve", lambda e: e.nop(), writes=[self.epoch], extra=lasts)
        self.op("act", lambda e: e.nop(), reads=[self.epoch])
        self.op("pe", lambda e: e.nop(), reads=[self.epoch])

    def emit(self):
        nc = self.nc
        for e in ENGS:
            k = 0
            for o in self.ops[e]:
                if o.signal and not o.is_dma:
                    k += 1
                    o.sigidx = k
        with contextlib.ExitStack() as st:
            esem = {e: st.enter_context(nc.semaphore("s_" + e)) for e in ENGS}
            block = st.enter_context(nc.Block())

            def make(e):
                def body(engh):
                    known = {}
                    for o in self.ops[e]:
                        for d in o.deps:
                            if d.is_dma:
                                key = id(d.dsem)
                                val = d.dval
                                sem = d.dsem.h
                            else:
                                key = d.eng
                                val = d.sigidx
                                sem = esem[d.eng]
                            if known.get(key, 0) >= val:
                                continue
                            engh.wait_ge(sem, val)
                            known[key] = val
                        ins = o.fn(engh)
                        if o.is_dma:
                            ins.then_inc(o.dsem.h, 16)
                        elif o.signal:
                            ins.then_inc(esem[e], 1)
                    if e == "sp":
                        for ds in self.dsems:
                            if ds.count > 0:
                                engh.wait_ge(ds.h, ds.count)
                return body

            block.sync(make("sp"))
            block.scalar(make("act"))
            block.gpsimd(make("pool"))
            block.vector(make("dve"))
            block.tensor(make("pe"))


AB_STOP = [99]
MM_TAGS = []
CUR_TAG = ["-"]
DBG = set()


def build_program(stage=99, skip_ffn=False):
    nc = bass.Bass("TRN2", target_bir_lowering=False)

    def din(name, shape, dt=F32):
        return nc.dram_tensor(name, list(shape), dt, kind="ExternalInput").ap()

    def dout(name, shape):
        return nc.dram_tensor(name, list(shape), F32, kind="ExternalOutput").ap()

    xs = din("xs", [TS, D])
    xp = din("xp", [TP, D])
    c2 = din("c2", [2, D])
    w_ada = din("w_ada", [DEPTH, D, 9 * D])
    smallp = din("smallp", [DEPTH, 120, 128])
    wg = din("wg", [DEPTH, 2, D, DFF])
    wu = din("wu", [DEPTH, 2, D, DFF])
    wd = din("wd", [DEPTH, 2, DFF, D])
    ident_d = din("ident", [128, 128])
    f_w_in = din("f_w_in", [2, D, D])
    f_w_out = din("f_w_out", [2, D, D])
    dft_big = din("dft_big", [TS, 2, TS], BF16)
    dft_small = din("dft_small", [256, 2, 256], BF16)
    csg_d = din("csg", [256, 512], BF16)
    ab_w_in = din("ab_w_in", [2, D, 1440])
    ab_w_out = din("ab_w_out", [2, D, D])
    a_sink = din("a_sink", [2, 8])
    b_g_cq = din("b_g_cq", [2, 384])
    b_w_uq = din("b_w_uq", [2, 384, 768])
    b_g_ckv = din("b_g_ckv", [2, 256])
    b_w_ukv = din("b_w_ukv", [2, 256, 1024])
    ca_k = din("ca_k", [2, 256, 128])
    ca_v = din("ca_v", [2, 256, 128])
    cb_ckv = din("cb_ckv", [2, 256, 256])
    cb_kpe = din("cb_kpe", [2, 256, 32])
    ropeA_d = din("ropeA", [128, 2, TS])
    ropeB_d = din("ropeB", [64, 2, TS])
    masks_d = din("masks", [128, 2, 512], BF16)
    nk_o = dout("nk", [2, 2, 256, 128])
    nv_o = dout("nv", [2, 2, 256, 128])
    nckv_o = dout("nckv", [2, 2, 256, 256])
    nkpe_o = dout("nkpe", [2, 2, 256, 32])
    ys = dout("ys", [TS, D])
    yp = dout("yp", [TP, D])

    st = contextlib.ExitStack()
    with st:
        P = Prog(nc)

        def sb(name, shape, dt):
            return st.enter_context(nc.sbuf_tensor(name, list(shape), dt))

        def dsem(name):
            s = DmaSem(st.enter_context(nc.semaphore("d_" + name)), name)
            P.dsems.append(s)
            return s

        xT = sb("xT", [128, KC, T], F32)
        ident = sb("ident_sb", [128, 128], F32)
        onesM = sb("onesM", [128, 128], BF16)
        spT = sb("spT", [128, DEPTH, 120], F32)
        scT = sb("scT", [128, 16], BF16)
        mods_sets = [sb("mods%d" % i, [128, 2, 72], F32) for i in range(2)]
        MODS = {"set": 0}
        csG = sb("csG", [128, 2, 512], BF16)
        identB = sb("identB", [128, 128], BF16)
        masks = sb("masks_sb", [128, 2, 512], BF16)
        esink = sb("esink", [128, 8], F32)
        gvec = sb("gvec", [128, 5], F32)
        PB = [st.enter_context(nc.psum_tensor("pb%d" % i, [128, 512], F32)) for i in range(8)]
        RB = [Res("pb%d" % i, excl=True) for i in range(8)]
        ARENA_BYTES = 121856
        arena = sb("arena", [128, ARENA_BYTES // 4], F32)

        def carve(off, shape, dt):
            esz = 4 if dt == F32 else 2
            n = 1
            for d_ in shape[1:]:
                n *= d_
            assert off % 4 == 0 and (n * esz) % 4 == 0 and off + n * esz <= ARENA_BYTES, (off, shape)
            a = arena[:, off // 4:(off + n * esz) // 4]
            if dt != F32:
                a = a.bitcast(dt)
            if len(shape) == 3:
                a = a.rearrange("p (a b) -> p a b", b=shape[2])
            elif len(shape) == 4:
                a = a.rearrange("p (a b c) -> p a b c", b=shape[2], c=shape[3])
            if shape[0] < 128:
                a = a[0:shape[0]]
            return a

        O_WGU = 0
        O_WDN = O_WGU + 16384
        O_ZB = O_WDN + 11264
        O_ZQ = O_ZB + 8192
        O_ST = O_ZQ + 8192
        O_LT = O_ST + 8192
        O_WORK = O_LT + 4096
        O_HT = O_WORK
        O_AT = O_HT + 16384
        O_SG = O_AT + 45056
        assert O_SG + 4096 <= ARENA_BYTES
        hT = carve(O_HT, [128, KC, 1024], BF16)
        aT = carve(O_AT, [128, NF, 1024], BF16)
        wgu = [carve(O_WGU + 4096 * i, [128, 2, KC, 128], BF16) for i in range(4)]
        wad = [carve(O_WGU + 8192 * i, [128, KC, 512], BF16) for i in range(2)]
        wdn = [carve(O_WDN + 2816 * i, [128, 11, 128], BF16) for i in range(4)]
        sg = [carve(O_SG + 2048 * i, [128, 512], F32) for i in range(2)]
        zb = carve(O_ZB, [128, KC, 512], BF16)
        zq = carve(O_ZQ, [128, KC, 512], BF16)
        st_mean = carve(O_ST, [128, 512], F32)
        st_a = carve(O_ST + 2048, [128, 512], F32)
        st_rstd = carve(O_ST + 4096, [128, 512], F32)
        st_mr = carve(O_ST + 6144, [128, 512], F32)
        lt = [carve(O_LT + 2048 * i, [128, 512], F32) for i in range(2)]
        iost = [carve(O_ZB, [128, D], F32), carve(O_ZQ, [128, D], F32)]
        sp_in = carve(O_HT, [120, DEPTH, 128], F32)
        c_in = carve(O_HT + 4096, [16, 128], F32)

        R_xT = [[Res("xT%d_%d" % (k, s)) for s in range(5)] for k in range(KC)]
        R_ident, R_ones, R_spT, R_scT = Res("ident"), Res("ones"), Res("spT"), Res("scT")
        R_mods_sets = [Res("mods0"), Res("mods1")]

        def cur_rm():
            return R_mods_sets[MODS["set"]]
        R_hT = [Res("hT%d" % s) for s in range(2)]
        R_aT = [[Res("aT%d_%d" % (f, s)) for s in range(2)] for f in range(NF)]
        R_wgu = [Res("wgu%d" % i) for i in range(4)]
        R_wdn = [Res("wdn%d" % i) for i in range(4)]
        R_wad = [Res("wad0"), Res("wad1")]
        R_sg = [Res("sg0"), Res("sg1")]
        R_zb, R_zq = Res("zb"), Res("zq")
        R_mean, R_sta, R_rstd, R_mr = Res("mean"), Res("sta"), Res("rstd"), Res("mr")
        R_lt = [Res("lt0"), Res("lt1")]
        R_io = [Res("io0"), Res("io1")]
        R_spin, R_cin = Res("spin"), Res("cin")
        S_wgu = [dsem("wgu%d" % i) for i in range(4)]
        S_wdn = [dsem("wdn%d" % i) for i in range(4)]
        S_wad = [dsem("wad0"), dsem("wad1")]
        S_io = [dsem("io0"), dsem("io1")]
        S_misc = dsem("misc")

        cnt = {"wgu": 0, "wdn": 0, "wad": 0, "io": 0, "sg": 0, "lt": 0, "eng": 0, "p2": 0}

        def alt_eng():
            cnt["eng"] += 1
            return "act" if cnt["eng"] % 2 else "dve"

        def copy_op(eng, out, in_, reads, writes):
            if eng == "act":
                return P.op("act", lambda e: e.copy(out, in_), reads=reads, writes=writes)
            return P.op("dve", lambda e: e.tensor_copy(out, in_), reads=reads, writes=writes)

        P.op("sp", lambda e: e.dma_start(out=ident[:], in_=ident_d), writes=[R_ident], dsem=dsem("ident"))
        P.op("dve", lambda e: e.memset(onesM[:], 1.0 / 1024.0), writes=[R_ones])
        P.op("sp", lambda e: e.dma_start(out=sp_in, in_=smallp.rearrange("l r p -> r l p")),
             writes=[R_spin], dsem=dsem("spin"))
        S_cin = dsem("cin")
        for s_ in range(2):
            P.op("sp", lambda e, s_=s_: e.dma_start(out=c_in[s_ * 8:(s_ + 1) * 8, :], in_=c2[s_].rearrange("(k p) -> k p", p=128)),
                 writes=[R_cin], dsem=S_cin)
        for l in range(DEPTH):
            P.op("pe", lambda e, l=l: e.transpose(PB[0][:, l * 120:(l + 1) * 120], sp_in[:, l, :], ident[0:120, 0:120]),
                 reads=[R_spin, R_ident], writes=[RB[0]])
        P.op("dve", lambda e: e.tensor_copy(spT[:].rearrange("p l r -> p (l r)"), PB[0][:, 0:480]),
             reads=[RB[0]], writes=[R_spT])
        P.op("act", lambda e: e.activation(c_in, c_in, AF.Silu), reads=[R_cin], writes=[R_cin])
        P.op("pe", lambda e: e.transpose(PB[1][:, 0:16], c_in, ident[0:16, 0:16]), reads=[R_cin, R_ident], writes=[RB[1]])
        P.op("dve", lambda e: e.tensor_copy(scT[:], PB[1][:, 0:16]), reads=[RB[1]], writes=[R_scT])

        def ada_steps(l, mset, bank):
            md, rm = mods_sets[mset], R_mods_sets[mset]
            steps = []

            def fill(cb):
                i = cnt["wad"] % 2
                cnt["wad"] += 1
                P.op("pool", lambda e: e.dma_start(
                    out=wad[i][:], in_=w_ada[l].rearrange("(k p) c -> p k c", p=128)[:, :, cb * 512:(cb + 1) * 512]),
                    reads=[P.epoch], writes=[R_wad[i], R_wgu[2 * i], R_wgu[2 * i + 1]], dsem=S_wad[i])
                for m in range(4):
                    for k in range(KC):
                        P.op("pe", lambda e, m=m, k=k: e.matmul(
                            PB[bank][:, m * 2:m * 2 + 2], wad[i][:, k, m * 128:(m + 1) * 128],
                            scT[:].rearrange("p (s k) -> p k s", k=8)[:, k, :], start=(k == 0), stop=(k == KC - 1)),
                            reads=[R_wad[i], R_wgu[2 * i], R_wgu[2 * i + 1], R_scT], writes=[RB[bank]])
                for s_ in range(2):
                    P.op("dve", lambda e, s_=s_: e.tensor_tensor(
                        md[:, s_, cb * 4:cb * 4 + 4], PB[bank][:, 0:8].rearrange("p (j s) -> p j s", s=2)[:, :, s_],
                        spT[:, l, cb * 4:cb * 4 + 4], ALU.add),
                        reads=[RB[bank], R_spT], writes=[rm])

            def fin():
                for j in (1, 4, 7):
                    P.op("dve", lambda e, j=j: e.tensor_scalar(md[:, :, j * 8:(j + 1) * 8], md[:, :, j * 8:(j + 1) * 8],
                                                               1.0, None, ALU.add), reads=[rm], writes=[rm])
                for j, f in ((2, 0.5 / ALPHA), (5, 1.0 / ALPHA), (8, 0.5 / ALPHA)):
                    P.op("dve", lambda e, j=j, f=f: e.tensor_scalar(md[:, :, j * 8:(j + 1) * 8], md[:, :, j * 8:(j + 1) * 8],
                                                                    f, None, ALU.mult), reads=[rm], writes=[rm])

            for cb in range(18):
                steps.append(lambda cb=cb: fill(cb))
            steps.append(fin)
            return steps

        def load_tokens(src, tok0, ntok, side=None):
            for tb in range(ntok // 128):
                if side:
                    side.pop(0)()
                i = cnt["io"] % 2
                cnt["io"] += 1
                t0 = tok0 + tb * 128
                P.op("sp", lambda e, i=i, tb=tb: e.dma_start(out=iost[i][:], in_=src[tb * 128:(tb + 1) * 128, :]),
                     writes=[R_io[i]], dsem=S_io[i])
                for h in range(2):
                    bank = 2 + 2 * (tb % 2) + h
                    for kk in range(4):
                        k = h * 4 + kk
                        P.op("pe", lambda e, i=i, k=k, kk=kk, bank=bank: e.transpose(
                            PB[bank][:, kk * 128:(kk + 1) * 128], iost[i][:, k * 128:(k + 1) * 128], ident[:]),
                            reads=[R_io[i], R_ident], writes=[RB[bank]])
                    s = t0 // 512
                    copy_op(alt_eng(), xT[:, h * 4:(h + 1) * 4, t0:t0 + 128],
                            PB[bank][:].rearrange("p (k t) -> p k t", t=128),
                            reads=[RB[bank]], writes=[R_xT[h * 4 + kk][s] for kk in range(4)])

        P.barrier()
        LOAD_SIDE = ada_steps(0, 0, 6)
        load_tokens(xs, 0, TS, LOAD_SIDE)
        load_tokens(xp, TS, TP, LOAD_SIDE)
        P.barrier()

        def mod_ap(s, j, k):
            return mods_sets[MODS["set"]][:, s, j * 8 + k:j * 8 + k + 1]

        def lng(l, i, k):
            return spT[:, l, 72 + i * 8 + k:72 + i * 8 + k + 1]

        def lnb(l, i, k):
            return spT[:, l, 96 + i * 8 + k:96 + i * 8 + k + 1]

        def layer_norm(l, i, st_idx, n=512, c0=0):
            ln_prep(l, i, st_idx, n, c0)
            ln_rest(l, i, st_idx, n, c0)

        def ln_prep(l, i, st_idx, n=512, c0=0):
            t0 = st_idx * 512 + c0
            rx = [R_xT[k][st_idx] for k in range(KC)]
            for k in range(KC):
                P.op("act", lambda e, k=k: e.copy(zb[:, k, 0:n], xT[:, k, t0:t0 + n]), reads=[rx[k]], writes=[R_zb])
                P.op("act", lambda e, k=k: e.activation(zq[:, k, 0:n], xT[:, k, t0:t0 + n], AF.Square),
                     reads=[rx[k]], writes=[R_zq])

        def ln_rest(l, i, st_idx, n=512, c0=0):
            t0 = st_idx * 512 + c0
            rx = [R_xT[k][st_idx] for k in range(KC)]
            for k in range(KC):
                P.op("pe", lambda e, k=k: e.matmul(PB[6][:, 0:n], onesM[:], zb[:, k, 0:n], start=(k == 0), stop=(k == KC - 1)),
                     reads=[R_zb, R_ones], writes=[RB[6]])
            for k in range(KC):
                P.op("pe", lambda e, k=k: e.matmul(PB[7][:, 0:n], onesM[:], zq[:, k, 0:n], start=(k == 0), stop=(k == KC - 1)),
                     reads=[R_zq, R_ones], writes=[RB[7]])
            P.op("act", lambda e: e.copy(st_mean[:, 0:n], PB[6][:, 0:n]), reads=[RB[6]], writes=[R_mean])
            P.op("dve", lambda e: e.tensor_tensor(st_a[:, 0:n], st_mean[:, 0:n], st_mean[:, 0:n], ALU.mult),
                 reads=[R_mean], writes=[R_sta])
            P.op("dve", lambda e: e.tensor_tensor(st_a[:, 0:n], PB[7][:, 0:n], st_a[:, 0:n], ALU.subtract),
                 reads=[RB[7], R_sta], writes=[R_sta])
            P.op("act", lambda e: e.activation(st_a[:, 0:n], st_a[:, 0:n], AF.Ln, bias=LN_EPS / (ALPHA * ALPHA), scale=1.0),
                 reads=[R_sta], writes=[R_sta])
            P.op("act", lambda e: e.activation(st_rstd[:, 0:n], st_a[:, 0:n], AF.Exp, scale=-0.5), reads=[R_sta], writes=[R_rstd])
            P.op("dve", lambda e: e.tensor_tensor(st_mr[:, 0:n], st_mean[:, 0:n], st_rstd[:, 0:n], ALU.mult),
                 reads=[R_mean, R_rstd], writes=[R_mr])
            for k in range(KC):
                j = cnt["lt"] % 2
                cnt["lt"] += 1
                P.op("dve", lambda e, k=k, j=j: e.tensor_tensor(lt[j][:, 0:n], xT[:, k, t0:t0 + n], st_rstd[:, 0:n], ALU.mult),
                     reads=[rx[k], R_rstd], writes=[R_lt[j]])
                P.op("dve", lambda e, j=j: e.tensor_tensor(lt[j][:, 0:n], lt[j][:, 0:n], st_mr[:, 0:n], ALU.subtract),
                     reads=[R_lt[j], R_mr], writes=[R_lt[j]])
                P.op("act", lambda e, k=k, j=j: e.activation(xT[:, k, t0:t0 + n], lt[j][:, 0:n], AF.Identity,
                                                             bias=lnb(l, i, k), scale=lng(l, i, k)),
                     reads=[R_lt[j], R_spT], writes=[rx[k]])

        def modulate(dst, dst_res, st_idx, ncol, s, jsh, jsc, dcol0=0, c0=0):
            t0 = st_idx * 512 + c0
            for k in range(KC):
                P.op("act", lambda e, k=k, b_=mod_ap(s, jsh, k), s_=mod_ap(s, jsc, k): e.activation(
                    dst[:, k, dcol0:dcol0 + ncol], xT[:, k, t0:t0 + ncol], AF.Identity, bias=b_, scale=s_),
                     reads=[R_xT[k][st_idx], cur_rm()], writes=[dst_res])

        def half_ffn(l, i, pre_ln=None, side=None):
            jb = 0 if i == 0 else 6
            lni = 0 if i == 0 else 2
            bts = ((0, 1), (2, 3), (4,))
            if pre_ln is not None:
                for st_idx in bts[0]:
                    layer_norm(l, pre_ln, st_idx)
            prev = None
            for bt, sts in enumerate(bts):
                nst = len(sts)
                if bt == 0:
                    for si, st_idx in enumerate(sts):
                        s = 1 if st_idx == 4 else 0
                        modulate(hT, R_hT[si], st_idx, 512, s, jb + 0, jb + 1, dcol0=si * 512)
                for f in range(NF):
                    wi = cnt["wgu"] % 4
                    cnt["wgu"] += 1
                    for ww, src in ((0, wg), (1, wu)):
                        P.op("pool", lambda e, wi=wi, ww=ww, src=src, f=f: e.dma_start(
                            out=wgu[wi][:, ww, :, :],
                            in_=src[l, i].rearrange("(k p) f -> p k f", p=128)[:, :, f * 128:(f + 1) * 128]),
                            reads=[P.epoch], writes=[R_wgu[wi]], dsem=S_wgu[wi])
                    for si, st_idx in enumerate(sts):
                        bsel = cnt["sg"] % 2
                        cnt["sg"] += 1
                        bg, bu_ = bsel, 2 + bsel
                        for ww, bank in ((0, bg), (1, bu_)):
                            for k in range(KC):
                                P.op("pe", lambda e, wi=wi, ww=ww, k=k, si=si, bank=bank: e.matmul(
                                    PB[bank][:, :], wgu[wi][:, ww, k, :],
                                    hT[:, k, si * 512:(si + 1) * 512], start=(k == 0), stop=(k == KC - 1)),
                                    reads=[R_wgu[wi], R_hT[si]], writes=[RB[bank]])
                        P.op("act", lambda e, bsel=bsel, bg=bg: e.activation(sg[bsel][:], PB[bg][:, :], AF.Silu),
                             reads=[RB[bg]], writes=[R_sg[bsel]])
                        P.op("dve", lambda e, bsel=bsel, bu_=bu_, f=f, si=si: e.tensor_tensor(
                            aT[:, f, si * 512:(si + 1) * 512], sg[bsel][:], PB[bu_][:, :], ALU.mult),
                            reads=[R_sg[bsel], RB[bu_]], writes=[R_aT[f][si]])
                jobs = []
                nxt_done = False
                if pre_ln is not None and bt + 1 < len(bts):
                    jobs += [(pre_ln, st_idx) for st_idx in bts[bt + 1]]
                if prev is not None:
                    jobs += [(lni, st_idx) for st_idx in prev]
                for dc in range(KC):
                    wis = []
                    for hf in range(2):
                        wi = cnt["wdn"] % 4
                        cnt["wdn"] += 1
                        wis.append(wi)
                        P.op("pool", lambda e, wi=wi, dc=dc, hf=hf: e.dma_start(
                            out=wdn[wi][:],
                            in_=wd[l, i].rearrange("(f p) d -> p f d", p=128)[:, hf * 11:(hf + 1) * 11, dc * 128:(dc + 1) * 128]),
                            reads=[P.epoch], writes=[R_wdn[wi]], dsem=S_wdn[wi])
                    for si, st_idx in enumerate(sts):
                        s = 1 if st_idx == 4 else 0
                        bank = (4, 5, 0, 1, 2)[cnt["p2"] % 5]
                        cnt["p2"] += 1
                        for f in range(NF):
                            P.op("pe", lambda e, wi=wis[f // 11], f=f, si=si, bank=bank: e.matmul(
                                PB[bank][:, :], wdn[wi][:, f % 11, :], aT[:, f, si * 512:(si + 1) * 512],
                                start=(f == 0), stop=(f == NF - 1)),
                                reads=[R_wdn[wis[f // 11]], R_aT[f][si]], writes=[RB[bank]])
                        t0 = st_idx * 512
                        P.op("dve", lambda e, dc=dc, bank=bank, t0=t0, g_=mod_ap(s, jb + 2, dc): e.scalar_tensor_tensor(
                            xT[:, dc, t0:t0 + 512], PB[bank][:, :], g_, xT[:, dc, t0:t0 + 512],
                            ALU.mult, ALU.add),
                            reads=[RB[bank], cur_rm(), R_xT[dc][st_idx]], writes=[R_xT[dc][st_idx]])
                    if jobs and dc % 2 == 0:
                        ln_prep(l, jobs[0][0], jobs[0][1])
                    if jobs and dc % 2 == 1:
                        lj, sj = jobs.pop(0)
                        ln_rest(l, lj, sj)
                    if side:
                        side.pop(0)()
                    if dc == 5 and bt + 1 < len(bts) and not jobs:
                        for si2, st2 in enumerate(bts[bt + 1]):
                            modulate(hT, R_hT[si2], st2, 512, 1 if st2 == 4 else 0, jb + 0, jb + 1, dcol0=si2 * 512)
                        nxt_done = True
                for lj, sj in jobs:
                    layer_norm(l, lj, sj)
                jobs = []
                if bt + 1 < len(bts) and not nxt_done:
                    for si2, st2 in enumerate(bts[bt + 1]):
                        modulate(hT, R_hT[si2], st2, 512, 1 if st2 == 4 else 0, jb + 0, jb + 1, dcol0=si2 * 512)
                prev = sts
            for st_idx in prev:
                layer_norm(l, lni, st_idx)
            while side:
                side.pop(0)()

        uT_all = carve(O_WORK, [128, KC, 2048], BF16)
        ucs_g = carve(O_WORK + 32768, [128, 16, 512], BF16)
        f_win = carve(O_WORK + 32768, [128, KC, 1024], BF16)
        dfts = [carve(O_WGU, [128, 16, 2, 256], BF16), carve(O_WORK + 49152, [128, 16, 2, 256], BF16)]
        f_wout = carve(O_ZB, [128, KC, 1024], BF16)
        f_hT = carve(O_ST, [128, KC, 512], BF16)
        f_fT = [carve(O_LT + 1024 * i, [128, 2, 256], BF16) for i in range(2)]
        R_uT = [Res("uT%d" % c) for c in range(KC)]
        R_ucs, R_fwin, R_fwout, R_fhT, R_csG = Res("ucs"), Res("fwin"), Res("fwout"), Res("fhT"), Res("csG")
        R_dft1 = Res("dft1")
        R_fT = [Res("fT0"), Res("fT1")]
        S_dft = [dsem("dft0"), dsem("dft1")]
        S_fwin, S_fwout = dsem("fwin"), dsem("fwout")
        P.op("sp", lambda e: e.dma_start(out=csG[:], in_=csg_d.rearrange("(c p) k -> p c k", p=128)),
             writes=[R_csG], dsem=dsem("csg"))
        cnt["dft"] = 0
        cnt["fb"] = 0

        def f_mixer(l):
            fi = l // 2
            P.op("pool", lambda e: e.dma_start(out=f_wout, in_=f_w_out[fi].rearrange("(k p) c -> p k c", p=128)),
                 reads=[P.epoch], writes=[R_fwout], dsem=S_fwout)
            for (tok0, L, s) in ((0, TS, 0), (TS, 256, 1), (TS + 256, 256, 1)):
                nb = L // 128
                dsrc = dft_big if L == TS else dft_small
                P.op("pool", lambda e: e.dma_start(out=f_win, in_=f_w_in[fi].rearrange("(k p) c -> p k c", p=128)),
                     reads=[P.epoch], writes=[R_fwin, R_ucs], dsem=S_fwin)
                n = min(512, L)
                for tt in range(L // n):
                    t0 = tok0 + tt * n
                    st_idx = t0 // 512
                    modulate(f_hT, R_fhT, st_idx, n, s, 3, 4, c0=t0 - st_idx * 512)
                    for c in range(KC):
                        bank = cnt["fb"] % 2
                        cnt["fb"] += 1
                        for k in range(KC):
                            P.op("pe", lambda e, c=c, k=k, bank=bank, n=n: e.matmul(
                                PB[bank][:, 0:n], f_win[:, k, c * 128:(c + 1) * 128], f_hT[:, k, 0:n],
                                start=(k == 0), stop=(k == KC - 1)),
                                reads=[R_fwin, R_fhT], writes=[RB[bank]])
                        copy_op(alt_eng(), uT_all[:, c, tt * n:(tt + 1) * n], PB[bank][:, 0:n], reads=[RB[bank]], writes=[R_uT[c]])
                for g in range(4):
                    for tb in range(nb):
                        bank = 2 + tb % 2
                        for cc in range(2):
                            P.op("pe", lambda e, tb=tb, cc=cc, bank=bank, g=g: e.matmul(
                                PB[bank][:, :], uT_all[:, 2 * g + cc, tb * 128:(tb + 1) * 128], csG[:, cc, :],
                                start=(cc == 0), stop=(cc == 1)),
                                reads=[R_uT[2 * g + cc], R_csG], writes=[RB[bank]])
                        copy_op(alt_eng(), ucs_g[:, tb, :], PB[bank][:, :], reads=[RB[bank]], writes=[R_ucs, R_fwin])
                    for kt in range(L // 256):
                        di = cnt["dft"] % 2
                        cnt["dft"] += 1
                        rd = list(R_wgu) if di == 0 else [R_dft1]
                        for sgn in range(2):
                            P.op("sp", lambda e, di=di, kt=kt, nb=nb, dsrc=dsrc, sgn=sgn: e.dma_start(
                                out=dfts[di][:, 0:nb, sgn, :],
                                in_=dsrc.rearrange("(i p) s k -> p i s k", p=128)[:, :, sgn, kt * 256:(kt + 1) * 256]),
                                reads=[P.epoch], writes=rd, dsem=S_dft[di])
                        fb = kt % 2
                        for cc in range(2):
                            bank = 4 + cc
                            for i in range(nb):
                                for sgn in range(2):
                                    P.op("pe", lambda e, di=di, cc=cc, i=i, sgn=sgn, bank=bank, nb=nb: e.matmul(
                                        PB[bank][:, 0:256], ucs_g[:, i, sgn * 256 + cc * 128:sgn * 256 + (cc + 1) * 128],
                                        dfts[di][:, i, sgn, :], start=(i == 0 and sgn == 0), stop=(i == nb - 1 and sgn == 1)),
                                        reads=[R_ucs] + rd, writes=[RB[bank]])
                            copy_op(alt_eng(), f_fT[fb][:, cc, :], PB[bank][:, 0:256], reads=[RB[bank]], writes=[R_fT[fb]])
                        ta = tok0 + kt * 256
                        st_idx = ta // 512
                        for dc in range(KC):
                            bank = 6 + dc % 2
                            for cc in range(2):
                                P.op("pe", lambda e, fb=fb, cc=cc, dc=dc, bank=bank, g=g: e.matmul(
                                    PB[bank][:, 0:256], f_wout[:, 2 * g + cc, dc * 128:(dc + 1) * 128], f_fT[fb][:, cc, :],
                                    start=(cc == 0), stop=(cc == 1)),
                                    reads=[R_fwout, R_fT[fb]], writes=[RB[bank]])
                            P.op("dve", lambda e, dc=dc, bank=bank, ta=ta, g_=mod_ap(s, 5, dc): e.scalar_tensor_tensor(
                                xT[:, dc, ta:ta + 256], PB[bank][:, 0:256], g_, xT[:, dc, ta:ta + 256],
                                ALU.mult, ALU.add),
                                reads=[RB[bank], cur_rm(), R_xT[dc][st_idx]], writes=[R_xT[dc][st_idx]])

        A_SCALE = 0.125
        B_SCALE = 96.0 ** -0.5
        W0 = O_WORK
        qaT = carve(W0, [128, 4, 2048], BF16)
        oTB = carve(W0, [128, 4, 2048], BF16)
        kaT = carve(W0 + 16384, [128, 2304], BF16)
        va = carve(W0 + 20992, [128, 18, 2, 128], BF16)
        wuq_h = [carve(W0 + 16384 + 768 * i, [128, 3, 128], BF16) for i in range(2)]
        wukv_h = [carve(W0 + 18432 + 768 * i, [128, 2, 192], BF16) for i in range(2)]
        wuq_nat = carve(W0 + 20480, [128, 3, 96], BF16)
        cqnT = carve(W0 + 30208, [128, 3, 2048], BF16)
        ckvnT = carve(W0 + 42496, [128, 2, 2304], BF16)
        kpeT = carve(W0 + 51712, [128, 2304], BF16)
        PT = [carve(W0 + 56320 + 1024 * i, [128, 512], BF16) for i in range(3)] + [carve(W0 + 59392, [128, 512], BF16)]
        zq_ab = carve(W0 + 56320, [128, 3, 512], BF16)
        oblk = [carve(W0 + 59392 + 1024 * i, [128, 4, 128], BF16) for i in range(2)]
        stgk = carve(W0 + 59392, [128, 2, 64], F32)
        gckv_bc = carve(W0 + 59392 + 512, [128, 256], F32)
        rden = carve(W0 + 61440, [128, 512], F32)
        tmpden = carve(W0 + 63488, [128, 512], F32)
        stg = [tmpden, rden]
        wsl = [carve(O_WGU + 2048 * i, [128, KC, 128], BF16) for i in range(3)]
        wperm = [carve(O_WGU + 6144 + 2048 * i, [128, KC, 128], BF16) for i in range(2)]
        ropeBk = carve(O_WGU + 10240, [128, 2, 2048], F32)
        hb = [dict(q=carve(O_WGU + 13312 * i, [128, 2048], BF16),
                   k=carve(O_WGU + 13312 * i + 4096, [128, 2304], BF16),
                   v=carve(O_WGU + 13312 * i + 8704, [128, 18, 128], BF16)) for i in range(2)]
        ropeT = carve(O_ZB, [128, 2, 2048], F32)
        a_hT = carve(O_ST, [128, KC, 512], BF16)
        ta_ = carve(O_ST + 8192, [128, 512], F32)
        tb_ = carve(O_ST + 10240, [128, 512], F32)
        woutS = carve(O_ST, [128, 4, 1024], BF16)

        R_qaT = [Res("qaT%d" % j) for j in range(4)]
        R_kaT, R_va, R_cqn, R_ckvn, R_kpeT = Res("kaT"), Res("va"), Res("cqn"), Res("ckvn"), Res("kpeT")
        R_PT = [Res("PT%d" % i) for i in range(4)]
        R_oblk = [Res("oblk0"), Res("oblk1")]
        R_rden, R_tmpden, R_stgk, R_gbc, R_zqab = Res("rden"), Res("tmpden"), Res("stgk"), Res("gbc"), Res("zqab")
        R_esk = Res("esink_t")
        R_stg = [R_tmpden, R_rden]
        R_wsl = [Res("wsl%d" % i) for i in range(3)]
        R_wperm = [Res("wperm0"), Res("wperm1")]
        R_ropeBk, R_ropeT = Res("ropeBk"), Res("ropeT")
        R_hb = [dict(q=Res("hbq%d" % i), k=Res("hbk%d" % i), v=Res("hbv%d" % i)) for i in range(2)]
        R_ahT, R_ta, R_tb = Res("ahT"), Res("ta"), Res("tb")
        R_woutS = Res("woutS")
        R_wuqh = [Res("wuqh0"), Res("wuqh1")]
        R_wukvh = [Res("wukvh0"), Res("wukvh1")]
        R_wuqn = Res("wuqn")
        R_identB, R_masks, R_esink, R_gvec = Res("identB"), Res("masks"), Res("esink"), Res("gvec")
        S_wsl = [dsem("wsl%d" % i) for i in range(3)]
        S_rope, S_ropeBk, S_woutS, S_wuq = dsem("rope"), dsem("ropeBk"), dsem("woutS"), dsem("wuq")
        S_wukv = [dsem("wukv0"), dsem("wukv1")]
        S_stg = [dsem("stg0"), dsem("stg1")]
        S_stgk, S_esink, S_gvec, S_gbc = dsem("stgk"), dsem("esink"), dsem("gvec"), dsem("gbc")
        cnt.update(wsl=0, pt=0, po=0, ob=0, ps=0, wp=0, s3=0)

        def mm(out, lhsT, rhs, start, stop, reads, writes):
            MM_TAGS.append(CUR_TAG[0])
            P.op("pe", lambda e: e.matmul(out, lhsT, rhs, start=start, stop=stop), reads=reads, writes=writes)

        def tpose(out, in_, idn, reads, writes):
            P.op("pe", lambda e: e.transpose(out, in_, idn), reads=reads, writes=writes)

        def ttop(out, in0, in1, op, reads, writes):
            P.op("dve", lambda e: e.tensor_tensor(out, in0, in1, op), reads=reads, writes=writes)

        def actop(out, in_, func, reads, writes, bias=None, scale=None):
            kw = {}
            if bias is not None:
                kw["bias"] = bias
            if scale is not None:
                kw["scale"] = scale
            P.op("act", lambda e: e.activation(out, in_, func, **kw), reads=reads, writes=writes)

        def dma(eng, out, in_, reads, writes, ds):
            P.op(eng, lambda e: e.dma_start(out=out, in_=in_), reads=reads, writes=writes, dsem=ds)

        def memset(ap, val, writes):
            P.op("dve", lambda e: e.memset(ap, val), writes=writes)

        def xacc(dc, bank, ncol, tq, s):
            st_idx = tq // 512
            g_ = mod_ap(s, 5, dc)
            P.op("dve", lambda e: e.scalar_tensor_tensor(
                xT[:, dc, tq:tq + ncol], PB[bank][:, 0:ncol], g_, xT[:, dc, tq:tq + ncol], ALU.mult, ALU.add),
                reads=[RB[bank], cur_rm(), R_xT[dc][st_idx]], writes=[R_xT[dc][st_idx]])

        P.op("dve", lambda e: e.tensor_copy(identB[:], ident[:]), reads=[R_ident], writes=[R_identB])
        dma("sp", masks[:], masks_d, [], [R_masks], dsem("masks"))

        def load_wgroup(i_l, col0, ncols, perm=None):
            si = cnt["wsl"] % 3
            cnt["wsl"] += 1
            src = ab_w_in[i_l].rearrange("(k p) c -> p k c", p=128)
            if isinstance(col0, tuple):
                for gi, c0 in enumerate(col0):
                    dma("pool", wsl[si][:, :, gi * 64:(gi + 1) * 64], src[:, :, c0:c0 + 64], [P.epoch], [R_wsl[si]], S_wsl[si])
            else:
                dma("pool", wsl[si][:, :, 0:ncols], src[:, :, col0:col0 + ncols], [P.epoch], [R_wsl[si]], S_wsl[si])
            if perm is None:
                return wsl[si], R_wsl[si], None, None
            pi = cnt["wp"] % 2
            cnt["wp"] += 1
            if perm == "A":
                srcv = wsl[si][:].rearrange("p k (g hf b i) -> p (k g) hf b i", g=2, hf=2, b=2)
                dstv = wperm[pi][:].rearrange("p k (g b hf i) -> p (k g) b hf i", g=2, hf=2, b=2)
                for b_ in range(2):
                    copy_op("act", dstv[:, :, b_, :, :], srcv[:, :, :, b_, :], reads=[R_wsl[si]], writes=[R_wperm[pi]])
            else:
                memset(wperm[pi][:, :, 0:64], 0.0, [R_wperm[pi]])
                srcv = wsl[si][:, :, 0:32].rearrange("p k (hf b i) -> p k hf b i", hf=2, b=2)
                dstv = wperm[pi][:, :, 0:64].rearrange("p k (b z hf i) -> p k b z hf i", b=2, z=2, hf=2)
                for b_ in range(2):
                    copy_op("dve", dstv[:, :, b_, 0, :, :], srcv[:, :, :, b_, :], reads=[R_wsl[si]], writes=[R_wperm[pi]])
            return wsl[si], R_wsl[si], wperm[pi], R_wperm[pi]

        def rope_A(dst, dst_res, ps, ps_res, n, tcol0):
            ttop(ta_[:, 0:n], ps, ropeT[:, 0, tcol0:tcol0 + n], ALU.mult, [ps_res, R_ropeT], [R_ta])
            for base in (0, 64):
                ttop(tb_[base:base + 32, 0:n], ps[base + 32:base + 64], ropeT[base + 32:base + 64, 1, tcol0:tcol0 + n],
                     ALU.mult, [ps_res, R_ropeT], [R_tb])
                ttop(tb_[base + 32:base + 64, 0:n], ps[base:base + 32], ropeT[base:base + 32, 1, tcol0:tcol0 + n],
                     ALU.mult, [ps_res, R_ropeT], [R_tb])
            ttop(dst, ta_[:, 0:n], tb_[:, 0:n], ALU.add, [R_ta, R_tb], [dst_res])

        def rope_B(dst64, dst_res, ps64, ps_res, n, tcol0, table, table_res):
            ttop(ta_[0:64, 0:n], ps64, table[0:64, 0, tcol0:tcol0 + n], ALU.mult, [ps_res, table_res], [R_ta])
            ttop(tb_[0:32, 0:n], ps64[32:64], table[32:64, 1, tcol0:tcol0 + n], ALU.mult, [ps_res, table_res], [R_tb])
            ttop(tb_[32:64, 0:n], ps64[0:32], table[0:32, 1, tcol0:tcol0 + n], ALU.mult, [ps_res, table_res], [R_tb])
            ttop(dst64, ta_[0:64, 0:n], tb_[0:64, 0:n], ALU.add, [R_ta, R_tb], [dst_res])

        def rms_feat(banks, nchunks, n, gcol0, dstT, dst_res, dcol0, nfeat):
            for c in range(nchunks):
                actop(zq_ab[:, c, 0:n], PB[banks[c]][:, 0:n], AF.Square, [RB[banks[c]]], [R_zqab])
            for c in range(nchunks):
                mm(PB[3][:, 0:n], onesM[:], zq_ab[:, c, 0:n], c == 0, c == nchunks - 1, [R_zqab, R_ones], [RB[3]])
            actop(tmpden[:, 0:n], PB[3][:, 0:n], AF.Ln, [RB[3]], [R_tmpden], bias=RMS_EPS, scale=1024.0 / nfeat)
            actop(rden[:, 0:n], tmpden[:, 0:n], AF.Exp, [R_tmpden], [R_rden], scale=-0.5)
            for c in range(nchunks):
                ttop(ta_[:, 0:n], PB[banks[c]][:, 0:n], rden[:, 0:n], ALU.mult, [RB[banks[c]], R_rden], [R_ta])
                actop(dstT[:, c, dcol0:dcol0 + n], ta_[:, 0:n], AF.Identity, [R_ta, R_gvec], [dst_res],
                      scale=gvec[:, gcol0 + c:gcol0 + c + 1])

        def ab_mixer(l, do_sample=True):
            i_l = l // 2
            dma("sp", esink[:], a_sink[i_l].partition_broadcast(128), [], [R_esink], S_esink)
            actop(esink[:], esink[:], AF.Exp, [R_esink], [R_esink])
            for col, srcv in ((0, b_g_cq), (3, b_g_ckv)):
                nch = 3 if col == 0 else 2
                P.op("sp", lambda e, col=col, srcv=srcv, nch=nch: e.dma_start(
                    out=gvec[:, col:col + nch], in_=srcv[i_l].rearrange("(k p) -> p k", p=128), allow_slow_non_contiguous=True),
                    writes=[R_gvec], dsem=S_gvec)
            seqs = ((0, TS, 0, True, None), (TS, 512, 1, False, 0))
            for (tok0, L, s, latent, pidx) in seqs:
                if latent and not do_sample:
                    continue
                ab_seq(i_l, tok0, L, s, latent, pidx)
                P.barrier()

        def ab_seq(i_l, tok0, L, s, latent, pidx):
            nqb = L // 128
            koff = 256 if latent else 0
            Lk = L + koff
            nkc = Lk // 128
            n = min(512, L)
            ntile = L // n
            memset(va[:, 0:nkc, :, 64:128], 1.0, [R_va])
            if latent:
                dma("sp", ropeT, ropeA_d, [P.epoch], [R_ropeT], S_rope)
                dma("sp", ropeBk[0:64], ropeB_d, [P.epoch], [R_ropeBk] + R_wgu + R_wdn, S_ropeBk)
                for b_ in range(2):
                    rows = slice(b_ * 128, (b_ + 1) * 128)
                    dma("sp", stg[b_][:, 0:128], ca_k[i_l, rows, :], [P.epoch], [R_stg[b_]], S_stg[b_])
                    dma("sp", stg[b_][:, 128:256], ca_v[i_l, rows, :], [P.epoch], [R_stg[b_]], S_stg[b_])
                    dma("sp", stg[b_][:, 256:512], cb_ckv[i_l, rows, :], [P.epoch], [R_stg[b_]], S_stg[b_])
                    if b_ == 0:
                        for bb in range(2):
                            dma("sp", stgk[:, bb, 0:32], cb_kpe[i_l, bb * 128:(bb + 1) * 128, :], [P.epoch], [R_stgk], S_stgk)
                    srcv = stg[b_][:, 0:128].rearrange("p (g hf b i) -> p g hf b i", g=2, hf=2, b=2)
                    dstv = ta_[:, 0:128].rearrange("p (g b hf i) -> p g b hf i", g=2, hf=2, b=2)
                    for b2 in range(2):
                        copy_op("dve", dstv[:, :, b2, :, :], srcv[:, :, :, b2, :], reads=[R_stg[b_]], writes=[R_ta])
                    tpose(PB[6][:, 0:128], ta_[:, 0:128], ident[:], [R_ta, R_ident], [RB[6]])
                    copy_op("act", kaT[:, rows], PB[6][:, 0:128], reads=[RB[6]], writes=[R_kaT])
                    copy_op("act", va[:, b_, :, 0:64], stg[b_][:, 128:256].rearrange("p (g d) -> p g d", g=2),
                            reads=[R_stg[b_]], writes=[R_va])
                    for c in range(2):
                        tpose(PB[7][:, c * 128:(c + 1) * 128], stg[b_][:, 256 + c * 128:256 + (c + 1) * 128], ident[:],
                              [R_stg[b_], R_ident], [RB[7]])
                    copy_op("dve", ckvnT[:, :, rows], PB[7][:, 0:256].rearrange("p (c t) -> p c t", c=2), reads=[RB[7]], writes=[R_ckvn])
                    memset(tb_[:, 0:64], 0.0, [R_tb])
                    srcv = stgk[:, b_, 0:32].rearrange("p (hf b i) -> p hf b i", hf=2, b=2)
                    dstv = tb_[:, 0:64].rearrange("p (b z hf i) -> p b z hf i", b=2, z=2, hf=2)
                    for b2 in range(2):
                        copy_op("dve", dstv[:, b2, 0, :, :], srcv[:, :, b2, :], reads=[R_stgk], writes=[R_tb])
                    tpose(PB[6][0:64, 128:256], tb_[:, 0:64], ident[:], [R_tb, R_ident], [RB[6]])
                    copy_op("act", kpeT[0:64, rows], PB[6][0:64, 128:256], reads=[RB[6]], writes=[R_kpeT])
            else:
                dma("sp", gckv_bc, b_g_ckv[i_l].partition_broadcast(128), [P.epoch], [R_gbc], S_gbc)
            if AB_STOP[0] < 1:
                return
            CUR_TAG[0] = "P" + ("s" if latent else "p")
            for tt in range(ntile):
                t0 = tok0 + tt * n
                st_idx = t0 // 512
                lc0 = tt * n
                modulate(a_hT, R_ahT, st_idx, n, s, 3, 4, c0=t0 - st_idx * 512)

                def proj(bank, w, wr, mrows=128):
                    for k in range(KC):
                        mm(PB[bank][0:mrows, 0:n], w[:, k, 0:mrows], a_hT[:, k, 0:n], k == 0, k == KC - 1, [wr, R_ahT], [RB[bank]])

                def tok_major(bank, col, w, wr, tb, ncol):
                    for k in range(KC):
                        mm(PB[bank][:, col:col + ncol], a_hT[:, k, tb * 128:(tb + 1) * 128], w[:, k, 0:ncol],
                           k == 0, k == KC - 1, [wr, R_ahT], [RB[bank]])

                for j in range(4):
                    _, _, wp_, wpr = load_wgroup(i_l, (j * 64, (4 + j) * 64), 128, perm="A")
                    bank = j % 3
                    proj(bank, wp_, wpr)
                    if latent:
                        rope_A(qaT[:, j, lc0:lc0 + n], R_qaT[j], PB[bank][:, 0:n], RB[bank], n, lc0)
                    else:
                        copy_op(alt_eng(), qaT[:, j, lc0:lc0 + n], PB[bank][:, 0:n], reads=[RB[bank]], writes=[R_qaT[j]])
                if AB_STOP[0] < 1.2:
                    return
                wn, wnr, wp_, wpr = load_wgroup(i_l, 512, 128, perm="A")
                proj(1, wp_, wpr)
                if latent:
                    rope_A(kaT[:, koff + lc0:koff + lc0 + n], R_kaT, PB[1][:, 0:n], RB[1], n, lc0)
                else:
                    copy_op(alt_eng(), kaT[:, lc0:lc0 + n], PB[1][:, 0:n], reads=[RB[1]], writes=[R_kaT])
                    for tb in range(n // 128):
                        tok_major(4 + tb % 2, (tb // 2) * 256, wn, wnr, tb, 128)
                if AB_STOP[0] < 1.3:
                    return
                wn, wnr, _, _ = load_wgroup(i_l, 640, 128)
                for tb in range(n // 128):
                    bank = 4 + tb % 2
                    cb_ = (tb // 2) * 256
                    tok_major(bank, cb_ + 128, wn, wnr, tb, 128)
                    ch = (koff + lc0) // 128 + tb
                    copy_op("act", va[:, ch, :, 0:64], PB[bank][:, cb_ + 128:cb_ + 256].rearrange("p (g d) -> p g d", g=2),
                            reads=[RB[bank]], writes=[R_va])
                    if not latent and "nostg" not in DBG:
                        sb_ = tb % 2
                        copy_op("dve", stg[sb_][:, 0:256], PB[bank][:, cb_:cb_ + 256], reads=[RB[bank]], writes=[R_stg[sb_]])
                        if "nostore" in DBG:
                            continue
                        pidx, r0 = tb // 2, (tb % 2) * 128
                        dma("sp", nk_o[pidx, i_l, r0:r0 + 128, :], stg[sb_][:, 0:128], [R_stg[sb_]], [R_stg[sb_]], S_stg[sb_])
                        dma("sp", nv_o[pidx, i_l, r0:r0 + 128, :], stg[sb_][:, 128:256], [R_stg[sb_]], [R_stg[sb_]], S_stg[sb_])
                if AB_STOP[0] < 1.4:
                    return
                for c in range(3):
                    wn, wnr, _, _ = load_wgroup(i_l, 768 + 128 * c, 128)
                    proj(c, wn, wnr)
                rms_feat((0, 1, 2), 3, n, 0, cqnT, R_cqn, lc0, 384)
                if AB_STOP[0] < 1.5:
                    return
                for c in range(2):
                    wn, wnr, _, _ = load_wgroup(i_l, 1152 + 128 * c, 128)
                    proj(c, wn, wnr)
                    if not latent:
                        for tb in range(n // 128):
                            tok_major(4 + tb % 2, (tb // 2) * 256 + 128 * c, wn, wnr, tb, 128)
                rms_feat((0, 1), 2, n, 3, ckvnT, R_ckvn, koff + lc0, 256)
                if AB_STOP[0] < 1.6:
                    return
                wn, wnr, wp_, wpr = load_wgroup(i_l, 1408, 32, perm="K")
                proj(2, wp_, wpr, mrows=64)
                if latent:
                    rope_B(kpeT[0:64, koff + lc0:koff + lc0 + n], R_kpeT, PB[2][0:64, 0:n], RB[2], n, lc0, ropeBk, R_ropeBk)
                else:
                    copy_op(alt_eng(), kpeT[0:64, lc0:lc0 + n], PB[2][0:64, 0:n], reads=[RB[2]], writes=[R_kpeT])
                    for tb in range(n // 128):
                        bank = 4 + tb % 2
                        cb_ = (tb // 2) * 256
                        tok_major(3, tb * 32, wn, wnr, tb, 32)
                        sb_ = tb % 2
                        actop(ta_[:, 0:256], PB[bank][:, cb_:cb_ + 256], AF.Square, [RB[bank]], [R_ta])
                        P.op("dve", lambda e: e.reduce_sum(tb_[:, 0:1], ta_[:, 0:256], axis=mybir.AxisListType.X),
                             reads=[R_ta], writes=[R_tb])
                        actop(tb_[:, 1:2], tb_[:, 0:1], AF.Sqrt, [R_tb], [R_tb], bias=RMS_EPS, scale=1.0 / 256.0)
                        P.op("dve", lambda e: e.reciprocal(tb_[:, 2:3], tb_[:, 1:2]), reads=[R_tb], writes=[R_tb])
                        P.op("dve", lambda e, bank=bank, sb_=sb_, cb_=cb_: e.scalar_tensor_tensor(
                            stg[sb_][:, 256:512], PB[bank][:, cb_:cb_ + 256], tb_[:, 2:3], gckv_bc, ALU.mult, ALU.mult),
                            reads=[RB[bank], R_tb, R_gbc, R_stg[sb_]], writes=[R_stg[sb_]])
                        copy_op("act", stgk[:, sb_, 0:32], PB[3][:, tb * 32:(tb + 1) * 32], reads=[RB[3]], writes=[R_stgk])
                        pidx, r0 = tb // 2, (tb % 2) * 128
                        dma("sp", nckv_o[pidx, i_l, r0:r0 + 128, :], stg[sb_][:, 256:512], [R_stg[sb_]], [R_stg[sb_]], S_stg[sb_])
                        dma("sp", nkpe_o[pidx, i_l, r0:r0 + 128, :], stgk[:, sb_, 0:32], [R_stgk], [R_stgk], S_stgk)
            P.barrier()
            if AB_STOP[0] < 2:
                return
            CUR_TAG[0] = "A" + ("s" if latent else "p")
            dma("pool", woutS, ab_w_out[i_l, 0:512, :].rearrange("(c p) d -> p c d", p=128), [P.epoch], [R_woutS], S_woutS)
            esink_t = carve(O_ZB, [128, 2, 512], F32)
            oblk2 = [carve(O_ZB + 4096 + 2048 * i, [128, 4, 256], BF16) for i in range(2)]
            memset(esink_t[64:128, :, :], 0.0, [R_esk])
            for g in range(2):
                for j in range(4):
                    P.op("dve", lambda e, g=g, j=j: e.tensor_scalar(
                        esink_t[64:128, g, j * 128:(j + 1) * 128], esink_t[64:128, g, j * 128:(j + 1) * 128],
                        esink[64:128, 4 * g + j:4 * g + j + 1], None, ALU.add), reads=[R_esink, R_esk], writes=[R_esk])
            for nbp in range(nqb // 2):
                oi = cnt["ob"] % 2
                cnt["ob"] += 1
                ob = oblk2[oi]
                for sub in range(2):
                    nb = 2 * nbp + sub
                    for g in range(2):
                        if latent:
                            chunks = [(0, None), (1, None)]
                            if nb > 0:
                                chunks.append((2 + nb - 1, 0))
                            chunks.append((2 + nb, None))
                            if nb < nqb - 1:
                                chunks.append((2 + nb + 1, 1))
                        else:
                            chunks = [(2 * (nb // 2), None), (2 * (nb // 2) + 1, None)]
                        po = 4 + cnt["po"] % 2
                        cnt["po"] += 1
                        qsl = qaT[64 * g:64 * g + 64, :, nb * 128:(nb + 1) * 128]
                        pend = []
                        nch = len(chunks)
                        for ci in range(nch + 2):
                            if ci < nch:
                                c, mk = chunks[ci]
                                ps = cnt["s3"] % 3
                                cnt["s3"] += 1
                                mm(PB[ps][:, :].rearrange("p (j q) -> p j q", q=128), kaT[64 * g:64 * g + 64, c * 128:(c + 1) * 128], qsl,
                                   True, mk is None, [R_kaT] + R_qaT, [RB[ps]])
                                if mk is not None:
                                    mm(PB[ps][:, :], identB[:], masks[:, mk, :], False, True, [R_identB, R_masks], [RB[ps]])
                                pt = cnt["pt"] % 3
                                cnt["pt"] += 1
                                actop(PT[pt], PB[ps][:, :], AF.Exp, [RB[ps]], [R_PT[pt]], scale=A_SCALE)
                                pend.append((c, pt))
                            if ci >= 2:
                                c2, pt2 = pend[ci - 2]
                                mm(PB[po][:, :], va[:, c2, g, :], PT[pt2], ci - 2 == 0, ci - 2 == nch - 1, [R_va, R_PT[pt2]], [RB[po]])
                        ttop(tmpden[64:128, :], PB[po][64:128, :], esink_t[64:128, g, :], ALU.add, [RB[po], R_esk], [R_tmpden])
                        actop(tmpden[64:128, :], tmpden[64:128, :], AF.Ln, [R_tmpden], [R_tmpden])
                        actop(rden[0:64, :], tmpden[64:128, :], AF.Exp, [R_tmpden], [R_rden], scale=-1.0)
                        for par in range(2):
                            ttop(ob[64 * par:64 * par + 64, 2 * g:2 * g + 2, sub * 128:(sub + 1) * 128],
                                 PB[po][0:64, :].rearrange("p (jj par q) -> p jj par q", par=2, q=128)[:, :, par, :],
                                 rden[0:64, :].rearrange("p (jj par q) -> p jj par q", par=2, q=128)[:, :, par, :],
                                 ALU.mult, [RB[po], R_rden], [R_oblk[oi]])
                tq = tok0 + nbp * 256
                for dc in range(KC):
                    bank = 3 if dc % 2 == 0 else 7
                    for pr in range(4):
                        mm(PB[bank][:, 0:256], woutS[:, pr, dc * 128:(dc + 1) * 128], ob[:, pr, :], pr == 0, pr == 3,
                           [R_woutS, R_oblk[oi]], [RB[bank]])
                    xacc(dc, bank, 256, tq, s)
            P.barrier()
            if AB_STOP[0] < 3:
                return
            CUR_TAG[0] = "Bprep" + ("s" if latent else "p")
            dma("pool", woutS, ab_w_out[i_l, 512:1024, :].rearrange("(c p) d -> p c d", p=128), [P.epoch], [R_woutS], S_woutS)
            if latent:
                dma("sp", ropeT[0:64], ropeB_d, [P.epoch], [R_ropeT], S_rope)
            for bi in range(2):
                memset(hb[bi]["v"][:, 0:18, 64:128], 1.0, [R_hb[bi]["v"]])
                memset(wukv_h[bi][:, :, 0:64], 0.0, [R_wukvh[bi]])
                memset(wuq_h[bi][:, :, 0:64], 0.0, [R_wuqh[bi]])
            def hsel(h):
                if latent:
                    return h % 2, 0, 0, 0
                return h // 4, (h % 4) * 512, (h % 4) * 512, (h % 4) * 4

            def b_prep(h):
                bi = h % 2
                hi, qo, ko, vo = hsel(h)
                H, RH = hb[hi], R_hb[hi]
                units = []

                def u_w():
                    dma("pool", wuq_nat, b_w_uq[i_l].rearrange("(k p) c -> p k c", p=128)[:, :, h * 96:(h + 1) * 96],
                        [P.epoch], [R_wuqn], S_wuq)
                    copy_op("dve", wuq_h[bi][:, :, 64:128], wuq_nat[:, :, 0:64], reads=[R_wuqn], writes=[R_wuqh[bi]])
                    srcv = wuq_nat[:, :, 64:96].rearrange("p k (hf b i) -> p k hf b i", hf=2, b=2)
                    dstv = wuq_h[bi][:, :, 0:64].rearrange("p k (b z hf i) -> p k b z hf i", b=2, z=2, hf=2)
                    for b2 in range(2):
                        copy_op("dve", dstv[:, :, b2, 0, :, :], srcv[:, :, :, b2, :], reads=[R_wuqn], writes=[R_wuqh[bi]])
                    dma("pool", wukv_h[bi][:, :, 64:192], b_w_ukv[i_l].rearrange("(k p) c -> p k c", p=128)[:, :, h * 128:(h + 1) * 128],
                        [P.epoch], [R_wukvh[bi]], S_wukv[bi])
                    copy_op("dve", H["k"][0:64, ko:ko + Lk], kpeT[0:64, 0:Lk], reads=[R_kpeT], writes=[RH["k"]])
                units.append(u_w)

                def nbank():
                    bk = (6, 7)[cnt["pb"] % 2]
                    cnt["pb"] += 1
                    return bk

                def u_q(tt):
                    CUR_TAG[0] = "Bprep" + ("s" if latent else "p")
                    lc0 = tt * n
                    bk = nbank()
                    for kc in range(3):
                        mm(PB[bk][:, 0:n], wuq_h[bi][:, kc, :], cqnT[:, kc, lc0:lc0 + n], kc == 0, kc == 2, [R_wuqh[bi], R_cqn], [RB[bk]])
                    copy_op("dve", H["q"][64:128, qo + lc0:qo + lc0 + n], PB[bk][64:128, 0:n], reads=[RB[bk]], writes=[RH["q"]])
                    if latent:
                        rope_B(H["q"][0:64, qo + lc0:qo + lc0 + n], RH["q"], PB[bk][0:64, 0:n], RB[bk], n, lc0, ropeT, R_ropeT)
                    else:
                        copy_op("dve", H["q"][0:64, qo + lc0:qo + lc0 + n], PB[bk][0:64, 0:n], reads=[RB[bk]], writes=[RH["q"]])

                def u_k(k0):
                    CUR_TAG[0] = "Bprep" + ("s" if latent else "p")
                    m = min(512, Lk - k0)
                    bk = nbank()
                    for kc in range(2):
                        mm(PB[bk][:, 0:m], wukv_h[bi][:, kc, 0:128], ckvnT[:, kc, k0:k0 + m], kc == 0, kc == 1, [R_wukvh[bi], R_ckvn], [RB[bk]])
                    copy_op("dve", H["k"][64:128, ko + k0:ko + k0 + m], PB[bk][64:128, 0:m], reads=[RB[bk]], writes=[RH["k"]])

                def u_v(c0):
                    CUR_TAG[0] = "Bprep" + ("s" if latent else "p")
                    nc_ = min(8, nkc - c0)
                    bk = nbank()
                    for c in range(nc_):
                        for kc in range(2):
                            mm(PB[bk][:, c * 64:(c + 1) * 64], ckvnT[:, kc, (c0 + c) * 128:(c0 + c + 1) * 128], wukv_h[bi][:, kc, 128:192],
                               kc == 0, kc == 1, [R_wukvh[bi], R_ckvn], [RB[bk]])
                    copy_op("dve", H["v"][:, vo + c0:vo + c0 + nc_, 0:64], PB[bk][:, 0:nc_ * 64].rearrange("p (c d) -> p c d", d=64),
                            reads=[RB[bk]], writes=[RH["v"]])

                for tt in range(ntile):
                    units.append(lambda tt=tt: u_q(tt))
                for k0 in range(0, Lk, 512):
                    units.append(lambda k0=k0: u_k(k0))
                for c0 in range(0, nkc, 8):
                    units.append(lambda c0=c0: u_v(c0))
                return units

            def b_att(h, tt, side):
                hi, qo, ko, vo = hsel(h)
                H, RH = hb[hi], R_hb[hi]
                lc0 = tt * n
                po = 4 + cnt["po"] % 2
                cnt["po"] += 1
                if latent:
                    work = [(c, 0, n, c == 0, c == nkc - 1) for c in range(nkc)]
                else:
                    work = [(c, 256 * (c // 2), 256, c % 2 == 0, c % 2 == 1) for c in range(nkc)]
                pend = []
                nw = len(work)
                for ci in range(nw + 3):
                    CUR_TAG[0] = "Batt" + ("s" if latent else "p")
                    if ci < nw:
                        c, q0, qn, _, _ = work[ci]
                        ps = cnt["s3"] % 4
                        cnt["s3"] += 1
                        mm(PB[ps][:, 0:qn], H["k"][:, ko + c * 128:ko + (c + 1) * 128], H["q"][:, qo + lc0 + q0:qo + lc0 + q0 + qn], True, True,
                           [RH["k"], RH["q"]], [RB[ps]])
                        pt = cnt["pt"] % 4
                        cnt["pt"] += 1
                        actop(PT[pt][:, 0:qn], PB[ps][:, 0:qn], AF.Exp, [RB[ps]], [R_PT[pt]], scale=B_SCALE)
                        pend.append(pt)
                    if ci >= 3:
                        c, q0, qn, st_, sp_ = work[ci - 3]
                        pt2 = pend[ci - 3]
                        mm(PB[po][:, q0:q0 + qn], H["v"][:, vo + c, :], PT[pt2][:, 0:qn], st_, sp_, [RH["v"], R_PT[pt2]], [RB[po]])
                    if side and ci % 6 == 5:
                        side.pop(0)()
                P.op("dve", lambda e, po=po: e.reciprocal(rden[0:64, 0:n], PB[po][64:128, 0:n]), reads=[RB[po]], writes=[R_rden])
                hp = 64 * (h % 2)
                ttop(oTB[hp:hp + 64, h // 2, lc0:lc0 + n], PB[po][0:64, 0:n], rden[0:64, 0:n], ALU.mult,
                     [RB[po], R_rden], [R_qaT[h // 2]])

            cnt["pb"] = 0
            if latent:
                for u_ in b_prep(0):
                    u_()
                for h in range(8):
                    side = b_prep(h + 1) if h + 1 < 8 else []
                    for tt in range(ntile):
                        b_att(h, tt, side)
                    while side:
                        side.pop(0)()
            else:
                for h in range(8):
                    for u_ in b_prep(h):
                        u_()
                for h in range(8):
                    b_att(h, 0, None)
            CUR_TAG[0] = "Bout" + ("s" if latent else "p")
            for tt in range(ntile):
                lc0 = tt * n
                for dc in range(KC):
                    bank = 3 if dc % 2 == 0 else 7
                    for pr in range(4):
                        mm(PB[bank][:, 0:n], woutS[:, pr, dc * 128:(dc + 1) * 128], oTB[:, pr, lc0:lc0 + n], pr == 0, pr == 3,
                           [R_woutS, R_qaT[pr]], [RB[bank]])
                    xacc(dc, bank, n, tok0 + lc0, s)

        def store_tokens(dst, tok0, ntok):
            for tb in range(ntok // 128):
                i = cnt["io"] % 2
                cnt["io"] += 1
                t0 = tok0 + tb * 128
                s = t0 // 512
                for h in range(2):
                    bank = 2 * (tb % 2) + h
                    for kk in range(4):
                        k = h * 4 + kk
                        P.op("pe", lambda e, k=k, kk=kk, bank=bank, t0=t0: e.transpose(
                            PB[bank][:, kk * 128:(kk + 1) * 128], xT[:, k, t0:t0 + 128], ident[:]),
                            reads=[R_xT[k][s], R_ident], writes=[RB[bank]])
                    copy_op(alt_eng(), iost[i][:, h * 512:(h + 1) * 512], PB[bank][:, :], reads=[RB[bank]], writes=[R_io[i]])
                P.op("sp", lambda e, i=i, tb=tb: e.dma_start(out=dst[tb * 128:(tb + 1) * 128, :], in_=iost[i][:]),
                     reads=[R_io[i]], writes=[R_io[i]], dsem=S_io[i])

        nlayers = DEPTH if stage >= 10 else (2 if stage == 3 else 1)
        while LOAD_SIDE:
            LOAD_SIDE.pop(0)()
        for l in range(nlayers):
            MODS["set"] = l % 2
            if stage >= 1 and not skip_ffn:
                half_ffn(l, 0)
            if stage >= 2:
                P.barrier()
                if l % 2 == 1 and stage >= 3:
                    f_mixer(l)
                if l % 2 == 0 and stage >= 4:
                    ab_mixer(l, do_sample=(stage >= 5))
                P.barrier()
                side = ada_steps(l + 1, (l + 1) % 2, 3) if l + 1 < nlayers else None
                if not skip_ffn:
                    half_ffn(l, 1, pre_ln=1, side=side)
                else:
                    for st_idx in range(5):
                        layer_norm(l, 1, st_idx)
                    while side:
                        side.pop(0)()
        P.barrier()
        store_tokens(ys, 0, TS)
        store_tokens(yp, TS, TP)
        P.emit()
    return nc


_CACHE = {}


def _constants():
    if "const" in _CACHE:
        return _CACHE["const"]
    bf = ml_dtypes.bfloat16
    out = {}
    for name, L in (("dft_big", TS), ("dft_small", 256)):
        n = np.arange(L, dtype=np.int64)
        ang = 2.0 * np.pi * ((n[:, None] * n[None, :]) % L).astype(np.float64) / L
        m = np.stack([np.cos(ang), -np.sin(ang)], axis=1) / np.sqrt(L)
        out[name] = np.ascontiguousarray(m.astype(np.float32).astype(bf))
    n = np.arange(256, dtype=np.int64)
    ang = 2.0 * np.pi * ((n[:, None] * n[None, :]) % 256).astype(np.float64) / 256
    out["csg"] = np.ascontiguousarray((np.concatenate([np.cos(ang), np.sin(ang)], axis=1) / 16.0).astype(np.float32).astype(bf))
    t = np.arange(TS)
    pos = np.stack([t // 64, t % 64]).astype(np.float64)
    ra = np.zeros((128, 2, TS), np.float64)
    for p in range(128):
        dp = p % 64
        b_, hf, i = dp // 32, (dp % 32) // 16, dp % 16
        ang = pos[hf] * (10000.0 ** (-i / 16.0))
        ra[p, 0] = np.cos(ang)
        ra[p, 1] = np.sin(ang) * (-1.0 if b_ == 1 else 1.0)
    out["ropeA"] = ra.astype(np.float32)
    rb = np.zeros((64, 2, TS), np.float64)
    for p in range(64):
        b_, r = p // 32, p % 32
        if r < 16:
            hf, i = r // 8, r % 8
            ang = pos[hf] * (10000.0 ** (-i / 8.0))
            rb[p, 0] = np.cos(ang)
            rb[p, 1] = np.sin(ang) * (-1.0 if b_ == 1 else 1.0)
    out["ropeB"] = rb.astype(np.float32)
    kj = np.arange(128)[:, None]
    qi = np.arange(128)[None, :]
    NEG = -30000.0
    m0 = np.where(qi <= kj, 0.0, NEG)
    m1 = np.where(kj <= qi, 0.0, NEG)
    out["masks"] = np.ascontiguousarray(np.stack([np.tile(m0, (1, 4)), np.tile(m1, (1, 4))], axis=1).astype(np.float32).astype(bf))
    _CACHE["const"] = out
    return out


def _prep_inputs(inp):
    f32 = np.float32
    small = np.concatenate([
        np.asarray(inp["b_ada"], f32).reshape(DEPTH, 72, 128),
        np.asarray(inp["ln_g"], f32).reshape(DEPTH, 24, 128),
        np.asarray(inp["ln_b"], f32).reshape(DEPTH, 24, 128)], axis=1)
    shared = {
        "w_ada": np.ascontiguousarray(inp["w_ada"], f32),
        "smallp": np.ascontiguousarray(small),
        "wg": np.ascontiguousarray(inp["ffn_w_gate"], f32),
        "wu": np.ascontiguousarray(inp["ffn_w_up"], f32),
        "wd": np.ascontiguousarray(inp["ffn_w_down"], f32),
        "ident": np.eye(128, dtype=f32),
        "f_w_in": np.ascontiguousarray(inp["f_w_in"], f32),
        "f_w_out": np.ascontiguousarray(inp["f_w_out"], f32),
    }
    for k_ in ("ab_w_in", "ab_w_out", "a_sink", "b_g_cq", "b_w_uq", "b_g_ckv", "b_w_ukv"):
        shared[k_] = np.ascontiguousarray(inp[k_], f32)
    shared.update(_constants())
    maps = []
    for b in range(8):
        m = dict(shared)
        m["xs"] = np.ascontiguousarray(inp["x_sample"][b], f32)
        m["xp"] = np.ascontiguousarray(np.asarray(inp["x_prompt"][2 * b:2 * b + 2], f32).reshape(TP, D))
        m["c2"] = np.ascontiguousarray(np.stack([np.asarray(inp["c"][b], f32), np.asarray(inp["c_ctx"], f32)]))
        m["ca_k"] = np.ascontiguousarray(np.asarray(inp["cache_a_k"][b], f32).reshape(2, 256, 128))
        m["ca_v"] = np.ascontiguousarray(np.asarray(inp["cache_a_v"][b], f32).reshape(2, 256, 128))
        m["cb_ckv"] = np.ascontiguousarray(inp["cache_b_ckv"][b], f32)
        m["cb_kpe"] = np.ascontiguousarray(inp["cache_b_kpe"][b], f32)
        maps.append(m)
    return maps


def kernel(**inputs):
    stage = inputs.pop("_stage", 99)
    ncores = inputs.pop("_ncores", 8)
    if stage not in _CACHE:
        _CACHE[stage] = build_program(stage)
    nc = _CACHE[stage]
    maps = _prep_inputs(inputs)[:ncores]
    res = bu.run_bass_kernel_spmd(nc, maps, core_ids=list(range(ncores)))
    r = res.results
    y_s = np.stack([r[b]["ys"] for b in range(ncores)])
    y_p = np.concatenate([r[b]["yp"].reshape(2, 256, D) for b in range(ncores)])
    nk = np.concatenate([r[b]["nk"] for b in range(ncores)]).reshape(2 * ncores, 2, 256, 2, 64)
    nv = np.concatenate([r[b]["nv"] for b in range(ncores)]).reshape(2 * ncores, 2, 256, 2, 64)
    nckv = np.concatenate([r[b]["nckv"] for b in range(ncores)])
    nkpe = np.concatenate([r[b]["nkpe"] for b in range(ncores)])
    return y_p, y_s, nk, nv, nckv, nkpe
```
